# Optimizing a Trainium2 kernel written in Bass

```python
import math
import jax, jax.numpy as jnp
from jax import lax
import numpy as np

D_MODEL = 1024
BATCH = 8
SEQ = 2048
DEPTH = 1
DEC_BATCH = 128
DEC_SEQ = 4
PAST_LEN = 16384
PAGE_SIZE = 128

MIX_WIDTH = D_MODEL
HG_WIDTH = MIX_WIDTH // 2
GD_WIDTH = MIX_WIDTH - HG_WIDTH
HG_HEAD_DIM = 128
HG_HEADS = HG_WIDTH // HG_HEAD_DIM
GD_HEAD_DIM = 128
GD_HEADS = GD_WIDTH // GD_HEAD_DIM
GD_CONV = 4
N_MEM = 256
MEM_HEADS = 4
MEM_HEAD_DIM = D_MODEL // MEM_HEADS
D_FF = 2816
FFN_CONV = 3
CHUNK = 64
LN_EPS = 1e-5
RMS_EPS = 1e-6
ALPHA = (2.0 * DEPTH) ** 0.25
BETA = (8.0 * DEPTH) ** -0.25
IN_COLS = 4 * HG_WIDTH + 4 * GD_WIDTH + 2 * GD_HEADS

kernel_name = "hymba_hgrn2_gdn_deepnorm_step"


def _layer_norm(x, g, b):
    xf = x.astype(jnp.float32)
    mu = jnp.mean(xf, -1, keepdims=True)
    var = jnp.mean(jnp.square(xf - mu), -1, keepdims=True)
    y = (xf - mu) * lax.rsqrt(var + LN_EPS) * g.astype(jnp.float32) + b.astype(jnp.float32)
    return y.astype(x.dtype)


def _rms_norm(x, g):
    xf = x.astype(jnp.float32)
    return xf * lax.rsqrt(jnp.mean(jnp.square(xf), -1, keepdims=True) + RMS_EPS) * g.astype(jnp.float32)


def _l2norm(x):
    xf = x.astype(jnp.float32)
    return xf * lax.rsqrt(jnp.sum(jnp.square(xf), -1, keepdims=True) + RMS_EPS)


def _causal_dwconv(x, buf, w):
    k_w = w.shape[0]
    t = x.shape[1]
    xp = jnp.concatenate([buf.astype(x.dtype), x], axis=1)
    out = sum(xp[:, j:j + t] * w[j].astype(x.dtype) for j in range(k_w))
    return out, xp[:, t:]


def _to_chunks(a, c):
    b, t, h, d = a.shape
    n = -(-t // c)
    a = jnp.pad(a.astype(jnp.float32), ((0, 0), (0, n * c - t), (0, 0), (0, 0)))
    return a.reshape(b, n, c, h, d).transpose(1, 0, 3, 2, 4)


def _from_chunks(o, t):
    n, b, h, c, d = o.shape
    return o.transpose(1, 0, 3, 2, 4).reshape(b, n * c, h, d)[:, :t]


def _hgrn2_chunked(q, k, v, log_f, s0):
    t = q.shape[1]
    c = min(CHUNK, t)
    qc, kc, vc, gc = (_to_chunks(a, c) for a in (q, k, v, log_f))
    incl = jnp.tril(jnp.ones((c, c), dtype=bool))[:, :, None]

    def step(s, inp):
        qi, ki, vi, gi = inp
        g_cum = jnp.cumsum(gi, axis=2)
        rel = jnp.where(incl, g_cum[:, :, :, None, :] - g_cum[:, :, None, :, :], -jnp.inf)
        scores = jnp.einsum('bhtd,bhsd,bhtsd->bhts', qi, ki, jnp.exp(rel))
        o = (jnp.einsum('bhts,bhsv->bhtv', scores, vi)
             + jnp.einsum('bhtd,bhdv->bhtv', qi * jnp.exp(g_cum), s))
        g_last = g_cum[:, :, -1:, :]
        s = (jnp.exp(g_last[:, :, 0, :])[..., None] * s
             + jnp.einsum('bhsd,bhsv->bhdv', ki * jnp.exp(g_last - g_cum), vi))
        return s, o

    s_fin, o = lax.scan(step, s0.astype(jnp.float32), (qc, kc, vc, gc))
    return _from_chunks(o, t), s_fin.astype(s0.dtype)


def _gdn_chunked(q, k, v, log_a, beta, s0):
    t = q.shape[1]
    c = min(CHUNK, t)
    qc, kc, vc, gc, bc = (_to_chunks(a, c) for a in (q, k, v, log_a, beta))
    incl = jnp.tril(jnp.ones((c, c), dtype=bool))
    strict = jnp.tril(jnp.ones((c, c), dtype=bool), -1)
    eye = jnp.eye(c, dtype=jnp.float32)

    def step(s, inp):
        qi, ki, vi, gi, bi = inp
        g_cum = jnp.cumsum(gi[..., 0], axis=-1)
        gamma = jnp.exp(g_cum)[..., None]
        decay = jnp.exp(jnp.where(incl, g_cum[..., :, None] - g_cum[..., None, :], -jnp.inf))
        a_mat = jnp.where(strict, bi * jnp.einsum('bhtd,bhsd->bhts', ki, ki) * decay, 0.0)
        rhs = bi * (vi - gamma * jnp.einsum('bhtd,bhdv->bhtv', ki, s))
        u = lax.linalg.triangular_solve(eye + a_mat, rhs, left_side=True, lower=True,
                                        unit_diagonal=True)
        o = (gamma * jnp.einsum('bhtd,bhdv->bhtv', qi, s)
             + jnp.einsum('bhts,bhsv->bhtv', jnp.einsum('bhtd,bhsd->bhts', qi, ki) * decay, u))
        g_last = g_cum[..., -1:]
        s = (jnp.exp(g_last)[..., None] * s
             + jnp.einsum('bhsd,bhsv->bhdv', ki * jnp.exp(g_last - g_cum)[..., None], u))
        return s, o

    s_fin, o = lax.scan(step, s0.astype(jnp.float32), (qc, kc, vc, gc, bc))
    return _from_chunks(o, t), s_fin.astype(s0.dtype)


def _mem_attn(h, mem_k, mem_v, w_mq, w_mo):
    bsz, t, _ = h.shape
    q = jnp.einsum('btd,de->bte', h, w_mq).reshape(bsz, t, MEM_HEADS, MEM_HEAD_DIM)
    s = jnp.einsum('bthd,bmhd->bhtm', q, mem_k.astype(h.dtype)).astype(jnp.float32) * MEM_HEAD_DIM ** -0.5
    p = jax.nn.softmax(s, axis=-1).astype(h.dtype)
    o = jnp.einsum('bhtm,bmhd->bthd', p, mem_v.astype(h.dtype)).reshape(bsz, t, D_MODEL)
    return jnp.einsum('btd,de->bte', o, w_mo)


def _conv_ffn(h, buf, w_up, w_conv, b_conv, w_down):
    up = jnp.einsum('btd,df->btf', h, w_up)
    gate, val = jnp.split(up, 2, axis=-1)
    gate_c, new_buf = _causal_dwconv(gate, buf, w_conv)
    act = jax.nn.gelu(gate_c + b_conv.astype(h.dtype), approximate=False) * val
    return jnp.einsum('btf,fd->btd', act, w_down), new_buf


def _layer(x, s_hg, s_gd, buf_gd, buf_ffn, mem_k, mem_v, lb,
           w_in, w_gd_conv, gd_a_log, gd_dt_bias, hg_norm_g, gd_norm_g, w_out,
           ln1_g, ln1_b, w_mq, w_mo, ln2_g, ln2_b,
           w_up, w_ffn_conv, b_ffn_conv, w_down, ln3_g, ln3_b):
    bsz, t, _ = x.shape
    f32 = jnp.float32
    proj = jnp.einsum('btd,de->bte', x, w_in)
    splits = np.cumsum([HG_WIDTH] * 4 + [3 * GD_WIDTH, GD_WIDTH, GD_HEADS]).tolist()
    hq, hf, hi, hgate, gqkv, gz, gb, ga = jnp.split(proj, splits, axis=-1)

    def heads(a, n_h):
        return a.reshape(bsz, t, n_h, -1)

    f = lb + (1.0 - lb) * jax.nn.sigmoid(hf.astype(f32))
    o_hg, new_hg = _hgrn2_chunked(heads(jax.nn.silu(hq), HG_HEADS), heads(1.0 - f, HG_HEADS),
                                  heads(hi, HG_HEADS), heads(jnp.log(f), HG_HEADS), s_hg)
    o_hg = (_rms_norm(o_hg, hg_norm_g) * jax.nn.silu(heads(hgate, HG_HEADS).astype(f32))
            ).reshape(bsz, t, HG_WIDTH)

    qkv, new_buf_gd = _causal_dwconv(gqkv, buf_gd, w_gd_conv)
    gq, gk, gv = jnp.split(jax.nn.silu(qkv), 3, axis=-1)
    q = _l2norm(heads(gq, GD_HEADS)) * GD_HEAD_DIM ** -0.5
    k = _l2norm(heads(gk, GD_HEADS))
    beta = jax.nn.sigmoid(gb.astype(f32))[..., None]
    log_a = (-jnp.exp(gd_a_log.astype(f32))
             * jax.nn.softplus(ga.astype(f32) + gd_dt_bias.astype(f32)))[..., None]
    o_gd, new_gd = _gdn_chunked(q, k, heads(gv, GD_HEADS), log_a, beta, s_gd)
    o_gd = (_rms_norm(o_gd, gd_norm_g) * jax.nn.silu(heads(gz, GD_HEADS).astype(f32))
            ).reshape(bsz, t, GD_WIDTH)

    mix = jnp.concatenate([o_hg, o_gd], axis=-1).astype(x.dtype)
    h = _layer_norm(ALPHA * x + jnp.einsum('btd,de->bte', mix, w_out), ln1_g, ln1_b)
    h = _layer_norm(ALPHA * h + _mem_attn(h, mem_k, mem_v, w_mq, w_mo), ln2_g, ln2_b)
    ff, new_buf_ffn = _conv_ffn(h, buf_ffn, w_up, w_ffn_conv, b_ffn_conv, w_down)
    y = _layer_norm(ALPHA * h + ff, ln3_g, ln3_b)
    return y, new_hg, new_gd, new_buf_gd, new_buf_ffn


def setup_inputs(seed: int = 0) -> dict:
    key = jax.random.key(seed)
    ks = iter(jax.random.split(key, 40))
    L, D = DEPTH, D_MODEL

    def nrm(shape, scale):
        return jax.random.normal(next(ks), shape, jnp.float32) * scale

    dt = jnp.exp(jax.random.uniform(next(ks), (L, GD_HEADS), jnp.float32,
                                    minval=math.log(1e-3), maxval=math.log(1e-1)))
    a_init = jax.random.uniform(next(ks), (L, GD_HEADS), jnp.float32, minval=1.0, maxval=16.0)
    return {
        "x_prompt": nrm((BATCH, SEQ, D), 1.0),
        "x_sample": nrm((DEC_BATCH, DEC_SEQ, D), 1.0),
        "state_hgrn": nrm((L, DEC_BATCH, HG_HEADS, HG_HEAD_DIM, HG_HEAD_DIM), 0.5),
        "state_gdn": nrm((L, DEC_BATCH, GD_HEADS, GD_HEAD_DIM, GD_HEAD_DIM), 0.3),
        "state_gdn_conv": nrm((L, DEC_BATCH, GD_CONV - 1, 3 * GD_WIDTH), 1.0),
        "state_ffn_conv": nrm((L, DEC_BATCH, FFN_CONV - 1, D_FF), 1.0),
        "cache_mem_k": nrm((L, DEC_BATCH, N_MEM, MEM_HEADS, MEM_HEAD_DIM), 1.0),
        "cache_mem_v": nrm((L, DEC_BATCH, N_MEM, MEM_HEADS, MEM_HEAD_DIM), 1.0),
        "mem_prompt": nrm((BATCH, N_MEM, D), 1.0),
        "hgrn_lb_logits": nrm((L + 1, HG_WIDTH), 0.5),
        "w_in": nrm((L, D, IN_COLS), D ** -0.5),
        "w_gd_conv": nrm((L, GD_CONV, 3 * GD_WIDTH), GD_CONV ** -0.5),
        "gd_a_log": jnp.log(a_init),
        "gd_dt_bias": dt + jnp.log(-jnp.expm1(-dt)),
        "hg_norm_g": 1.0 + nrm((L, HG_HEAD_DIM), 0.02),
        "gd_norm_g": 1.0 + nrm((L, GD_HEAD_DIM), 0.02),
        "w_out": nrm((L, MIX_WIDTH, D), MIX_WIDTH ** -0.5 * BETA),
        "ln1_g": 1.0 + nrm((L, D), 0.02),
        "ln1_b": nrm((L, D), 0.02),
        "w_mq": nrm((L, D, D), D ** -0.5),
        "w_mkv": nrm((L, D, 2 * D), D ** -0.5),
        "w_mo": nrm((L, D, D), D ** -0.5 * BETA),
        "ln2_g": 1.0 + nrm((L, D), 0.02),
        "ln2_b": nrm((L, D), 0.02),
        "w_up": nrm((L, D, 2 * D_FF), D ** -0.5),
        "w_ffn_conv": nrm((L, FFN_CONV, D_FF), FFN_CONV ** -0.5),
        "b_ffn_conv": nrm((L, D_FF), 0.02),
        "w_down": nrm((L, D_FF, D), D_FF ** -0.5 * BETA),
        "ln3_g": 1.0 + nrm((L, D), 0.02),
        "ln3_b": nrm((L, D), 0.02),
    }


def reference(x_prompt, x_sample, state_hgrn, state_gdn, state_gdn_conv, state_ffn_conv,
              cache_mem_k, cache_mem_v, mem_prompt, hgrn_lb_logits, w_in, w_gd_conv,
              gd_a_log, gd_dt_bias, hg_norm_g, gd_norm_g, w_out, ln1_g, ln1_b,
              w_mq, w_mkv, w_mo, ln2_g, ln2_b, w_up, w_ffn_conv, b_ffn_conv, w_down,
              ln3_g, ln3_b):
    bp = x_prompt.shape[0]
    dt_p = x_prompt.dtype
    lb_all = jnp.cumsum(jax.nn.softmax(hgrn_lb_logits.astype(jnp.float32), axis=0), axis=0)

    xp, xs = x_prompt, x_sample
    p_hg, p_gd, p_bgd, p_bff, p_mk, p_mv = [], [], [], [], [], []
    s_hg, s_gd, s_bgd, s_bff = [], [], [], []
    for l in range(DEPTH):
        lp = (lb_all[l], w_in[l], w_gd_conv[l], gd_a_log[l], gd_dt_bias[l], hg_norm_g[l],
              gd_norm_g[l], w_out[l], ln1_g[l], ln1_b[l], w_mq[l], w_mo[l], ln2_g[l], ln2_b[l],
              w_up[l], w_ffn_conv[l], b_ffn_conv[l], w_down[l], ln3_g[l], ln3_b[l])
        mkv = jnp.einsum('bmd,de->bme', mem_prompt, w_mkv[l])
        mk, mv = jnp.split(mkv, 2, axis=-1)
        mk = mk.reshape(bp, N_MEM, MEM_HEADS, MEM_HEAD_DIM)
        mv = mv.reshape(bp, N_MEM, MEM_HEADS, MEM_HEAD_DIM)
        xp, a, b, c, d = _layer(
            xp,
            jnp.zeros((bp, HG_HEADS, HG_HEAD_DIM, HG_HEAD_DIM), dt_p),
            jnp.zeros((bp, GD_HEADS, GD_HEAD_DIM, GD_HEAD_DIM), dt_p),
            jnp.zeros((bp, GD_CONV - 1, 3 * GD_WIDTH), dt_p),
            jnp.zeros((bp, FFN_CONV - 1, D_FF), dt_p),
            mk, mv, *lp)
        p_hg.append(a); p_gd.append(b); p_bgd.append(c); p_bff.append(d)
        p_mk.append(mk); p_mv.append(mv)
        xs, a, b, c, d = _layer(xs, state_hgrn[l], state_gdn[l], state_gdn_conv[l],
                                state_ffn_conv[l], cache_mem_k[l], cache_mem_v[l], *lp)
        s_hg.append(a); s_gd.append(b); s_bgd.append(c); s_bff.append(d)

    return (xp, xs,
            jnp.stack(p_hg), jnp.stack(p_gd), jnp.stack(p_bgd), jnp.stack(p_bff),
            jnp.stack(p_mk), jnp.stack(p_mv),
            jnp.stack(s_hg), jnp.stack(s_gd), jnp.stack(s_bgd), jnp.stack(s_bff))
```

```python
import numpy as np
from contextlib import ExitStack
import concourse.bass as bass
import concourse.mybir as mybir
from concourse.bass_utils import run_bass_kernel_spmd

F32 = mybir.dt.float32
BF16 = mybir.dt.bfloat16
AF = mybir.ActivationFunctionType
ALU = mybir.AluOpType

COMPUTE = ("pe", "act", "dve", "pool")
NS_DMA = 12
DEBUG_LINES = False

D = 1024
SEQ = 2048
NCORE = 8
SB = 16
ST = 4
DFF = 2816
NMEM = 256
ALPHA = 2.0 ** 0.25
LN_EPS = 1e-5
RMS_EPS = 1e-6
BIG = 30000.0

P_L0, P_L1, P_GDC, P_HGG, P_GDG = 0, 4, 8, 56, 57
P_LN = 58
P_FC, P_FB, P_ALOG, P_DTB, NPRM = 106, 172, 194, 195, 196
C_ID, C_MT, C_MB1, C_MB2, C_SCAN, C_BM, C_SEL, NCST = 0, 128, 256, 384, 512, 1024, 1040, 1552


class R:
    __slots__ = ("name", "last_w", "reads", "excl")

    def __init__(self, name="", excl=False):
        self.name = name
        self.last_w = None
        self.reads = []
        self.excl = excl


def alias(new_rs, old_rs):
    hz = []
    for o in old_rs:
        if o.last_w is not None:
            hz.append(o.last_w)
        hz.extend(o.reads)
    hz = list(set(hz))
    for n in new_rs:
        n.last_w = None
        n.reads = list(hz)


class Node:
    __slots__ = ("gid", "stream", "cls", "fn", "deps", "dur", "lat", "start", "finish", "idx", "waits", "sig")

    def __init__(self, gid, stream, cls, fn, deps, dur, lat):
        self.gid, self.stream, self.cls, self.fn, self.deps, self.dur, self.lat = gid, stream, cls, fn, deps, dur, lat
        self.start = self.finish = 0.0
        self.idx = -1
        self.waits = []
        self.sig = False


class Em:
    STREAMS = ("pe", "act", "dve", "pool", "sp")
    HOP = 0.50
    PHOP = 0.25

    def __init__(self, nc, reorder=True):
        self.nc = nc
        self.nodes = []
        self.lines = {}
        self.reorder = reorder

    def _add(self, stream, cls, fn, r, w, dur, lat):
        gid = len(self.nodes)
        deps = set()
        nodes = self.nodes
        for res in r:
            if res.last_w is not None:
                deps.add(res.last_w)
            if res.excl:
                deps.update(x for x in res.reads if nodes[x].cls != cls)
        for res in w:
            if res.last_w is not None:
                deps.add(res.last_w)
            deps.update(res.reads)
        deps.discard(gid)
        nodes.append(Node(gid, stream, cls, fn, deps, dur, lat))
        if DEBUG_LINES:
            import sys
            f = sys._getframe(2)
            ln = []
            while f is not None and len(ln) < 3:
                ln.append(f.f_lineno)
                f = f.f_back
            self.lines[gid] = ln
        for res in r:
            res.reads.append(gid)
        for res in w:
            res.last_w = gid
            res.reads = []
        return gid

    def op(self, eng, fn, r=(), w=(), dur=0.15):
        return self._add(eng, eng, fn, r, w, dur, dur)

    def dma(self, stream, fn, r=(), w=(), nbytes=65536):
        issue = 1.1 if stream == "pool" else 0.08
        lat = 2.0 + nbytes / 200e3
        return self._add(stream, "dma_" + stream, fn, r, w, issue, lat)

    def schedule(self):
        nodes = self.nodes
        n = len(nodes)
        if not self.reorder:
            for i, nd in enumerate(nodes):
                nd.start = float(i)
            return
        succs = [[] for _ in range(n)]
        indeg = [0] * n
        for nd in nodes:
            indeg[nd.gid] = len(nd.deps)
            for d in nd.deps:
                succs[d].append(nd.gid)
        prio = [0.0] * n
        for i in range(n - 1, -1, -1):
            m = 0.0
            for sgid in succs[i]:
                if prio[sgid] > m:
                    m = prio[sgid]
            prio[i] = nodes[i].lat + self.PHOP + m
        ready = {s: [] for s in self.STREAMS}
        rtime = [0.0] * n
        free = {s: 0.0 for s in self.STREAMS}
        for nd in nodes:
            if indeg[nd.gid] == 0:
                ready[nd.stream].append(nd.gid)
        done = 0
        while done < n:
            best = None
            for sname in self.STREAMS:
                rl = ready[sname]
                if not rl:
                    continue
                t = free[sname]
                c_now = None
                c_late = None
                for g in rl:
                    rt = rtime[g]
                    if rt <= t:
                        if c_now is None or prio[g] > prio[c_now] or (prio[g] == prio[c_now] and g < c_now):
                            c_now = g
                    elif c_late is None or rt < rtime[c_late] or (rt == rtime[c_late] and g < c_late):
                        c_late = g
                g = c_now if c_now is not None else c_late
                st = max(t, rtime[g])
                if best is None or st < best[0] or (st == best[0] and prio[g] > prio[best[1]]):
                    best = (st, g, sname)
            st, g, sname = best
            nd = nodes[g]
            nd.start = st
            nd.finish = st + nd.lat
            free[sname] = st + nd.dur
            ready[sname].remove(g)
            done += 1
            for sgid in succs[g]:
                indeg[sgid] -= 1
                if indeg[sgid] == 0:
                    sn = nodes[sgid]
                    rt = 0.0
                    for d in sn.deps:
                        f = nodes[d].finish + (self.HOP if nodes[d].stream != sn.stream else 0.02)
                        if f > rt:
                            rt = f
                    rtime[sgid] = rt
                    ready[sn.stream].append(sgid)
        self.est_total = max(nd.finish for nd in nodes)

    def plan_sync(self):
        nodes = self.nodes
        order = sorted(range(len(nodes)), key=lambda g: (nodes[g].start, g))
        self.order = order
        cnt = {}
        for g in order:
            nd = nodes[g]
            nd.idx = cnt.get(nd.cls, 0)
            cnt[nd.cls] = nd.idx + 1
        self.cnt = cnt
        known = {s: {} for s in self.STREAMS}
        known_dma = {s: {} for s in self.STREAMS}
        snap = {}
        ring_nodes = {}
        for g in order:
            nd = nodes[g]
            if nd.cls.startswith("dma_"):
                ring_nodes.setdefault(nd.cls, []).append(g)

        def is_known(stream, d):
            if d.cls.startswith("dma_"):
                st = known_dma[stream].get(d.cls)
                return st is not None and d.idx in st
            return known[stream].get(d.cls, -1) >= d.idx

        def learn(stream, d):
            if d.cls.startswith("dma_"):
                known_dma[stream].setdefault(d.cls, set()).add(d.idx)
            else:
                if known[stream].get(d.cls, -1) < d.idx:
                    known[stream][d.cls] = d.idx
            sn = snap.get(d.gid)
            if sn is not None:
                k, kd = sn
                mine = known[stream]
                for c, i in k.items():
                    if mine.get(c, -1) < i:
                        mine[c] = i
                for c, st in kd.items():
                    known_dma[stream].setdefault(c, set()).update(st)

        for g in order:
            nd = nodes[g]
            stream = nd.stream
            deps = sorted((nodes[d] for d in nd.deps), key=lambda x: (x.cls, x.idx), reverse=True)
            waits = []
            for d in deps:
                if d.cls == "pe" and nd.cls == "pe":
                    assert d.idx < nd.idx
                    continue
                if is_known(stream, d):
                    continue
                waits.append(d.gid)
                if not d.cls.startswith("dma_"):
                    d.sig = True
                learn(stream, d)
            if nd.cls.startswith("dma_") and nd.idx >= NS_DMA:
                gd = nodes[ring_nodes[nd.cls][nd.idx - NS_DMA]]
                if not is_known(stream, gd):
                    waits.append(gd.gid)
                    learn(stream, gd)
            nd.waits = waits
            kd = known_dma[stream]
            for c in list(kd.keys()):
                if len(kd[c]) > 64:
                    kd[c] = set(sorted(kd[c])[-48:])
            snap[g] = (dict(known[stream]), {c: set(st) for c, st in kd.items()})

    def emit(self):
        nc = self.nc
        self.schedule()
        self.plan_sync()
        nodes = self.nodes
        per_stream = {s: [] for s in self.STREAMS}
        for g in self.order:
            per_stream[nodes[g].stream].append(nodes[g])
        rank = {}
        for e in COMPUTE:
            k = 0
            for nd in per_stream[e]:
                if nd.cls == e and nd.sig:
                    k += 1
                    rank[nd.gid] = k
        with ExitStack() as es:
            sem = {e: es.enter_context(nc.semaphore("s_" + e)) for e in COMPUTE}
            rings = {}
            for ring in ("dma_sp", "dma_pool", "dma_act"):
                if self.cnt.get(ring, 0) > 0:
                    rings[ring] = [es.enter_context(nc.semaphore("%s_%d" % (ring, i))) for i in range(NS_DMA)]
            block = es.enter_context(nc.Block())

            def run_stream(sname, eng):
                for nd in per_stream[sname]:
                    for dg in nd.waits:
                        d = nodes[dg]
                        if d.cls.startswith("dma_"):
                            eng.wait_ge(rings[d.cls][d.idx % NS_DMA], 16 * (d.idx // NS_DMA + 1))
                        else:
                            eng.wait_ge(sem[d.cls], rank[dg])
                    ins = nd.fn(eng)
                    if nd.cls.startswith("dma_"):
                        ins.then_inc(rings[nd.cls][nd.idx % NS_DMA], 16)
                    elif nd.sig:
                        ins.then_inc(sem[nd.cls], 1)
                if sname == "sp":
                    for ring, sems in rings.items():
                        n_ = self.cnt[ring]
                        for i in range(min(NS_DMA, n_)):
                            last = i + NS_DMA * ((n_ - 1 - i) // NS_DMA)
                            eng.wait_ge(sems[i], 16 * (last // NS_DMA + 1))

            @block.tensor
            def _(pe):
                run_stream("pe", pe)

            @block.scalar
            def _(act):
                run_stream("act", act)

            @block.vector
            def _(dve):
                run_stream("dve", dve)

            @block.gpsimd
            def _(pool):
                run_stream("pool", pool)

            @block.sync
            def _(sp):
                run_stream("sp", sp)


class Pool_:
    def __init__(self, tiles, rs=None):
        self.tiles = tiles
        self.rs = rs if rs is not None else [R() for _ in tiles]
        self.i = 0

    def get(self):
        t, r = self.tiles[self.i], self.rs[self.i]
        self.i = (self.i + 1) % len(self.tiles)
        return t, r


class MK:
    def __init__(self, n_ptiles=4, do_sample=True, dbg=None, reorder=True):
        self.n_ptiles = n_ptiles
        self.do_sample = do_sample
        self.dbg = dbg or {}
        self.nc = bass.Bass("TRN2", target_bir_lowering=False)
        self.em = Em(self.nc, reorder=reorder)
        self.es = ExitStack()

    def din(self, name, shape):
        return self.nc.dram_tensor(name, list(shape), F32, kind="ExternalInput").ap()

    def dout(self, name, shape):
        return self.nc.dram_tensor(name, list(shape), F32, kind="ExternalOutput").ap()

    def sbt(self, name, shape, dt):
        return self.es.enter_context(self.nc.sbuf_tensor("sb_" + name, list(shape), dt))

    @staticmethod
    def nfree(ap):
        try:
            shp = ap.shape
        except Exception:
            shp = ap[:].shape
        n = 1
        for x in shp[1:]:
            n *= x
        return n

    def mm(self, out, lhsT, rhs, r, w, start=True, stop=True):
        n = self.nfree(rhs)
        d = max(0.06, n / 1900.0) * (4.0 if rhs.dtype == F32 else 1.0)
        self.em.op("pe", lambda e: e.matmul(out, lhsT=lhsT, rhs=rhs, start=start, stop=stop), r=r, w=w, dur=d)

    def tr(self, out, in_, ident, r, w):
        d = 0.07 * (4.0 if in_.dtype == F32 else 1.0)
        self.em.op("pe", lambda e: e.transpose(out=out, in_=in_, identity=ident), r=r, w=w, dur=d)

    def act(self, out, in_, func, r, w, bias=None, scale=None):
        kw = {}
        if bias is not None:
            kw["bias"] = bias
        if scale is not None:
            kw["scale"] = scale
        self.em.op("act", lambda e: e.activation(out=out, in_=in_, func=func, **kw), r=r, w=w, dur=0.2 + self.nfree(out) / 1200.0)

    def cp(self, eng, out, in_, r, w):
        if eng == "act":
            self.em.op("act", lambda e: e.copy(out=out, in_=in_), r=r, w=w, dur=0.2 + self.nfree(out) / 1200.0)
        else:
            self.em.op(eng, lambda e: e.tensor_copy(out=out, in_=in_), r=r, w=w, dur=0.12 + self.nfree(out) / 1500.0)

    def tt(self, out, in0, in1, op, r, w, eng="dve"):
        self.em.op(eng, lambda e: e.tensor_tensor(out=out, in0=in0, in1=in1, op=op), r=r, w=w, dur=0.12 + self.nfree(out) / 960.0)

    def ts(self, out, in0, s1, s2, op0, op1, r, w, eng="dve"):
        if s2 is None:
            self.em.op(eng, lambda e: e.tensor_scalar(out=out, in0=in0, scalar1=s1, scalar2=None, op0=op0), r=r, w=w, dur=0.12 + self.nfree(out) / 1500.0)
        else:
            self.em.op(eng, lambda e: e.tensor_scalar(out=out, in0=in0, scalar1=s1, scalar2=s2, op0=op0, op1=op1), r=r, w=w, dur=0.12 + self.nfree(out) / 1500.0)

    def stt(self, out, in0, scalar, in1, op0, op1, r, w, eng="dve"):
        self.em.op(eng, lambda e: e.scalar_tensor_tensor(out=out, in0=in0, scalar=scalar, in1=in1, op0=op0, op1=op1), r=r, w=w, dur=0.12 + self.nfree(out) / 960.0)

    def memset(self, ap, val, w, eng="dve"):
        self.em.op(eng, lambda e: e.memset(ap, val), w=w)

    def dma(self, out, in_, r=(), w=(), q="sp"):
        try:
            nb = 128 * self.nfree(in_) * 4
        except Exception:
            nb = 65536
        self.em.dma(q, lambda e: e.dma_start(out=out, in_=in_), r=r, w=w, nbytes=nb)

    def ps_alloc(self):
        assert self.ps_free, "out of PSUM banks"
        b = self.ps_free.pop(0)
        return b

    def ps_release(self, b):
        self.ps_free.append(b)

    def w_init(self):
        self.NSLOT = 3
        self.wslots = [self.sbt("wslot%d" % i, [128, 4096], BF16) for i in range(self.NSLOT)]
        self.wR = [[R() for _ in range(4)] for _ in range(self.NSLOT)]
        self.wlist = []
        self.w_issued = 0
        self.w_cur = -1

    def w_plan(self, spec):
        self.wlist.append(spec)

    def w_issue_upto(self, i):
        while self.w_issued <= min(i, len(self.wlist) - 1):
            g = self.w_issued
            slot = self.wslots[g % self.NSLOT]
            rr = self.wR[g % self.NSLOT]
            nd = len(self.wlist[g])
            for di, (dst_fn, src) in enumerate(self.wlist[g]):
                ww = rr
                self.dma(dst_fn(slot), src, w=ww, q="pool")
            self.w_issued += 1

    def w_next(self):
        self.w_cur += 1
        i = self.w_cur
        self.w_issue_upto(i + self.NSLOT - 1)
        return self.wslots[i % self.NSLOT], self.wR[i % self.NSLOT]

    def build(self):
        nc = self.nc
        P = 128
        self.xp = self.din("xp", [SEQ, D])
        self.xs = self.din("xs", [SB, ST, D])
        self.st_hg = self.din("st_hg", [4, 128, SB, 128])
        self.st_gd = self.din("st_gd", [4, 128, SB, 128])
        self.st_gc = self.din("st_gc", [SB, 3, 1536])
        self.st_fc = self.din("st_fc", [SB, 2, DFF])
        self.ck = self.din("ck", [SB, NMEM, D])
        self.cv = self.din("cv", [SB, NMEM, D])
        self.memp = self.din("memp", [NMEM, D])
        self.prm_d = self.din("prm", [128, NPRM])
        self.cst_d = self.din("cst", [128, NCST])
        self.w_in = self.din("w_in", [8, 128, 4096])
        self.w_tail = self.din("w_tail", [128, 64])
        self.w_out = self.din("w_out", [2, 128, 4096])
        self.w_mq = self.din("w_mq", [2, 128, 4096])
        self.w_mkv = self.din("w_mkv", [4, 128, 4096])
        self.w_mo = self.din("w_mo", [2, 128, 4096])
        self.w_up = self.din("w_up", [11, 128, 4096])
        self.w_down = self.din("w_down", [8, 128, 22 * 128])
        self.yp = self.dout("yp", [SEQ, D])
        self.ys = self.dout("ys", [SB, ST, D])
        self.p_hg = self.dout("p_hg", [4, 128, 128])
        self.p_gd = self.dout("p_gd", [4, 128, 128])
        self.p_gc = self.dout("p_gc", [3, 1536])
        self.p_fc = self.dout("p_fc", [2, DFF])
        self.p_mk = self.dout("p_mk", [NMEM, D])
        self.p_mv = self.dout("p_mv", [NMEM, D])
        self.s_hg = self.dout("s_hg", [4, 128, SB, 128])
        self.s_gd = self.dout("s_gd", [4, 128, SB, 128])
        self.s_gc = self.dout("s_gc", [SB, 3, 1536])
        self.s_fc = self.dout("s_fc", [SB, 2, DFF])
        self.dbg_out = {k: self.dout("dbg_" + k, shp) for k, shp in self.dbg.items()}

        with self.es:
            self.alloc()
            self.setup()
            self.plan_weights()
            tiles = [dict(kind="p", N=512, ti=ti, col0=ti * 512, last=(ti == self.n_ptiles - 1)) for ti in range(self.n_ptiles)]
            if self.do_sample:
                tiles.append(dict(kind="s", N=64, ti=0, col0=0, last=True))
            for i, T_ in enumerate(tiles):
                self.next_tile = tiles[i + 1] if i + 1 < len(tiles) else None
                self.tile(T_)
            self.em.emit()
        return nc

    def alloc(self):
        nc = self.nc
        self.cst = self.sbt("cst", [128, NCST], F32); self.Rcst = R()
        self.prm = self.sbt("prm", [128, NPRM], F32); self.Rprm = R()
        self.cb = self.sbt("cb", [128, 5 * 128], BF16); self.Rcb = R()
        self.sm = self.sbt("sm", [128, 16], F32); self.Rsm = R()
        self.io = [self.sbt("io%d" % i, [128, D], F32) for i in range(2)]
        self.Rio = [R(), R()]
        self.io_i = 0
        self.yo = [self.sbt("yo%d" % i, [128, D], F32) for i in range(2)]
        self.Ryo = [R(), R()]
        self.yo_i = 0
        self.x_pref = {}
        self.hs = self.sbt("hs", [128, 8, 512], F32); self.Rhs = [R() for _ in range(8)]
        self.hb = self.sbt("hb", [128, 8, 512], BF16); self.Rhb = [R() for _ in range(8)]
        self.mq = self.sbt("mq", [128, 8, 512], BF16); self.Rmq = [R() for _ in range(8)]
        self.AR = self.sbt("arena", [128, 8192], F32)
        self.tF = Pool_([self.sbt("tF%d" % i, [128, 520], F32) for i in range(12)])
        self.tB = Pool_([self.sbt("tB%d" % i, [128, 512], BF16) for i in range(12)])
        self.g_sF = Pool_([self.sbt("sF%d" % i, [128, 128], F32) for i in range(24)])
        self.g_sBp = Pool_([self.sbt("sB%d" % i, [128, 128], BF16) for i in range(48)])
        self.h_sF = [Pool_(self.g_sF.tiles[6 * h:6 * h + 6], self.g_sF.rs[6 * h:6 * h + 6]) for h in range(4)]
        self.h_sBp = [Pool_(self.g_sBp.tiles[12 * h:12 * h + 12], self.g_sBp.rs[12 * h:12 * h + 12]) for h in range(4)]
        self.h_sBl = [Pool_([self.sbt("sBl%d_%d" % (h, i), [128, 128], BF16) for i in range(8)]) for h in range(4)]
        self.sF, self.sBp, self.sBl = self.g_sF, self.g_sBp, self.h_sBl[0]
        self.interleaving = False
        self.egs = self.sbt("egs", [128, 4, 16], F32); self.Regs = [R() for _ in range(4)]
        self.egl_all = self.sbt("egl_all", [128, 4, 8], F32); self.Regl = [[R() for _ in range(8)] for _ in range(4)]
        self.vtp = Pool_([self.sbt("vtp%d" % i, [128, 1024], BF16) for i in range(2)])
        self.w_init()
        self.mkF = self.sbt("mkF", [128, 8, 256], BF16); self.RmkF = R()
        self.mvT = self.sbt("mvT", [128, 2, D], BF16); self.RmvT = R()
        self.Shg = self.sbt("Shg", [128, 4, 128], F32); self.RShg = [R() for _ in range(4)]
        self.Shgb = self.sbt("Shgb", [128, 4, 128], BF16); self.RShgb = [R() for _ in range(4)]
        self.Sgd = self.sbt("Sgd", [128, 4, 128], F32); self.RSgd = [R() for _ in range(4)]
        self.Sgdb = self.sbt("Sgdb", [128, 4, 128], BF16); self.RSgdb = [R() for _ in range(4)]
        self.gdh = self.sbt("gdh", [128, 12, 48], F32); self.Rgdh = [R() for _ in range(12)]
        self.ffh = self.sbt("ffh", [128, 22, 32], F32); self.Rffh = [R() for _ in range(22)]
        self.tms = self.sbt("tms", [64, 8, 20], F32); self.Rtms = R()
        self.rowg = self.sbt("rowg", [4, 512], F32); self.Rrowg = R()
        self.psb = [self.es.enter_context(nc.psum_tensor("psb%d" % i, [128, 512], F32)) for i in range(8)]
        self.Rps = [R(excl=True) for _ in range(8)]
        self.g_ps_free = list(range(8))
        self.ps_free = self.g_ps_free
        self.arena_rs = []

    def aview(self, off_f32, nelem_f32, dt, pattern=None, **kw):
        ap = self.AR[:, off_f32:off_f32 + nelem_f32]
        if dt == BF16:
            ap = ap.bitcast(BF16)
        if pattern:
            ap = ap.rearrange(pattern, **kw)
        return ap

    def arena_switch(self, new_rs):
        alias(new_rs, self.arena_rs)
        self.arena_rs = list(new_rs)

    def setup(self):
        cst, prm = self.cst, self.prm
        self.dma(cst[:], self.cst_d, w=[self.Rcst])
        self.dma(prm[:], self.prm_d, w=[self.Rprm])
        self.identf = cst[:, C_ID:C_ID + 128]
        cb = self.cb
        self.identb = cb[:, 0:128]
        self.ones1 = cb[:, 128:256]
        self.ones128 = cb[:, 256:384]
        self.ones1024 = cb[:, 384:512]
        self.cp("dve", self.identb, self.identf, r=[self.Rcst], w=[self.Rcb])
        self.mx = self.sbt("mx", [64, 256], F32); self.Rmx = R()
        self.mxb = self.sbt("mxb", [64, 128], BF16)
        self.Mb1x2 = self.mx[:, 0:128]
        self.Mb2x2 = self.mx[:, 128:256]
        self.idb2 = self.mxb[:, 0:128]
        for j in range(2):
            self.cp("dve", self.mx[:, j * 64:(j + 1) * 64], cst[0:64, C_MB1:C_MB1 + 64], r=[self.Rcst], w=[self.Rmx])
            self.cp("dve", self.mx[:, 128 + j * 64:128 + (j + 1) * 64], cst[0:64, C_MB2:C_MB2 + 64], r=[self.Rcst], w=[self.Rmx])
            self.cp("dve", self.mxb[:, j * 64:(j + 1) * 64], self.identf[0:64, 0:64], r=[self.Rcst], w=[self.Rmx])
        self.memset(self.ones1, 1.0, w=[self.Rcb])
        self.memset(self.ones128, 1.0 / 128.0, w=[self.Rcb])
        self.memset(self.ones1024, 1.0 / 1024.0, w=[self.Rcb])
        sm = self.sm
        self.tt(sm[:, 0:4], prm[:, P_L0:P_L0 + 4], prm[:, P_L1:P_L1 + 4], ALU.subtract, r=[self.Rprm], w=[self.Rsm])
        self.act(sm[:, 4:8], sm[:, 0:4], AF.Sigmoid, r=[self.Rsm], w=[self.Rsm], scale=-1.0)
        self.act(sm[:, 0:4], sm[:, 0:4], AF.Sigmoid, r=[self.Rsm], w=[self.Rsm])
        self.act(sm[0:4, 8:9], prm[0:4, P_ALOG:P_ALOG + 1], AF.Exp, r=[self.Rprm], w=[self.Rsm])
        self.ts(sm[0:4, 8:9], sm[0:4, 8:9], -1.0, None, ALU.mult, None, r=[self.Rsm], w=[self.Rsm])
        self.memset(sm[:, 9:10], LN_EPS, w=[self.Rsm])
        self.memset(sm[:, 10:11], RMS_EPS, w=[self.Rsm])
        self.memset(sm[:, 11:12], 1.0, w=[self.Rsm])
        self.one_col = sm[:, 11:12]
        self.eps_ln = sm[:, 9:10]
        self.eps_rms = sm[:, 10:11]
        self.lb = lambda h: sm[:, h:h + 1]
        self.oml = lambda h: sm[:, 4 + h:5 + h]
        for h in range(4):
            self.memset(self.Shg[:, h, :], 0.0, w=[self.RShg[h]])
            self.memset(self.Shgb[:, h, :], 0.0, w=[self.RShgb[h]])
            self.memset(self.Sgd[:, h, :], 0.0, w=[self.RSgd[h]])
            self.memset(self.Sgdb[:, h, :], 0.0, w=[self.RSgdb[h]])
        self.memset(self.gdh[:], 0.0, w=self.Rgdh)
        self.memset(self.ffh[:], 0.0, w=self.Rffh)

    def plan_weights(self):
        def v3(slot, n):
            return slot[:, 0:8 * n].rearrange("p (k n) -> p k n", n=n)

        full = lambda src: [(lambda s: s[:, 0:4096], src)]
        for g in range(4):
            self.w_plan(full(self.w_mkv[g]))
        ntile = self.n_ptiles + (1 if self.do_sample else 0)
        for _ in range(ntile):
            for h in range(4):
                self.w_plan(full(self.w_in[h]))
            self.w_plan([(lambda s: s[:, 0:64], self.w_tail)])
            for h in range(4):
                self.w_plan(full(self.w_in[4 + h]))
            for g in range(2):
                self.w_plan(full(self.w_out[g]))
            for g in range(2):
                self.w_plan(full(self.w_mq[g]))
            for g in range(2):
                self.w_plan(full(self.w_mo[g]))
            for g in range(11):
                self.w_plan(full(self.w_up[g]))
            for g in range(8):
                self.w_plan([(lambda s: s[:, 0:22 * 128], self.w_down[g])])
        self.v3 = v3

    def mark(self, name):
        if not hasattr(self, 'marks'):
            self.marks = []
        self.marks.append((name, len(self.em.nodes)))

    def tile(self, T):
        self.T = T
        self.mark('tile_%s%d' % (T['kind'], T['ti']))
        upto = getattr(self, "upto", 99)
        if upto >= 0 and T["kind"] == "p" and T["ti"] == 0:
            self.stage_memkv()
        if upto >= 1:
            if T["kind"] == "s":
                self.stage_sample_hist()
            self.stage_load_x()
            if T["kind"] == "s" and "hs_s" in self.dbg_out:
                for k in range(8):
                    self.dma(self.dbg_out["hs_s"][:, k, :], self.hs[:, k, 0:64], r=[self.Rhs[k]])
        self.mark('mixer')
        if upto >= 2:
            self.stage_mixer()
        def skip(n):
            for _ in range(n):
                self.w_next()
        if upto < 2:
            skip(9)
        elif upto < 3:
            skip(5)
        elif upto < 4:
            skip(4)
        self.mark('wout')
        if upto >= 5:
            self.stage_wout_ln1()
        else:
            skip(2)
        self.mark('attn')
        if upto >= 6:
            self.stage_attn_ln2()
        else:
            skip(4)
        self.mark('ffn')
        if upto >= 7:
            self.stage_ffn_ln3()
        else:
            skip(19)
        self.mark('store')
        if upto >= 8:
            self.stage_store_y()

    def layer_norm(self, N, gcol, bcol, shadow):
        hs, hb = self.hs, self.hb
        b_mean = self.ps_alloc(); b_sq = self.ps_alloc()
        pm, pq = self.psb[b_mean], self.psb[b_sq]
        for m in range(8):
            yb, Ryb = self.tB.get()
            ysq, Rysq = self.tB.get()
            self.cp("act", yb[:, :N], hs[:, m, :N], r=[self.Rhs[m]], w=[Ryb])
            self.act(ysq[:, :N], hs[:, m, :N], AF.Square, r=[self.Rhs[m]], w=[Rysq])
            self.mm(pm[:, :N], self.ones1024, yb[:, :N], r=[self.Rcb, Ryb], w=[self.Rps[b_mean]], start=(m == 0), stop=(m == 7))
            self.mm(pq[:, :N], self.ones1024, ysq[:, :N], r=[self.Rcb, Rysq], w=[self.Rps[b_sq]], start=(m == 0), stop=(m == 7))
        mean, Rmean = self.tF.get()
        var, Rvar = self.tF.get()
        rstd, Rrstd = self.tF.get()
        nmr, Rnmr = self.tF.get()
        self.cp("act", mean[:, :N], pm[:, :N], r=[self.Rps[b_mean]], w=[Rmean])
        self.tt(var[:, :N], mean[:, :N], mean[:, :N], ALU.mult, r=[Rmean], w=[Rvar])
        self.tt(var[:, :N], pq[:, :N], var[:, :N], ALU.subtract, r=[self.Rps[b_sq], Rvar], w=[Rvar])
        self.ps_release(b_mean); self.ps_release(b_sq)
        self.act(var[:, :N], var[:, :N], AF.Ln, r=[Rvar], w=[Rvar], bias=self.eps_ln)
        self.act(rstd[:, :N], var[:, :N], AF.Exp, r=[Rvar], w=[Rrstd], scale=-0.5)
        self.tt(nmr[:, :N], mean[:, :N], rstd[:, :N], ALU.mult, r=[Rmean, Rrstd], w=[Rnmr])
        for m in range(8):
            t1, Rt1 = self.tF.get()
            self.tt(t1[:, :N], hs[:, m, :N], rstd[:, :N], ALU.mult, r=[self.Rhs[m], Rrstd], w=[Rt1])
            self.tt(t1[:, :N], t1[:, :N], nmr[:, :N], ALU.subtract, r=[Rt1, Rnmr], w=[Rt1])
            g = self.prm[:, gcol + m:gcol + m + 1]
            b = self.prm[:, bcol + m:bcol + m + 1]
            self.act(hs[:, m, :N], t1[:, :N], AF.Identity, r=[Rt1, self.Rprm], w=[self.Rhs[m]], scale=g, bias=b)
            if shadow:
                self.act(hb[:, m, :N], t1[:, :N], AF.Identity, r=[Rt1, self.Rprm], w=[self.Rhb[m]], scale=g, bias=b)

    def proj_residual(self, src, Rsrc, nk, N, ngroups, slot_view, cols_per_group):
        mpg = cols_per_group // 128
        for g in range(ngroups):
            slot, Rw = self.w_next()
            wv = slot_view(slot)
            for mi in range(mpg):
                m = g * mpg + mi
                b = self.ps_alloc(); ps = self.psb[b]
                for k in range(nk):
                    self.mm(ps[:, :N], wv[:, k, mi * 128:(mi + 1) * 128], src[:, k, :N], r=Rw + [Rsrc[k]], w=[self.Rps[b]], start=(k == 0), stop=(k == nk - 1))
                self.stt(self.hs[:, m, :N], self.hs[:, m, :N], ALPHA, ps[:, :N], ALU.mult, ALU.add, r=[self.Rhs[m], self.Rps[b]], w=[self.Rhs[m]])
                self.ps_release(b)

    def stage_memkv(self):
        N = NMEM
        for blk in range(2):
            io, Rio = self.io[blk], self.Rio[blk]
            self.dma(io[:], self.memp[blk * 128:(blk + 1) * 128, :], w=[Rio])
            for half in range(2):
                b = self.ps_alloc(); ps = self.psb[b]
                for kk in range(4):
                    k = half * 4 + kk
                    self.tr(ps[:, kk * 128:(kk + 1) * 128], io[:, k * 128:(k + 1) * 128], self.identf, r=[Rio, self.Rcst], w=[self.Rps[b]])
                for kk in range(4):
                    k = half * 4 + kk
                    self.cp("act" if kk % 2 else "dve", self.mq[:, k, blk * 128:(blk + 1) * 128], ps[:, kk * 128:(kk + 1) * 128], r=[self.Rps[b]], w=[self.Rmq[k]])
                self.ps_release(b)
        v3 = self.v3
        mku = getattr(self, "mku", 99)
        if mku < 1:
            return
        for g in range(4):
            if g >= mku:
                return
            slot, Rw = self.w_next()
            wv = v3(slot, 512)
            if g < 2:
                for mi in range(4):
                    dch = g * 4 + mi
                    b = self.ps_alloc(); ps = self.psb[b]
                    for k in range(8):
                        self.mm(ps[:, :N], wv[:, k, mi * 128:(mi + 1) * 128], self.mq[:, k, :N], r=Rw + [self.Rmq[k]], w=[self.Rps[b]], start=(k == 0), stop=(k == 7))
                    self.cp("act", self.mkF[:, dch, :], ps[:, :N], r=[self.Rps[b]], w=[self.RmkF])
                    self.ps_release(b)
            for blk in range(2):
                b = self.ps_alloc(); ps = self.psb[b]
                for k in range(8):
                    self.mm(ps[:, :], self.mq[:, k, blk * 128:(blk + 1) * 128], wv[:, k, :], r=Rw + [self.Rmq[k]], w=[self.Rps[b]], start=(k == 0), stop=(k == 7))
                o, Ro = self.tF.get()
                self.cp("dve", o[:, 0:512], ps[:, :], r=[self.Rps[b]], w=[Ro])
                if g >= 2:
                    self.cp("act", self.mvT[:, blk, (g - 2) * 512:(g - 1) * 512], ps[:, :], r=[self.Rps[b]], w=[self.RmvT])
                self.ps_release(b)
                dst = self.p_mk if g < 2 else self.p_mv
                self.dma(dst[blk * 128:(blk + 1) * 128, (g % 2) * 512:(g % 2 + 1) * 512], o[:, 0:512], r=[Ro])

    def x_block_dma(self, T, blk):
        i = self.io_i; self.io_i ^= 1
        io, Rio = self.io[i], self.Rio[i]
        if T["kind"] == "p":
            r0 = T["col0"] + blk * 128
            self.dma(io[:], self.xp[r0:r0 + 128, :], w=[Rio])
        else:
            for t in range(ST):
                self.dma(io[t * SB:(t + 1) * SB, :], self.xs[:, t, :], w=[Rio])
        return io, Rio

    def stage_load_x(self):
        T = self.T
        N = T["N"]
        nblk = (N + 127) // 128
        key = (T["kind"], T["ti"])
        for blk in range(nblk):
            if (key, blk) in self.x_pref:
                io, Rio = self.x_pref.pop((key, blk))
            else:
                io, Rio = self.x_block_dma(T, blk)
            np_ = 128 if T["kind"] == "p" else 64
            for half in range(2):
                b = self.ps_alloc(); ps = self.psb[b]
                for kk in range(4):
                    k = half * 4 + kk
                    self.tr(ps[:, kk * 128:kk * 128 + np_], io[0:np_, k * 128:(k + 1) * 128], self.identf[0:np_, 0:np_], r=[Rio, self.Rcst], w=[self.Rps[b]])
                c0 = blk * 128
                k0 = half * 4
                self.cp("act", self.hs[:, k0:k0 + 4, c0:c0 + np_], ps[:, :].rearrange("p (k n) -> p k n", n=128)[:, :, 0:np_], r=[self.Rps[b]], w=self.Rhs[k0:k0 + 4])
                self.ps_release(b)
                self.cp("dve", self.hb[:, k0:k0 + 4, c0:c0 + np_], self.hs[:, k0:k0 + 4, c0:c0 + np_], r=self.Rhs[k0:k0 + 4], w=self.Rhb[k0:k0 + 4])
        nxt = self.next_tile
        if nxt is not None:
            nb = 2 if nxt["kind"] == "p" else 1
            for blk in range(nb):
                self.x_pref[((nxt["kind"], nxt["ti"]), blk)] = self.x_block_dma(nxt, blk)

    def stage_sample_hist(self):
        for (src, nrow, ncol, halo, Rh) in ((self.st_gc, 3, 1536, self.gdh, self.Rgdh), (self.st_fc, 2, DFF, self.ffh, self.Rffh)):
            np_ = nrow * SB
            for c0 in range(0, ncol, 1024):
                w_ = min(1024, ncol - c0)
                i = self.yo_i; self.yo_i ^= 1
                io, Rio = self.yo[i], self.Ryo[i]
                for r_ in range(nrow):
                    self.dma(io[r_ * SB:(r_ + 1) * SB, 0:w_], src[:, r_, c0:c0 + w_], w=[Rio])
                for jj in range(w_ // 128):
                    j = c0 // 128 + jj
                    b = self.ps_alloc(); ps = self.psb[b]
                    self.tr(ps[:, 0:np_], io[0:np_, jj * 128:(jj + 1) * 128], self.identf[0:np_, 0:np_], r=[Rio, self.Rcst], w=[self.Rps[b]])
                    self.cp("act" if jj % 2 else "dve", halo[:, j, 0:np_], ps[:, 0:np_], r=[self.Rps[b]], w=[Rh[j]])
                    self.ps_release(b)

    def stage_wout_ln1(self):
        N = self.T["N"]
        self.proj_residual(self.mq, self.Rmq, 8, N, 2, lambda s: self.v3(s, 512), 512)
        self.layer_norm(N, P_LN + 0, P_LN + 8, shadow=True)

    def stage_store_y(self):
        T = self.T
        N = T["N"]
        nblk = (N + 127) // 128
        for blk in range(nblk):
            np_ = min(128, N - blk * 128)
            i = self.yo_i; self.yo_i ^= 1
            io, Rio = self.yo[i], self.Ryo[i]
            for half in range(2):
                b = self.ps_alloc(); ps = self.psb[b]
                for kk in range(4):
                    k = half * 4 + kk
                    self.tr(ps[0:np_, kk * 128:(kk + 1) * 128], self.hs[:, k, blk * 128:blk * 128 + np_], self.identf, r=[self.Rhs[k], self.Rcst], w=[self.Rps[b]])
                self.cp("act" if half else "dve", io[0:np_, half * 512:(half + 1) * 512], ps[0:np_, :], r=[self.Rps[b]], w=[Rio])
                self.ps_release(b)
            if T["kind"] == "p":
                r0 = T["col0"] + blk * 128
                self.dma(self.yp[r0:r0 + 128, :], io[:], r=[Rio])
            else:
                for t in range(ST):
                    self.dma(self.ys[:, t, :], io[t * SB:(t + 1) * SB, :], r=[Rio])

    def stage_mixer(self):
        T = self.T
        samp = T["kind"] == "s"
        upto = getattr(self, "upto", 99)
        self.unit = SB if samp else 1
        self.nch = 1 if samp else T["N"] // 64
        self.mo = 64 if samp else 0
        if samp:
            self.sst2 = [self.aview(0, 2048, F32, "p (b v) -> p b v", v=128), self.aview(6144, 2048, F32, "p (b v) -> p b v", v=128)]
            self.Rsst2 = [R(), R()]
            self.sstb2 = [self.aview(2048, 1024, BF16, "p (b v) -> p b v", v=128), self.aview(3072, 1024, BF16, "p (b v) -> p b v", v=128)]
            self.vblk2 = [self.aview(4096, 1024, BF16), self.aview(5120, 1024, BF16)]
            self.Rsstb2, self.Rvblk2 = [R(), R()], [R(), R()]
            self.arena_switch(self.Rsst2 + self.Rsstb2 + self.Rvblk2)
            self.sst_q = [(self.st_hg, h) for h in range(4)] + [(self.st_gd, h) for h in range(4)]
            self.sst_issued = 0
            self.sst_cur = -1
            self.issue_state_load()
            self.set_ctx(0)
            for h in range(4):
                P = self.hgrn_prelude(h, None)
                for _ in self.hgrn_loop(h, P):
                    pass
                self.hgrn_epilogue(h, P)
            self.mark('gdn')
            if upto >= 3:
                self.gdn_rows()
            if upto >= 4:
                for h in range(4):
                    P = self.gdn_prelude(h, None)
                    for _ in self.gdn_loop(h, P):
                        pass
                    self.gdn_epilogue(h, P)
            self.set_ctx(None)
            return
        views = []
        allr = []
        for h in range(4):
            base = h * 2048
            V = dict(qt=self.aview(base, 256, BF16), kt=self.aview(base + 256, 256, BF16), kg=self.aview(base + 512, 256, BF16),
                     sg=self.aview(base + 768, 256, BF16), vt=self.aview(base + 1024, 512, BF16), oc=self.aview(base + 1536, 512, F32))
            for k in list(V.keys()):
                V["R" + k] = R()
                allr.append(V["R" + k])
            views.append(V)
        self.arena_switch(allr)
        Ps = [self.hgrn_prelude(h, views[h]) for h in range(4)]
        for _ in self.hgrn_loop_batched(Ps):
            pass
        for h in range(4):
            self.hgrn_epilogue(h, Ps[h])
        self.mark('gdn')
        if upto >= 3:
            self.gdn_rows()
        if upto >= 4:
            views = []
            allr = []
            for h in range(4):
                base = h * 1536
                V = dict(qn=self.aview(base, 256, BF16), kn=self.aview(base + 256, 256, BF16), vv=self.aview(base + 512, 256, BF16),
                         sg=self.aview(base + 768, 256, BF16), oc=self.aview(base + 1024, 512, F32))
                for k in list(V.keys()):
                    V["R" + k] = R()
                    allr.append(V["R" + k])
                views.append(V)
            self.arena_switch(allr)
            Ps = [self.gdn_prelude(h, views[h]) for h in range(4)]
            self.run_interleaved([self.gdn_loop_pair(h, Ps[h]) for h in range(4)])
            for h in range(4):
                self.gdn_epilogue(h, Ps[h])

    def set_ctx(self, h):
        if h is None:
            self.sF, self.sBp, self.ps_free = self.g_sF, self.g_sBp, self.g_ps_free
            self.sBl = self.h_sBl[0]
        else:
            self.sF, self.sBp, self.sBl = self.h_sF[h], self.h_sBp[h], self.h_sBl[h]
            self.ps_free = self.h_ps_free[h] if self.interleaving else self.g_ps_free

    def run_interleaved(self, gens):
        assert len(self.g_ps_free) == 8, "PSUM banks must all be free before interleaving"
        self.interleaving = True
        self.h_ps_free = [[2 * h, 2 * h + 1] for h in range(4)]
        active = list(range(len(gens)))
        for i in list(active):
            self.set_ctx(i)
            for _ in range(i * getattr(self, "stagger", 1)):
                try:
                    next(gens[i])
                except StopIteration:
                    active.remove(i)
                    break
        while active:
            for i in list(active):
                self.set_ctx(i)
                try:
                    next(gens[i])
                except StopIteration:
                    active.remove(i)
        for h in range(4):
            assert sorted(self.h_ps_free[h]) == [2 * h, 2 * h + 1]
        self.interleaving = False
        self.g_ps_free[:] = list(range(8))
        self.set_ctx(None)

    def proj_fm(self, wv, Rw, c0, N, M=128):
        b = self.ps_alloc(); ps = self.psb[b]
        for k in range(8):
            self.mm(ps[0:M, :N], wv[:, k, c0:c0 + M], self.hb[:, k, :N], r=Rw + [self.Rhb[k]], w=[self.Rps[b]], start=(k == 0), stop=(k == 7))
        return b

    def cumsum_rows(self, out, Rout, src, Rsrc, npart, N):
        if self.T["kind"] == "p":
            msk = self.cst[0:npart, C_SCAN:C_SCAN + N]
            self.em.op("dve", lambda e: e.tensor_tensor_scan(out=out[0:npart, :N], data0=msk, data1=src[0:npart, :N], initial=0.0, op0=ALU.mult, op1=ALU.add), r=[Rsrc, self.Rcst], w=[Rout], dur=0.12 + N / 960.0)
        else:
            self.cp("dve", out[0:npart, 0:SB], src[0:npart, 0:SB], r=[Rsrc], w=[Rout])
            for t in range(1, ST):
                self.tt(out[0:npart, t * SB:(t + 1) * SB], out[0:npart, (t - 1) * SB:t * SB], src[0:npart, t * SB:(t + 1) * SB], ALU.add, r=[Rsrc, Rout], w=[Rout])

    def rms_gate_out(self, oc, Roc, N, sg, Rsg, gcol, mchunk):
        sq, Rsq = self.tB.get()
        self.act(sq[:, :N], oc[:, :N], AF.Square, r=[Roc], w=[Rsq])
        b2 = self.ps_alloc(); pm = self.psb[b2]
        self.mm(pm[:, :N], self.ones128, sq[:, :N], r=[self.Rcb, Rsq], w=[self.Rps[b2]])
        rs, Rrs = self.tF.get()
        self.act(rs[:, :N], pm[:, :N], AF.Ln, r=[self.Rps[b2]], w=[Rrs], bias=self.eps_rms)
        self.ps_release(b2)
        self.act(rs[:, :N], rs[:, :N], AF.Exp, r=[Rrs], w=[Rrs], scale=-0.5)
        t1, Rt1 = self.tF.get()
        self.tt(t1[:, :N], oc[:, :N], rs[:, :N], ALU.mult, r=[Roc, Rrs], w=[Rt1])
        self.stt(self.mq[:, mchunk, :N], t1[:, :N], self.prm[:, gcol:gcol + 1], sg[:, :N], ALU.mult, ALU.mult, r=[Rt1, Rsg, self.Rprm], w=[self.Rmq[mchunk]])

    def issue_state_load(self):
        if self.sst_issued < len(self.sst_q):
            src, h = self.sst_q[self.sst_issued]
            i = self.sst_issued % 2
            self.dma(self.sst2[i], src[h], w=[self.Rsst2[i]])
            self.sst_issued += 1

    def load_sample_state(self, src, h):
        self.sst_cur += 1
        assert self.sst_q[self.sst_cur][1] == h
        i = self.sst_cur % 2
        self.sst, self.Rsst = self.sst2[i], self.Rsst2[i]
        self.sstb, self.Rsstb = self.sstb2[i], self.Rsstb2[i]
        self.vblk, self.Rvblk = self.vblk2[i], self.Rvblk2[i]
        self.snew, self.Rsnew = self.sst, self.Rsst
        self.cp("act", self.sstb, self.sst, r=[self.Rsst], w=[self.Rsstb])
        self.issue_state_load()

    def sample_state_update(self, ktm, Rktm, vsrc, Rvsrc, eg16, Reg, dst, h):
        vb3 = self.vblk[0:64, :].rearrange("p (b v) -> p b v", v=128)
        bm = self.cst[0:64, C_BM:C_BM + SB]
        self.tt(vb3, vsrc.unsqueeze(1).to_broadcast([64, SB, 128]), bm.unsqueeze(2).to_broadcast([64, SB, 128]), ALU.mult, r=[Rvsrc, self.Rcst], w=[self.Rvblk])
        for k4 in range(4):
            b = self.ps_alloc(); ps = self.psb[b]
            self.mm(ps[:, :], ktm, self.vblk[0:64, k4 * 512:(k4 + 1) * 512], r=[Rktm, self.Rvblk], w=[self.Rps[b]])
            sl = slice(4 * k4, 4 * k4 + 4)
            self.tt(self.snew[:, sl, :], self.sst[:, sl, :], eg16[:, sl].unsqueeze(2).to_broadcast([128, 4, 128]), ALU.mult, r=[self.Rsst, Reg], w=[self.Rsnew])
            self.tt(self.snew[:, sl, :], self.snew[:, sl, :], ps[:, :].rearrange("p (b v) -> p b v", v=128), ALU.add, r=[self.Rsnew, self.Rps[b]], w=[self.Rsnew])
            self.ps_release(b)
        self.dma(dst[h], self.snew, r=[self.Rsnew])

    def hgrn_prelude(self, h, V):
        T = self.T
        N = T["N"]; samp = T["kind"] == "s"; nch = self.nch
        slot, Rw = self.w_next()
        wv = self.v3(slot, 512)
        P = dict(h=h)
        if V is None:
            for k, pool in (("qt", self.tB), ("kt", self.tB), ("kg", self.tB), ("sg", self.tB), ("vt", self.vtp), ("oc", self.tF)):
                t_, r_ = pool.get()
                P[k], P["R" + k] = t_[:], r_
        else:
            P.update(V)
        qt, kt, kg, sg, vt, Rqt, Rkt, Rkg, Rsg, Rvt = P["qt"], P["kt"], P["kg"], P["sg"], P["vt"], P["Rqt"], P["Rkt"], P["Rkg"], P["Rsg"], P["Rvt"]
        vt3 = vt.rearrange("p (c n) -> p c n", n=128)
        P["vt3"] = vt3
        bv = self.proj_fm(wv, Rw, 384, N)
        vfm, Rvfm = self.tB.get()
        self.cp("act", vfm[:, :N], self.psb[bv][:, :N], r=[self.Rps[bv]], w=[Rvfm])
        self.ps_release(bv)
        b = self.ps_alloc(); pb_ = self.psb[b][:].bitcast(BF16)
        for c in range(nch):
            self.tr(pb_[0:64, c * 128:(c + 1) * 128], vfm[:, c * 64:(c + 1) * 64], self.identb, r=[Rvfm, self.Rcb], w=[self.Rps[b]])
        self.cp("dve", vt[0:64, 0:nch * 128], pb_[0:64, 0:nch * 128], r=[self.Rps[b]], w=[Rvt])
        self.ps_release(b)
        bq = self.proj_fm(wv, Rw, 0, N)
        qf, Rqf = self.tF.get()
        self.act(qf[:, :N], self.psb[bq][:, :N], AF.Silu, r=[self.Rps[bq]], w=[Rqf])
        self.ps_release(bq)
        bg = self.proj_fm(wv, Rw, 256, N)
        self.act(sg[:, :N], self.psb[bg][:, :N], AF.Silu, r=[self.Rps[bg]], w=[Rsg])
        self.ps_release(bg)
        bf = self.proj_fm(wv, Rw, 128, N)
        ee, Ree = self.tF.get(); l1, Rl1 = self.tF.get(); l2, Rl2 = self.tF.get()
        self.act(ee[:, :N], self.psb[bf][:, :N], AF.Exp, r=[self.Rps[bf]], w=[Ree], scale=-1.0)
        self.ps_release(bf)
        self.act(l1[:, :N], ee[:, :N], AF.Ln, r=[Ree, self.Rsm], w=[Rl1], scale=self.lb(h), bias=self.one_col)
        self.act(l2[:, :N], ee[:, :N], AF.Ln, r=[Ree, self.Rsm], w=[Rl2], bias=self.one_col)
        lf, Rlf = l1, Rl1
        self.tt(lf[:, :N], l1[:, :N], l2[:, :N], ALU.subtract, r=[Rl1, Rl2], w=[Rlf])
        sn, Rsn = l2, Rl2
        self.act(sn[:, :N], l2[:, :N], AF.Exp, r=[Rl2], w=[Rsn], scale=-1.0)
        self.tt(sn[:, :N], sn[:, :N], ee[:, :N], ALU.mult, r=[Rsn, Ree], w=[Rsn])
        g, Rg = self.tF.get()
        self.cumsum_rows(g, Rg, lf, Rlf, 128, N)
        ep, Rep = self.tF.get()
        en, Ren = g, Rg
        self.act(ep[:, :N], g[:, :N], AF.Exp, r=[Rg], w=[Rep])
        self.act(en[:, :N], g[:, :N], AF.Exp, r=[Rg], w=[Ren], scale=-1.0)
        self.tt(qt[:, :N], qf[:, :N], ep[:, :N], ALU.mult, r=[Rqf, Rep], w=[Rqt])
        self.stt(kt[:, :N], sn[:, :N], self.oml(h), en[:, :N], ALU.mult, ALU.mult, r=[Rsn, Ren, self.Rsm], w=[Rkt])
        eg = self.egs[:, h, :]
        if not samp:
            ep3 = ep[:, :N].rearrange("p (c j) -> p c j", j=64)
            self.cp("dve", eg[:, 0:nch].unsqueeze(2), ep3[:, :, 63:64], r=[Rep], w=[self.Regs[h]])
            self.tt(kg[:, :N].rearrange("p (c j) -> p c j", j=64), kt[:, :N].rearrange("p (c j) -> p c j", j=64),
                    ep3[:, :, 63:64].to_broadcast([128, nch, 64]), ALU.mult, r=[Rkt, Rep], w=[Rkg])
        else:
            self.cp("dve", eg[:, 0:SB], ep[:, 48:64], r=[Rep], w=[self.Regs[h]])
            self.tt(kg[:, :N].rearrange("p (t s) -> p t s", s=SB), kt[:, :N].rearrange("p (t s) -> p t s", s=SB),
                    ep[:, 48:64].unsqueeze(1).to_broadcast([128, ST, SB]), ALU.mult, r=[Rkt, Rep], w=[Rkg])
            self.load_sample_state(self.st_hg, h)
        return P

    def hgrn_loop(self, h, P):
        T = self.T
        N = T["N"]; samp = T["kind"] == "s"; nch = self.nch; mo = self.mo
        qt, kt, kg, vt3, oc = P["qt"], P["kt"], P["kg"], P["vt3"], P["oc"]
        Rqt, Rkt, Rkg, Rvt, Roc = P["Rqt"], P["Rkt"], P["Rkg"], P["Rvt"], P["Roc"]
        maskT = self.cst[0:64, C_MT + mo:C_MT + mo + 64]
        eg = self.egs[:, h, :]; Reg = self.Regs[h]
        for c in range(nch):
            cs = slice(c * 64, (c + 1) * 64)
            b = self.ps_alloc(); ps = self.psb[b]
            self.mm(ps[0:64, 0:64], kt[:, cs], qt[:, cs], r=[Rkt, Rqt], w=[self.Rps[b]])
            b2 = self.ps_alloc(); psb_ = self.psb[b2][:].bitcast(BF16)
            self.tr(psb_[0:64, 0:128], kg[:, cs], self.identb, r=[Rkg, self.Rcb], w=[self.Rps[b2]])
            yield
            sT, RsT = self.sBp.get()
            self.tt(sT[0:64, 0:64], ps[0:64, 0:64], maskT, ALU.mult, r=[self.Rps[b], self.Rcst], w=[RsT])
            self.ps_release(b)
            ktm, Rktm = self.sBp.get()
            self.cp("act", ktm[0:64, :], psb_[0:64, 0:128], r=[self.Rps[b2]], w=[Rktm])
            self.ps_release(b2)
            yield
            bo = self.ps_alloc(); po = self.psb[bo]
            self.mm(po[:, 0:64], vt3[0:64, c, :], sT[0:64, 0:64], r=[Rvt, RsT], w=[self.Rps[bo]], start=True, stop=False)
            if not samp:
                self.mm(po[:, 0:64], self.Shgb[:, h, :], qt[:, cs], r=[self.RShgb[h], Rqt], w=[self.Rps[bo]], start=False, stop=True)
                b = self.ps_alloc(); ps = self.psb[b]
                self.mm(ps[:, 0:128], ktm[0:64, :], vt3[0:64, c, :], r=[Rktm, Rvt], w=[self.Rps[b]])
                yield
                self.stt(self.Shg[:, h, :], self.Shg[:, h, :], eg[:, c:c + 1], ps[:, 0:128], ALU.mult, ALU.add, r=[self.RShg[h], Reg, self.Rps[b]], w=[self.RShg[h]])
                self.ps_release(b)
                self.cp("act", oc[:, cs], po[:, 0:64], r=[self.Rps[bo]], w=[Roc])
                self.ps_release(bo)
                yield
                self.cp("act", self.Shgb[:, h, :], self.Shg[:, h, :], r=[self.RShg[h]], w=[self.RShgb[h]])
                yield
            else:
                for sq_ in range(SB):
                    self.mm(po[:, sq_:64:SB], self.sstb[:, sq_, :], qt[:, sq_:64:SB], r=[self.Rsstb, Rqt], w=[self.Rps[bo]], start=False, stop=(sq_ == SB - 1))
                self.cp("act", oc[:, 0:64], po[:, 0:64], r=[self.Rps[bo]], w=[Roc])
                self.ps_release(bo)
                self.sample_state_update(ktm[0:64, :], Rktm, vt3[0:64, 0, :], Rvt, eg[:, 0:SB], Reg, self.s_hg, h)
                yield


    def hgrn_loop_batched(self, Ps):
        T = self.T
        nch = self.nch
        maskT = self.cst[0:64, C_MT:C_MT + 64]
        RS, RSb = self.RShg, self.RShgb
        Rq = [P["Rqt"] for P in Ps]; Rk = [P["Rkt"] for P in Ps]; Rkg = [P["Rkg"] for P in Ps]; Rv = [P["Rvt"] for P in Ps]; Roc = [P["Roc"] for P in Ps]
        oc_all = self.AR[:, 0:8192].rearrange("p (h x) -> p h x", h=4)
        for c in range(nch):
            cs = slice(c * 64, (c + 1) * 64)
            b1 = self.ps_alloc(); ps1 = self.psb[b1]
            b2 = self.ps_alloc(); pb2 = self.psb[b2][:].bitcast(BF16)
            for h, P in enumerate(Ps):
                self.mm(ps1[0:64, h * 64:(h + 1) * 64], P["kt"][:, cs], P["qt"][:, cs], r=[Rk[h], Rq[h]], w=[self.Rps[b1]])
            for h, P in enumerate(Ps):
                self.tr(pb2[0:64, h * 128:(h + 1) * 128], P["kg"][:, cs], self.identb, r=[Rkg[h], self.Rcb], w=[self.Rps[b2]])
            sT, RsT = self.tB.get()
            self.tt(sT[0:64, 0:256].rearrange("p (h s) -> p h s", h=4), ps1[0:64, 0:256].rearrange("p (h s) -> p h s", h=4),
                    maskT.unsqueeze(1).to_broadcast([64, 4, 64]), ALU.mult, r=[self.Rps[b1], self.Rcst], w=[RsT])
            self.ps_release(b1)
            ktm, Rktm = self.tB.get()
            self.cp("act", ktm[0:64, 0:512], pb2[0:64, 0:512], r=[self.Rps[b2]], w=[Rktm])
            self.ps_release(b2)
            bo = self.ps_alloc(); po = self.psb[bo]
            bd = self.ps_alloc(); pd = self.psb[bd]
            for h, P in enumerate(Ps):
                self.mm(po[:, h * 64:(h + 1) * 64], P["vt3"][0:64, c, :], sT[0:64, h * 64:(h + 1) * 64], r=[Rv[h], RsT], w=[self.Rps[bo]], start=True, stop=False)
                self.mm(po[:, h * 64:(h + 1) * 64], self.Shgb[:, h, :], P["qt"][:, cs], r=[RSb[h], Rq[h]], w=[self.Rps[bo]], start=False, stop=True)
            for h, P in enumerate(Ps):
                self.mm(pd[:, h * 128:(h + 1) * 128], ktm[0:64, h * 128:(h + 1) * 128], P["vt3"][0:64, c, :], r=[Rktm, Rv[h]], w=[self.Rps[bd]])
            self.tt(self.Shg[:], self.Shg[:], self.egs[:, :, c:c + 1].to_broadcast([128, 4, 128]), ALU.mult, r=RS + self.Regs, w=RS)
            self.tt(self.Shgb[:], self.Shg[:], pd[:, :].rearrange("p (h v) -> p h v", h=4), ALU.add, r=RS + [self.Rps[bd]], w=RSb)
            self.tt(self.Shg[:], self.Shg[:], pd[:, :].rearrange("p (h v) -> p h v", h=4), ALU.add, r=RS + [self.Rps[bd]], w=RS)
            self.ps_release(bd)
            self.cp("act", oc_all[:, :, 1536 + c * 64:1536 + (c + 1) * 64], po[:, 0:256].rearrange("p (h s) -> p h s", h=4), r=[self.Rps[bo]], w=Roc)
            self.ps_release(bo)
            yield

    def hgrn_epilogue(self, h, P):
        T = self.T
        N = T["N"]
        self.rms_gate_out(P["oc"], P["Roc"], N, P["sg"], P["Rsg"], P_HGG, h)
        if T["kind"] == "p" and T["last"]:
            self.dma(self.p_hg[h], self.Shg[:, h, :], r=[self.RShg[h]])

    def gdn_rows(self):
        T = self.T
        N = T["N"]; samp = T["kind"] == "s"; nch = self.nch
        slot, Rw = self.w_next()
        wv = self.v3(slot, 8)
        prm, sm = self.prm, self.sm
        bb = self.proj_fm(wv, Rw, 0, N, M=4)
        beta, Rbeta = self.tF.get()
        self.act(beta[0:4, :N], self.psb[bb][0:4, :N], AF.Sigmoid, r=[self.Rps[bb]], w=[Rbeta])
        self.ps_release(bb)
        ba = self.proj_fm(wv, Rw, 4, N, M=4)
        la, Rla = self.tF.get()
        self.act(la[0:4, :N], self.psb[ba][0:4, :N], AF.Exp, r=[self.Rps[ba], self.Rprm], w=[Rla], bias=prm[0:4, P_DTB:P_DTB + 1])
        self.ps_release(ba)
        self.act(la[0:4, :N], la[0:4, :N], AF.Ln, r=[Rla], w=[Rla], bias=1.0)
        self.ts(la[0:4, :N], la[0:4, :N], sm[0:4, 8:9], None, ALU.mult, None, r=[Rla, self.Rsm], w=[Rla])
        gc, Rgc = self.rowg, self.Rrowg
        self.cumsum_rows(gc, Rgc, la, Rla, 4, N)
        ngc, Rngc = self.tF.get(); ngam, Rngam = self.tF.get(); ekd, Rekd = self.tF.get()
        self.ts(ngc[0:4, :N], gc[0:4, :N], -1.0, None, ALU.mult, None, r=[Rgc], w=[Rngc])
        self.act(ngam[0:4, :N], gc[0:4, :N], AF.Exp, r=[Rgc], w=[Rngam])
        self.ts(ngam[0:4, :N], ngam[0:4, :N], -1.0, None, ALU.mult, None, r=[Rngam], w=[Rngam])
        if not samp:
            gc3 = gc[0:4, :N].rearrange("p (c j) -> p c j", j=64)
            self.tt(ekd[0:4, :N].rearrange("p (c j) -> p c j", j=64), gc3[:, :, 63:64].to_broadcast([4, nch, 64]), gc3, ALU.subtract, r=[Rgc], w=[Rekd])
        else:
            self.tt(ekd[0:4, :N].rearrange("p (t s) -> p t s", s=SB), gc[0:4, 48:64].unsqueeze(1).to_broadcast([4, ST, SB]),
                    gc[0:4, :N].rearrange("p (t s) -> p t s", s=SB), ALU.subtract, r=[Rgc], w=[Rekd])
        self.act(ekd[0:4, :N], ekd[0:4, :N], AF.Exp, r=[Rekd], w=[Rekd])
        rows = [(0, beta, Rbeta), (1, gc, Rgc), (4, ekd, Rekd)]
        b = self.ps_alloc(); ps = self.psb[b]
        for c in range(nch):
            for qi, row, Rrow in rows:
                self.tr(ps[0:64, c * 20 + qi * 4:c * 20 + qi * 4 + 4], row[0:4, c * 64:(c + 1) * 64], self.identf[0:4, 0:4], r=[Rrow, self.Rcst], w=[self.Rps[b]])
        ps3 = ps[0:64, 0:nch * 20].rearrange("p (c q) -> p c q", q=20)
        self.cp("dve", self.tms[0:64, 0:nch, 0:8], ps3[:, :, 0:8], r=[self.Rps[b]], w=[self.Rtms])
        self.cp("dve", self.tms[0:64, 0:nch, 16:20], ps3[:, :, 16:20], r=[self.Rps[b]], w=[self.Rtms])
        self.ps_release(b)
        self.ts(self.tms[0:64, 0:nch, 8:12], self.tms[0:64, 0:nch, 4:8], -1.0, None, ALU.mult, None, r=[self.Rtms], w=[self.Rtms])
        self.act(self.tms[0:64, 0:nch, 12:16], self.tms[0:64, 0:nch, 4:8], AF.Exp, r=[self.Rtms], w=[self.Rtms])
        self.ts(self.tms[0:64, 0:nch, 12:16], self.tms[0:64, 0:nch, 12:16], -1.0, None, ALU.mult, None, r=[self.Rtms], w=[self.Rtms])

    def gdn_prelude(self, h, V):
        T = self.T
        N = T["N"]; samp = T["kind"] == "s"; u = self.unit
        H = 3 * u
        slot, Rw = self.w_next()
        wv = self.v3(slot, 512)
        prm = self.prm
        P = dict(h=h)
        if V is None:
            for k, pool in (("qn", self.tB), ("kn", self.tB), ("vv", self.tB), ("sg", self.tB), ("oc", self.tF)):
                t_, r_ = pool.get()
                P[k], P["R" + k] = t_[:], r_
        else:
            P.update(V)
        for qi, key in enumerate(("qn", "kn", "vv")):
            o, Ro = P[key], P["R" + key]
            j = qi * 4 + h
            b = self.proj_fm(wv, Rw, qi * 128, N)
            cw, Rcw = self.tF.get()
            self.cp("act", cw[:, H:H + N], self.psb[b][:, :N], r=[self.Rps[b]], w=[Rcw])
            self.ps_release(b)
            self.cp("dve", cw[:, 0:H], self.gdh[:, j, 0:H], r=[self.Rgdh[j]], w=[Rcw])
            a, Ra = self.tF.get()
            wc = lambda tap: prm[:, P_GDC + j * 4 + tap:P_GDC + j * 4 + tap + 1]
            self.ts(a[:, :N], cw[:, 3 * u:3 * u + N], wc(3), None, ALU.mult, None, r=[Rcw, self.Rprm], w=[Ra])
            for tap in (2, 1, 0):
                self.stt(a[:, :N], cw[:, tap * u:tap * u + N], wc(tap), a[:, :N], ALU.mult, ALU.add, r=[Rcw, Ra, self.Rprm], w=[Ra])
            self.cp("dve", self.gdh[:, j, 0:H], cw[:, N:N + H], r=[Rcw], w=[self.Rgdh[j]])
            if T["last"]:
                self.conv_state_out(self.gdh[:, j, 0:H], self.Rgdh[j], H, j, self.s_gc if samp else self.p_gc, 3)
            if qi < 2:
                xf, Rxf = a, Ra
                self.act(xf[:, :N], a[:, :N], AF.Silu, r=[Ra], w=[Rxf])
                sq, Rsq = self.tB.get()
                self.act(sq[:, :N], xf[:, :N], AF.Square, r=[Rxf], w=[Rsq])
                b2 = self.ps_alloc(); pm = self.psb[b2]
                self.mm(pm[:, :N], self.ones1, sq[:, :N], r=[self.Rcb, Rsq], w=[self.Rps[b2]])
                rn, Rrn = self.tF.get()
                self.act(rn[:, :N], pm[:, :N], AF.Ln, r=[self.Rps[b2]], w=[Rrn], bias=self.eps_rms)
                self.ps_release(b2)
                self.act(rn[:, :N], rn[:, :N], AF.Exp, r=[Rrn], w=[Rrn], scale=-0.5)
                if qi == 0:
                    self.stt(o[:, :N], xf[:, :N], 128.0 ** -0.5, rn[:, :N], ALU.mult, ALU.mult, r=[Rxf, Rrn], w=[Ro])
                else:
                    self.tt(o[:, :N], xf[:, :N], rn[:, :N], ALU.mult, r=[Rxf, Rrn], w=[Ro])
            else:
                self.act(o[:, :N], a[:, :N], AF.Silu, r=[Ra], w=[Ro])
        bz = self.proj_fm(wv, Rw, 384, N)
        self.act(P["sg"][:, :N], self.psb[bz][:, :N], AF.Silu, r=[self.Rps[bz]], w=[P["Rsg"]])
        self.ps_release(bz)
        if samp:
            self.load_sample_state(self.st_gd, h)
        return P

    def gdn_loop(self, h, P):
        T = self.T
        N = T["N"]; samp = T["kind"] == "s"; nch = self.nch; mo = self.mo
        cst = self.cst
        qn, kn, vv, oc = P["qn"], P["kn"], P["vv"], P["oc"]
        Rqn, Rkn, Rvv, Roc = P["Rqn"], P["Rkn"], P["Rvv"], P["Roc"]
        Mb1 = cst[0:64, C_MB1 + mo:C_MB1 + mo + 64]
        Mb2 = cst[0:64, C_MB2 + mo:C_MB2 + mo + 64]
        sel_h = cst[0:4, C_SEL + h * 128:C_SEL + (h + 1) * 128]
        nsq = 1 if samp else 5
        idb64 = self.identb[0:64, 0:64]
        for c in range(nch):
            cs = slice(c * 64, (c + 1) * 64)
            tm = lambda qi: self.tms[0:64, c, qi * 4 + h:qi * 4 + h + 1]
            bG = self.ps_alloc(); pG = self.psb[bG]
            grow = self.rowg[0:4, cs]
            self.mm(pG[:, 0:64], sel_h, grow, r=[self.Rcst, self.Rrowg], w=[self.Rps[bG]])
            self.mm(pG[0:64, 64:128], sel_h[:, 0:64], grow, r=[self.Rcst, self.Rrowg], w=[self.Rps[bG]], start=True, stop=False)
            self.mm(pG[0:64, 64:128], self.identf[0:64, 0:64], Mb1, r=[self.Rcst], w=[self.Rps[bG]], start=False, stop=True)
            self.mm(pG[0:64, 128:192], sel_h[:, 0:64], grow, r=[self.Rcst, self.Rrowg], w=[self.Rps[bG]], start=True, stop=False)
            self.mm(pG[0:64, 128:192], self.identf[0:64, 0:64], Mb2, r=[self.Rcst], w=[self.Rps[bG]], start=False, stop=True)
            b = self.ps_alloc(); ps = self.psb[b]
            self.mm(ps[0:64, 0:64], kn[:, cs], kn[:, cs], r=[Rkn], w=[self.Rps[b]])
            self.mm(ps[0:64, 64:128], kn[:, cs], qn[:, cs], r=[Rkn, Rqn], w=[self.Rps[b]])
            yield
            gbc, Rgbc = self.sF.get(); Ds, RDs = self.sF.get(); DTi, RDTi = self.sF.get()
            self.act(Ds[0:64, 0:64], pG[0:64, 64:128], AF.Exp, r=[self.Rps[bG], self.Rtms], w=[RDs], scale=-1.0, bias=tm(1))
            self.act(gbc[:, 0:64], pG[:, 0:64], AF.Exp, r=[self.Rps[bG]], w=[Rgbc])
            self.act(DTi[0:64, 0:64], pG[0:64, 128:192], AF.Exp, r=[self.Rps[bG], self.Rtms], w=[RDTi], scale=1.0, bias=tm(2))
            self.ps_release(bG)
            yield
            A, RA = self.sBp.get(); PT, RPT = self.sBl.get()
            self.stt(A[0:64, 0:64], ps[0:64, 0:64], tm(0), Ds[0:64, 0:64], ALU.mult, ALU.mult, r=[self.Rps[b], self.Rtms, RDs], w=[RA])
            self.tt(PT[0:64, 0:64], ps[0:64, 64:128], DTi[0:64, 0:64], ALU.mult, r=[self.Rps[b], RDTi], w=[RPT])
            self.ps_release(b)
            qg, Rqg = self.sBl.get()
            self.tt(qg[:, 0:64], qn[:, cs], gbc[:, 0:64], ALU.mult, r=[Rqn, Rgbc], w=[Rqg])
            yield
            b = self.ps_alloc(); pb = self.psb[b][:].bitcast(BF16)
            self.tr(pb[0:64, 0:64], A[0:64, 0:64], idb64, r=[RA, self.Rcb], w=[self.Rps[b]])
            b2 = self.ps_alloc(); pb2 = self.psb[b2][:].bitcast(BF16)
            self.tr(pb2[0:64, 0:128], kn[:, cs], self.identb, r=[Rkn, self.Rcb], w=[self.Rps[b2]])
            self.tr(pb2[0:64, 128:256], vv[:, cs], self.identb, r=[Rvv, self.Rcb], w=[self.Rps[b2]])
            yield
            AT, RAT = self.sBp.get()
            self.cp("act", AT[0:64, 0:64], pb[0:64, 0:64], r=[self.Rps[b]], w=[RAT])
            self.ps_release(b)
            kd, Rkd = self.sBl.get(); vtm, Rvtm = self.sBl.get()
            self.act(kd[0:64, :], pb2[0:64, 0:128], AF.Identity, r=[self.Rps[b2], self.Rtms], w=[Rkd], scale=tm(4))
            self.cp("act", vtm[0:64, :], pb2[0:64, 128:256], r=[self.Rps[b2]], w=[Rvtm])
            self.ps_release(b2)
            yield
            Tt, RTt = self.sBp.get()
            self.tt(Tt[0:64, 0:64], idb64, AT[0:64, 0:64], ALU.subtract, r=[self.Rcb, RAT], w=[RTt])
            X, RX, XT, RXT = A, RA, AT, RAT
            for i in range(nsq):
                last = (i == nsq - 1)
                b = self.ps_alloc(); ps = self.psb[b]
                self.mm(ps[0:64, 0:64], XT[0:64, 0:64], X[0:64, 0:64], r=[RX, RXT], w=[self.Rps[b]])
                if not last:
                    self.mm(ps[0:64, 64:128], X[0:64, 0:64], XT[0:64, 0:64], r=[RX, RXT], w=[self.Rps[b]])
                yield
                Xn, RXn = self.sBp.get()
                self.cp("act", Xn[0:64, 0:64], ps[0:64, 0:64], r=[self.Rps[b]], w=[RXn])
                if not last:
                    XTn, RXTn = self.sBp.get()
                    self.cp("act", XTn[0:64, 0:64], ps[0:64, 64:128], r=[self.Rps[b]], w=[RXTn])
                else:
                    XTn, RXTn = None, None
                self.ps_release(b)
                yield
                b = self.ps_alloc(); ps = self.psb[b]
                self.mm(ps[0:64, 0:64], Xn[0:64, 0:64], Tt[0:64, 0:64], r=[RXn, RTt], w=[self.Rps[b]])
                yield
                Ttn, RTtn = self.sBp.get()
                self.tt(Ttn[0:64, 0:64], ps[0:64, 0:64], Tt[0:64, 0:64], ALU.add, r=[self.Rps[b], RTt], w=[RTtn])
                self.ps_release(b)
                X, RX, XT, RXT, Tt, RTt = Xn, RXn, XTn, RXTn, Ttn, RTtn
            b = self.ps_alloc()
            if not samp:
                ps = self.psb[b]
                self.mm(ps[0:64, 0:128], kn[:, cs], self.Sgdb[:, h, :], r=[Rkn, self.RSgdb[h]], w=[self.Rps[b]])
                ks_src = ps[0:64, 0:128]
            else:
                b1 = self.ps_alloc(); ps1 = self.psb[b1]
                for sq_ in range(SB):
                    self.mm(ps1[:, sq_:64:SB], self.sstb[:, sq_, :], kn[:, sq_:64:SB], r=[self.Rsstb, Rkn], w=[self.Rps[b1]])
                kst, Rkst = self.sBp.get()
                self.cp("act", kst[:, 0:64], ps1[:, 0:64], r=[self.Rps[b1]], w=[Rkst])
                self.ps_release(b1)
                pb = self.psb[b][:].bitcast(BF16)
                self.tr(pb[0:64, 0:128], kst[:, 0:64], self.identb, r=[Rkst, self.Rcb], w=[self.Rps[b]])
                ks_src = pb[0:64, 0:128]
            yield
            r1, Rr1 = self.sF.get()
            self.stt(r1[0:64, :], ks_src, tm(3), vtm[0:64, :], ALU.mult, ALU.add, r=[self.Rps[b], self.Rtms, Rvtm], w=[Rr1])
            self.ps_release(b)
            rb, Rrb = self.sBp.get()
            self.ts(rb[0:64, :], r1[0:64, :], tm(0), None, ALU.mult, None, r=[Rr1, self.Rtms], w=[Rrb])
            yield
            b = self.ps_alloc(); ps = self.psb[b]
            self.mm(ps[0:64, 0:128], Tt[0:64, 0:64], rb[0:64, :], r=[RTt, Rrb], w=[self.Rps[b]])
            yield
            Ub, RUb = self.sBp.get()
            self.cp("act", Ub[0:64, :], ps[0:64, 0:128], r=[self.Rps[b]], w=[RUb])
            self.ps_release(b)
            yield
            bo = self.ps_alloc(); pO = self.psb[bo]
            self.mm(pO[:, 0:64], Ub[0:64, :], PT[0:64, 0:64], r=[RUb, RPT], w=[self.Rps[bo]], start=True, stop=False)
            if not samp:
                self.mm(pO[:, 0:64], self.Sgdb[:, h, :], qg[:, 0:64], r=[self.RSgdb[h], Rqg], w=[self.Rps[bo]], start=False, stop=True)
                b = self.ps_alloc(); ps = self.psb[b]
                self.mm(ps[:, 0:128], kd[0:64, :], Ub[0:64, :], r=[Rkd, RUb], w=[self.Rps[b]])
                yield
                self.stt(self.Sgd[:, h, :], self.Sgd[:, h, :], gbc[:, 63:64], ps[:, 0:128], ALU.mult, ALU.add, r=[self.RSgd[h], Rgbc, self.Rps[b]], w=[self.RSgd[h]])
                self.ps_release(b)
                self.cp("act", oc[:, cs], pO[:, 0:64], r=[self.Rps[bo]], w=[Roc])
                self.ps_release(bo)
                yield
                self.cp("act", self.Sgdb[:, h, :], self.Sgd[:, h, :], r=[self.RSgd[h]], w=[self.RSgdb[h]])
                yield
            else:
                for sq_ in range(SB):
                    self.mm(pO[:, sq_:64:SB], self.sstb[:, sq_, :], qg[:, sq_:64:SB], r=[self.Rsstb, Rqg], w=[self.Rps[bo]], start=False, stop=(sq_ == SB - 1))
                self.cp("act", oc[:, 0:64], pO[:, 0:64], r=[self.Rps[bo]], w=[Roc])
                self.ps_release(bo)
                self.sample_state_update(kd[0:64, :], Rkd, Ub[0:64, :], RUb, gbc[:, 48:64], Rgbc, self.s_gd, h)
                yield

    def gdn_loop_pair(self, h, P):
        T = self.T
        cst = self.cst
        qn, kn, vv, oc = P["qn"], P["kn"], P["vv"], P["oc"]
        Rqn, Rkn, Rvv, Roc = P["Rqn"], P["Rkn"], P["Rvv"], P["Roc"]
        sel_h = cst[0:4, C_SEL + h * 128:C_SEL + (h + 1) * 128]
        Mb1x2, Mb2x2, idb2 = self.Mb1x2, self.Mb2x2, self.idb2
        nsq = 5
        for p_ in range(self.nch // 2):
            c0 = 2 * p_
            cs2 = slice(c0 * 64, c0 * 64 + 128)
            csj = [slice((c0 + j) * 64, (c0 + j + 1) * 64) for j in range(2)]
            hj = [slice(j * 64, (j + 1) * 64) for j in range(2)]
            tm = lambda j, qi: self.tms[0:64, c0 + j, qi * 4 + h:qi * 4 + h + 1]
            bG = self.ps_alloc(); pG = self.psb[bG]
            grow2 = self.rowg[0:4, cs2]
            self.mm(pG[:, 0:128], sel_h, grow2, r=[self.Rcst, self.Rrowg], w=[self.Rps[bG]])
            self.mm(pG[0:64, 128:256], sel_h[:, 0:64], grow2, r=[self.Rcst, self.Rrowg], w=[self.Rps[bG]], start=True, stop=False)
            self.mm(pG[0:64, 128:256], self.identf[0:64, 0:64], Mb1x2, r=[self.Rcst, self.Rmx], w=[self.Rps[bG]], start=False, stop=True)
            self.mm(pG[0:64, 256:384], sel_h[:, 0:64], grow2, r=[self.Rcst, self.Rrowg], w=[self.Rps[bG]], start=True, stop=False)
            self.mm(pG[0:64, 256:384], self.identf[0:64, 0:64], Mb2x2, r=[self.Rcst, self.Rmx], w=[self.Rps[bG]], start=False, stop=True)
            b = self.ps_alloc(); ps = self.psb[b]
            for j in range(2):
                self.mm(ps[0:64, hj[j]], kn[:, csj[j]], kn[:, csj[j]], r=[Rkn], w=[self.Rps[b]])
            for j in range(2):
                self.mm(ps[0:64, 128 + j * 64:128 + (j + 1) * 64], kn[:, csj[j]], qn[:, csj[j]], r=[Rkn, Rqn], w=[self.Rps[b]])
            yield
            gbc, Rgbc = self.sF.get(); Ds, RDs = self.sF.get(); DTi, RDTi = self.sF.get()
            for j in range(2):
                self.act(Ds[0:64, hj[j]], pG[0:64, 128 + j * 64:128 + (j + 1) * 64], AF.Exp, r=[self.Rps[bG], self.Rtms], w=[RDs], scale=-1.0, bias=tm(j, 1))
            self.act(gbc[:, 0:128], pG[:, 0:128], AF.Exp, r=[self.Rps[bG]], w=[Rgbc])
            for j in range(2):
                self.act(DTi[0:64, hj[j]], pG[0:64, 256 + j * 64:256 + (j + 1) * 64], AF.Exp, r=[self.Rps[bG], self.Rtms], w=[RDTi], scale=1.0, bias=tm(j, 2))
            self.ps_release(bG)
            yield
            A, RA = self.sBp.get(); PT, RPT = self.sBl.get()
            for j in range(2):
                self.stt(A[0:64, hj[j]], ps[0:64, hj[j]], tm(j, 0), Ds[0:64, hj[j]], ALU.mult, ALU.mult, r=[self.Rps[b], self.Rtms, RDs], w=[RA])
            self.tt(PT[0:64, 0:128], ps[0:64, 128:256], DTi[0:64, 0:128], ALU.mult, r=[self.Rps[b], RDTi], w=[RPT])
            self.ps_release(b)
            qg, Rqg = self.sBl.get()
            self.tt(qg[:, 0:128], qn[:, cs2], gbc[:, 0:128], ALU.mult, r=[Rqn, Rgbc], w=[Rqg])
            egl2 = self.egl_all[:, h, c0:c0 + 2]
            self.cp("dve", egl2, gbc[:, 63:128:64], r=[Rgbc], w=[self.Regl[h][c0], self.Regl[h][c0 + 1]])
            yield
            b = self.ps_alloc(); pb = self.psb[b][:].bitcast(BF16)
            for j in range(2):
                self.tr(pb[0:64, hj[j]], A[0:64, hj[j]], self.identb[0:64, 0:64], r=[RA, self.Rcb], w=[self.Rps[b]])
            b2 = self.ps_alloc(); pb2 = self.psb[b2][:].bitcast(BF16)
            for j in range(2):
                self.tr(pb2[0:64, j * 128:(j + 1) * 128], kn[:, csj[j]], self.identb, r=[Rkn, self.Rcb], w=[self.Rps[b2]])
            for j in range(2):
                self.tr(pb2[0:64, 256 + j * 128:256 + (j + 1) * 128], vv[:, csj[j]], self.identb, r=[Rvv, self.Rcb], w=[self.Rps[b2]])
            yield
            AT, RAT = self.sBp.get()
            self.cp("act", AT[0:64, 0:128], pb[0:64, 0:128], r=[self.Rps[b]], w=[RAT])
            self.ps_release(b)
            kds, vtms = [], []
            for j in range(2):
                kd, Rkd = self.sBl.get()
                self.act(kd[0:64, :], pb2[0:64, j * 128:(j + 1) * 128], AF.Identity, r=[self.Rps[b2], self.Rtms], w=[Rkd], scale=tm(j, 4))
                kds.append((kd, Rkd))
            for j in range(2):
                vtm, Rvtm = self.sBl.get()
                self.cp("act", vtm[0:64, :], pb2[0:64, 256 + j * 128:256 + (j + 1) * 128], r=[self.Rps[b2]], w=[Rvtm])
                vtms.append((vtm, Rvtm))
            self.ps_release(b2)
            yield
            Tt, RTt = self.sBp.get()
            self.tt(Tt[0:64, 0:128], idb2, AT[0:64, 0:128], ALU.subtract, r=[self.Rmx, RAT], w=[RTt])
            X, RX, XT, RXT = A, RA, AT, RAT
            for i in range(nsq):
                last = (i == nsq - 1)
                b = self.ps_alloc(); ps = self.psb[b]
                for j in range(2):
                    self.mm(ps[0:64, hj[j]], XT[0:64, hj[j]], X[0:64, hj[j]], r=[RX, RXT], w=[self.Rps[b]])
                if not last:
                    for j in range(2):
                        self.mm(ps[0:64, 128 + j * 64:128 + (j + 1) * 64], X[0:64, hj[j]], XT[0:64, hj[j]], r=[RX, RXT], w=[self.Rps[b]])
                yield
                Xn, RXn = self.sBp.get()
                self.cp("act", Xn[0:64, 0:128], ps[0:64, 0:128], r=[self.Rps[b]], w=[RXn])
                if not last:
                    XTn, RXTn = self.sBp.get()
                    self.cp("act", XTn[0:64, 0:128], ps[0:64, 128:256], r=[self.Rps[b]], w=[RXTn])
                else:
                    XTn, RXTn = None, None
                self.ps_release(b)
                yield
                b = self.ps_alloc(); ps = self.psb[b]
                for j in range(2):
                    self.mm(ps[0:64, hj[j]], Xn[0:64, hj[j]], Tt[0:64, hj[j]], r=[RXn, RTt], w=[self.Rps[b]])
                yield
                Ttn, RTtn = self.sBp.get()
                self.tt(Ttn[0:64, 0:128], ps[0:64, 0:128], Tt[0:64, 0:128], ALU.add, r=[self.Rps[b], RTt], w=[RTtn])
                self.ps_release(b)
                X, RX, XT, RXT, Tt, RTt = Xn, RXn, XTn, RXTn, Ttn, RTtn
            for j in range(2):
                c = c0 + j
                cs = csj[j]
                kd, Rkd = kds[j]; vtm, Rvtm = vtms[j]
                b = self.ps_alloc(); ps = self.psb[b]
                self.mm(ps[0:64, 0:128], kn[:, cs], self.Sgdb[:, h, :], r=[Rkn, self.RSgdb[h]], w=[self.Rps[b]])
                yield
                r1, Rr1 = self.sF.get()
                self.stt(r1[0:64, :], ps[0:64, 0:128], tm(j, 3), vtm[0:64, :], ALU.mult, ALU.add, r=[self.Rps[b], self.Rtms, Rvtm], w=[Rr1])
                self.ps_release(b)
                rb, Rrb = self.sBp.get()
                self.ts(rb[0:64, :], r1[0:64, :], tm(j, 0), None, ALU.mult, None, r=[Rr1, self.Rtms], w=[Rrb])
                yield
                b = self.ps_alloc(); ps = self.psb[b]
                self.mm(ps[0:64, 0:128], Tt[0:64, hj[j]], rb[0:64, :], r=[RTt, Rrb], w=[self.Rps[b]])
                yield
                Ub, RUb = self.sBp.get()
                self.cp("act", Ub[0:64, :], ps[0:64, 0:128], r=[self.Rps[b]], w=[RUb])
                self.ps_release(b)
                yield
                bo = self.ps_alloc(); pO = self.psb[bo]
                self.mm(pO[:, 0:64], Ub[0:64, :], PT[0:64, hj[j]], r=[RUb, RPT], w=[self.Rps[bo]], start=True, stop=False)
                self.mm(pO[:, 0:64], self.Sgdb[:, h, :], qg[:, hj[j]], r=[self.RSgdb[h], Rqg], w=[self.Rps[bo]], start=False, stop=True)
                b = self.ps_alloc(); ps = self.psb[b]
                self.mm(ps[:, 0:128], kd[0:64, :], Ub[0:64, :], r=[Rkd, RUb], w=[self.Rps[b]])
                yield
                self.stt(self.Sgdb[:, h, :], self.Sgd[:, h, :], self.egl_all[:, h, c:c + 1], ps[:, 0:128], ALU.mult, ALU.add, r=[self.RSgd[h], self.Regl[h][c], self.Rps[b]], w=[self.RSgdb[h]])
                self.stt(self.Sgd[:, h, :], self.Sgd[:, h, :], self.egl_all[:, h, c:c + 1], ps[:, 0:128], ALU.mult, ALU.add, r=[self.RSgd[h], self.Regl[h][c], self.Rps[b]], w=[self.RSgd[h]])
                self.ps_release(b)
                self.cp("act", oc[:, cs], pO[:, 0:64], r=[self.Rps[bo]], w=[Roc])
                self.ps_release(bo)
                yield

    def gdn_epilogue(self, h, P):
        T = self.T
        self.rms_gate_out(P["oc"], P["Roc"], T["N"], P["sg"], P["Rsg"], P_GDG, 4 + h)
        if T["kind"] == "p" and T["last"]:
            self.dma(self.p_gd[h], self.Sgd[:, h, :], r=[self.RSgd[h]])

    def conv_state_out(self, src, Rsrc, H, j, dst, nrow):
        b = self.ps_alloc(); ps = self.psb[b]
        self.tr(ps[0:H, 0:128], src, self.identf, r=[Rsrc, self.Rcst], w=[self.Rps[b]])
        o, Ro = self.sF.get()
        self.cp("act", o[0:H, :], ps[0:H, 0:128], r=[self.Rps[b]], w=[Ro])
        self.ps_release(b)
        if self.T["kind"] == "p":
            self.dma(dst[:, j * 128:(j + 1) * 128], o[0:H, :], r=[Ro])
        else:
            for t in range(nrow):
                self.dma(dst[:, t, j * 128:(j + 1) * 128], o[t * SB:(t + 1) * SB, :], r=[Ro])

    def stage_attn_ln2(self):
        T = self.T
        N = T["N"]; samp = T["kind"] == "s"
        v3 = self.v3
        for g in range(2):
            slot, Rw = self.w_next()
            wv = v3(slot, 512)
            for mi in range(4):
                m = g * 4 + mi
                b = self.proj_fm(wv, Rw, mi * 128, N)
                self.cp("act", self.mq[:, m, :N], self.psb[b][:, :N], r=[self.Rps[b]], w=[self.Rmq[m]])
                self.ps_release(b)
        oa = self.aview(0, 2048, BF16, "p (k n) -> p k n", n=512)
        Roa = [R() for _ in range(8)]
        new_rs = list(Roa)
        if samp:
            kcs = [self.aview(2048, 1024, BF16, "p (j d) -> p j d", d=D), self.aview(5120, 1024, BF16, "p (j d) -> p j d", d=D)]
            vcs = [self.aview(3072, 1024, BF16, "p (j d) -> p j d", d=D), self.aview(6144, 1024, BF16, "p (j d) -> p j d", d=D)]
            Rkcs, Rvcs = [R(), R()], [R(), R()]
            kF = self.aview(4096, 1024, BF16, "p (k m) -> p k m", m=256); RkF = R()
            new_rs += Rkcs + Rvcs + [RkF]
        self.arena_switch(new_rs)
        sc = 256.0 ** -0.5
        mq, Rmq = self.mq, self.Rmq
        if not samp:
            for h in range(4):
                pTs = []
                for j in range(2):
                    b = self.ps_alloc(); ps = self.psb[b]
                    for dc in range(2):
                        self.mm(ps[:, :N], self.mkF[:, 2 * h + dc, j * 128:(j + 1) * 128], mq[:, 2 * h + dc, :N], r=[self.RmkF, Rmq[2 * h + dc]], w=[self.Rps[b]], start=(dc == 0), stop=(dc == 1))
                    pT, RpT = self.tB.get()
                    self.act(pT[:, :N], ps[:, :N], AF.Exp, r=[self.Rps[b]], w=[RpT], scale=sc)
                    self.ps_release(b)
                    pTs.append((pT, RpT))
                b = self.ps_alloc(); ps = self.psb[b]
                for j in range(2):
                    self.mm(ps[:, :N], self.ones1, pTs[j][0][:, :N], r=[self.Rcb, pTs[j][1]], w=[self.Rps[b]], start=(j == 0), stop=(j == 1))
                rden, Rrden = self.tF.get()
                self.act(rden[:, :N], ps[:, :N], AF.Ln, r=[self.Rps[b]], w=[Rrden])
                self.act(rden[:, :N], rden[:, :N], AF.Exp, r=[Rrden], w=[Rrden], scale=-1.0)
                self.ps_release(b)
                for dc in range(2):
                    dch = 2 * h + dc
                    b = self.ps_alloc(); ps = self.psb[b]
                    for j in range(2):
                        self.mm(ps[:, :N], self.mvT[:, j, dch * 128:(dch + 1) * 128], pTs[j][0][:, :N], r=[self.RmvT, pTs[j][1]], w=[self.Rps[b]], start=(j == 0), stop=(j == 1))
                    self.tt(oa[:, dch, :N], ps[:, :N], rden[:, :N], ALU.mult, r=[self.Rps[b], Rrden], w=[Roa[dch]])
                    self.ps_release(b)
        else:
            def kv_load(i_):
                self.dma(kcs[i_ % 2], self.ck[i_].rearrange("(j p) d -> p j d", p=128), w=[Rkcs[i_ % 2]], q="pool")
                self.dma(vcs[i_ % 2], self.cv[i_].rearrange("(j p) d -> p j d", p=128), w=[Rvcs[i_ % 2]], q="pool")
            kv_load(0)
            for sq_ in range(SB):
                kc, vc, Rkc, Rvc = kcs[sq_ % 2], vcs[sq_ % 2], Rkcs[sq_ % 2], Rvcs[sq_ % 2]
                if sq_ + 1 < SB:
                    kv_load(sq_ + 1)
                for j in range(2):
                    b = self.ps_alloc(); pb = self.psb[b][:].bitcast(BF16)
                    for dch in range(8):
                        self.tr(pb[:, dch * 128:(dch + 1) * 128], kc[:, j, dch * 128:(dch + 1) * 128], self.identb, r=[Rkc, self.Rcb], w=[self.Rps[b]])
                    self.cp("act" if j else "dve", kF[:, :, j * 128:(j + 1) * 128], pb[:, 0:1024].rearrange("p (k m) -> p k m", m=128), r=[self.Rps[b]], w=[RkF])
                    self.ps_release(b)
                cols = slice(sq_, 64, SB)
                bS = self.ps_alloc(); pS = self.psb[bS]
                for h in range(4):
                    for j in range(2):
                        c0 = (h * 2 + j) * 4
                        for dc in range(2):
                            self.mm(pS[:, c0:c0 + 4], kF[:, 2 * h + dc, j * 128:(j + 1) * 128], mq[:, 2 * h + dc, cols], r=[RkF, Rmq[2 * h + dc]], w=[self.Rps[bS]], start=(dc == 0), stop=(dc == 1))
                pT, RpT = self.sBp.get()
                self.act(pT[:, 0:32], pS[:, 0:32], AF.Exp, r=[self.Rps[bS]], w=[RpT], scale=sc)
                self.ps_release(bS)
                bD = self.ps_alloc(); pD = self.psb[bD]
                for h in range(4):
                    for j in range(2):
                        c0 = (h * 2 + j) * 4
                        self.mm(pD[:, h * 4:(h + 1) * 4], self.ones1, pT[:, c0:c0 + 4], r=[self.Rcb, RpT], w=[self.Rps[bD]], start=(j == 0), stop=(j == 1))
                rden, Rrden = self.sF.get()
                self.act(rden[:, 0:16], pD[:, 0:16], AF.Ln, r=[self.Rps[bD]], w=[Rrden])
                self.act(rden[:, 0:16], rden[:, 0:16], AF.Exp, r=[Rrden], w=[Rrden], scale=-1.0)
                self.ps_release(bD)
                bO = self.ps_alloc(); pO = self.psb[bO]
                for h in range(4):
                    for dc in range(2):
                        dch = 2 * h + dc
                        for j in range(2):
                            c0 = (h * 2 + j) * 4
                            self.mm(pO[:, dch * 4:(dch + 1) * 4], vc[:, j, dch * 128:(dch + 1) * 128], pT[:, c0:c0 + 4], r=[Rvc, RpT], w=[self.Rps[bO]], start=(j == 0), stop=(j == 1))
                for h in range(4):
                    self.tt(oa[:, 2 * h:2 * h + 2, cols], pO[:, 8 * h:8 * h + 8].rearrange("p (a t) -> p a t", t=4),
                            rden[:, 4 * h:4 * h + 4].unsqueeze(1).to_broadcast([128, 2, 4]), ALU.mult,
                            r=[self.Rps[bO], Rrden], w=[Roa[2 * h], Roa[2 * h + 1]])
                self.ps_release(bO)
        self.proj_residual(oa, Roa, 8, N, 2, lambda s: v3(s, 512), 512)
        self.layer_norm(N, P_LN + 16, P_LN + 24, shadow=True)

    def stage_ffn_ln3(self):
        T = self.T
        N = T["N"]; samp = T["kind"] == "s"; u = self.unit
        H2 = 2 * u
        prm = self.prm
        actb = self.aview(0, 5632, BF16, "p (k n) -> p k n", n=512)
        Ract = [R() for _ in range(22)]
        self.arena_switch(Ract)
        for g in range(11):
            slot, Rw = self.w_next()
            wv = self.v3(slot, 512)
            for jj in range(2):
                j = 2 * g + jj
                bg = self.proj_fm(wv, Rw, jj * 128, N)
                bv = self.proj_fm(wv, Rw, 256 + jj * 128, N)
                gw, Rgw = self.tF.get()
                self.cp("act", gw[:, H2:H2 + N], self.psb[bg][:, :N], r=[self.Rps[bg]], w=[Rgw])
                self.ps_release(bg)
                self.cp("dve", gw[:, 0:H2], self.ffh[:, j, 0:H2], r=[self.Rffh[j]], w=[Rgw])
                a, Ra = self.tF.get()
                wc = lambda tap: prm[:, P_FC + j * 3 + tap:P_FC + j * 3 + tap + 1]
                self.ts(a[:, :N], gw[:, 2 * u:2 * u + N], wc(2), prm[:, P_FB + j:P_FB + j + 1], ALU.mult, ALU.add, r=[Rgw, self.Rprm], w=[Ra])
                for tap in (1, 0):
                    self.stt(a[:, :N], gw[:, tap * u:tap * u + N], wc(tap), a[:, :N], ALU.mult, ALU.add, r=[Rgw, Ra, self.Rprm], w=[Ra])
                self.cp("dve", self.ffh[:, j, 0:H2], gw[:, N:N + H2], r=[Rgw], w=[self.Rffh[j]])
                if T["last"]:
                    self.conv_state_out(self.ffh[:, j, 0:H2], self.Rffh[j], H2, j, self.s_fc if samp else self.p_fc, 2)
                ge, Rge = self.tF.get()
                self.act(ge[:, :N], a[:, :N], AF.Gelu, r=[Ra], w=[Rge])
                self.tt(actb[:, j, :N], ge[:, :N], self.psb[bv][:, :N], ALU.mult, r=[Rge, self.Rps[bv]], w=[Ract[j]])
                self.ps_release(bv)
        self.proj_residual(actb, Ract, 22, N, 8, lambda s: s[:, 0:22 * 128].rearrange("p (k n) -> p k n", n=128), 128)
        self.layer_norm(N, P_LN + 32, P_LN + 40, shadow=False)


def _pack_params(inp):
    prm = np.zeros((128, NPRM), np.float32)
    lbl = np.asarray(inp["hgrn_lb_logits"], np.float32)
    prm[:, P_L0:P_L0 + 4] = lbl[0].reshape(4, 128).T
    prm[:, P_L1:P_L1 + 4] = lbl[1].reshape(4, 128).T
    wc = np.asarray(inp["w_gd_conv"], np.float32)[0]
    prm[:, P_GDC:P_GDC + 48] = wc.reshape(4, 12, 128).transpose(2, 1, 0).reshape(128, 48)
    prm[:, P_HGG] = np.asarray(inp["hg_norm_g"], np.float32)[0]
    prm[:, P_GDG] = np.asarray(inp["gd_norm_g"], np.float32)[0]
    for i, k in enumerate(("ln1_g", "ln1_b", "ln2_g", "ln2_b", "ln3_g", "ln3_b")):
        prm[:, P_LN + 8 * i:P_LN + 8 * i + 8] = np.asarray(inp[k], np.float32)[0].reshape(8, 128).T
    fc = np.asarray(inp["w_ffn_conv"], np.float32)[0]
    prm[:, P_FC:P_FC + 66] = fc.reshape(3, 22, 128).transpose(2, 1, 0).reshape(128, 66)
    prm[:, P_FB:P_FB + 22] = np.asarray(inp["b_ffn_conv"], np.float32)[0].reshape(22, 128).T
    prm[0:4, P_ALOG] = np.asarray(inp["gd_a_log"], np.float32)[0]
    prm[0:4, P_DTB] = np.asarray(inp["gd_dt_bias"], np.float32)[0]
    return prm


def _consts():
    c = np.zeros((128, NCST), np.float32)
    c[:, C_ID:C_ID + 128] = np.eye(128, dtype=np.float32)
    i = np.arange(64)
    c[0:64, C_MT:C_MT + 64] = (i[:, None] <= i[None, :])
    c[0:64, C_MB1:C_MB1 + 64] = np.where(i[None, :] < i[:, None], 0.0, BIG)
    c[0:64, C_MB2:C_MB2 + 64] = np.where(i[None, :] >= i[:, None], 0.0, -BIG)
    tt_, ss_ = i // SB, i % SB
    same = ss_[:, None] == ss_[None, :]
    c[0:64, C_MT + 64:C_MT + 128] = same & (tt_[:, None] <= tt_[None, :])
    c[0:64, C_MB1 + 64:C_MB1 + 128] = np.where(same & (tt_[None, :] < tt_[:, None]), 0.0, BIG)
    c[0:64, C_MB2 + 64:C_MB2 + 128] = np.where(same & (tt_[None, :] >= tt_[:, None]), 0.0, -BIG)
    sm = np.ones(512, np.float32); sm[0::64] = 0.0
    c[:, C_SCAN:C_SCAN + 512] = sm[None, :]
    c[0:64, C_BM:C_BM + SB] = (ss_[:, None] == np.arange(SB)[None, :])
    for h in range(4):
        c[h, C_SEL + h * 128:C_SEL + (h + 1) * 128] = 1.0
    return c


_OUT_NAMES = ["yp", "ys", "p_hg", "p_gd", "p_gc", "p_fc", "p_mk", "p_mv", "s_hg", "s_gd", "s_gc", "s_fc"]


def make_in_maps(inp, cores):
    f = lambda k: np.ascontiguousarray(np.asarray(inp[k], np.float32))
    prm = _pack_params(inp)
    cst = _consts()
    shared = {"prm": prm, "cst": cst}
    def grp(W, col_lists):
        nk = W.shape[0] // 128
        Wr = W.reshape(nk, 128, W.shape[1])
        out = []
        for cols in col_lists:
            out.append(np.ascontiguousarray(Wr[:, :, cols].transpose(1, 0, 2).reshape(128, nk * len(cols))))
        return np.stack(out, axis=0)

    ar = np.arange
    w_in = f("w_in")[0]
    hg = [np.concatenate([ar(h * 128, (h + 1) * 128) + base for base in (0, 512, 1536, 1024)]) for h in range(4)]
    gd = [np.concatenate([ar(h * 128, (h + 1) * 128) + base for base in (2048, 2560, 3072, 3584)]) for h in range(4)]
    shared["w_in"] = grp(w_in, hg + gd)
    shared["w_tail"] = grp(w_in, [ar(4096, 4104)])[0]
    c512 = lambda n: [ar(g * 512, (g + 1) * 512) for g in range(n)]
    shared["w_out"] = grp(f("w_out")[0], c512(2))
    shared["w_mq"] = grp(f("w_mq")[0], c512(2))
    shared["w_mkv"] = grp(f("w_mkv")[0], c512(4))
    shared["w_mo"] = grp(f("w_mo")[0], c512(2))
    shared["w_up"] = grp(f("w_up")[0], [np.concatenate([ar(g * 256, (g + 1) * 256), DFF + ar(g * 256, (g + 1) * 256)]) for g in range(11)])
    shared["w_down"] = grp(f("w_down")[0], [ar(g * 128, (g + 1) * 128) for g in range(8)])
    xp, xs = f("x_prompt"), f("x_sample")
    st_hg, st_gd, st_gc, st_fc = f("state_hgrn")[0], f("state_gdn")[0], f("state_gdn_conv")[0], f("state_ffn_conv")[0]
    ck, cv, memp = f("cache_mem_k")[0], f("cache_mem_v")[0], f("mem_prompt")
    maps = []
    for c in cores:
        sl = slice(c * SB, (c + 1) * SB)
        m = dict(shared)
        m.update({"xp": xp[c], "xs": xs[sl], "st_hg": np.ascontiguousarray(st_hg[sl].transpose(1, 2, 0, 3)), "st_gd": np.ascontiguousarray(st_gd[sl].transpose(1, 2, 0, 3)), "st_gc": st_gc[sl], "st_fc": st_fc[sl],
                  "ck": ck[sl].reshape(SB, NMEM, D), "cv": cv[sl].reshape(SB, NMEM, D), "memp": memp[c]})
        maps.append(m)
    return maps


def assemble(results):
    cat = lambda k: np.concatenate([r[k] for r in results], axis=0)
    stk = lambda k: np.stack([r[k] for r in results], axis=0)
    yp = stk("yp")
    ys = cat("ys")
    return (yp, ys,
            stk("p_hg")[None], stk("p_gd")[None], stk("p_gc")[None], stk("p_fc")[None],
            stk("p_mk").reshape(1, NCORE, NMEM, 4, 256), stk("p_mv").reshape(1, NCORE, NMEM, 4, 256),
            np.concatenate([r["s_hg"].transpose(2, 0, 1, 3) for r in results], axis=0)[None],
            np.concatenate([r["s_gd"].transpose(2, 0, 1, 3) for r in results], axis=0)[None],
            cat("s_gc")[None], cat("s_fc")[None])


def kernel(**inputs):
    nc = MK().build()
    maps = make_in_maps(inputs, list(range(NCORE)))
    res = run_bass_kernel_spmd(nc, maps, core_ids=list(range(NCORE)))
    return assemble(res.results)
```

```python
import numpy as np
from contextlib import ExitStack
import concourse.bass as bass
import concourse.mybir as mybir
from concourse.bass_utils import run_bass_kernel_spmd

F32 = mybir.dt.float32
BF16 = mybir.dt.bfloat16
AF = mybir.ActivationFunctionType
ALU = mybir.AluOpType

COMPUTE = ("pe", "act", "dve", "pool")
NS_DMA = 12
DEBUG_LINES = False

D = 1024
SEQ = 2048
NCORE = 8
SB = 16
ST = 4
DFF = 2816
NMEM = 256
ALPHA = 2.0 ** 0.25
LN_EPS = 1e-5
RMS_EPS = 1e-6
BIG = 30000.0

P_L0, P_L1, P_GDC, P_HGG, P_GDG = 0, 4, 8, 56, 57
P_LN = 58
P_FC, P_FB, P_ALOG, P_DTB, NPRM = 106, 172, 194, 195, 196
C_ID, C_MT, C_MB1, C_MB2, C_SCAN, C_BM, C_SEL, NCST = 0, 128, 256, 384, 512, 1024, 1040, 1552


class R:
    __slots__ = ("name", "last_w", "reads", "excl")

    def __init__(self, name="", excl=False):
        self.name = name
        self.last_w = None
        self.reads = []
        self.excl = excl


def alias(new_rs, old_rs):
    hz = []
    for o in old_rs:
        if o.last_w is not None:
            hz.append(o.last_w)
        hz.extend(o.reads)
    hz = list(set(hz))
    for n in new_rs:
        n.last_w = None
        n.reads = list(hz)


class Node:
    __slots__ = ("gid", "stream", "cls", "fn", "deps", "dur", "lat", "start", "finish", "idx", "waits", "sig")

    def __init__(self, gid, stream, cls, fn, deps, dur, lat):
        self.gid, self.stream, self.cls, self.fn, self.deps, self.dur, self.lat = gid, stream, cls, fn, deps, dur, lat
        self.start = self.finish = 0.0
        self.idx = -1
        self.waits = []
        self.sig = False


class Em:
    STREAMS = ("pe", "act", "dve", "pool", "sp")
    HOP = 0.50
    PHOP = 0.25

    def __init__(self, nc, reorder=True):
        self.nc = nc
        self.nodes = []
        self.lines = {}
        self.reorder = reorder

    def _add(self, stream, cls, fn, r, w, dur, lat):
        gid = len(self.nodes)
        deps = set()
        nodes = self.nodes
        for res in r:
            if res.last_w is not None:
                deps.add(res.last_w)
            if res.excl:
                deps.update(x for x in res.reads if nodes[x].cls != cls)
        for res in w:
            if res.last_w is not None:
                deps.add(res.last_w)
            deps.update(res.reads)
        deps.discard(gid)
        nodes.append(Node(gid, stream, cls, fn, deps, dur, lat))
        if DEBUG_LINES:
            import sys
            f = sys._getframe(2)
            ln = []
            while f is not None and len(ln) < 3:
                ln.append(f.f_lineno)
                f = f.f_back
            self.lines[gid] = ln
        for res in r:
            res.reads.append(gid)
        for res in w:
            res.last_w = gid
            res.reads = []
        return gid

    def op(self, eng, fn, r=(), w=(), dur=0.15):
        return self._add(eng, eng, fn, r, w, dur, dur)

    def dma(self, stream, fn, r=(), w=(), nbytes=65536):
        issue = 1.1 if stream == "pool" else 0.08
        lat = 2.0 + nbytes / 200e3
        return self._add(stream, "dma_" + stream, fn, r, w, issue, lat)

    def schedule(self):
        nodes = self.nodes
        n = len(nodes)
        if not self.reorder:
            for i, nd in enumerate(nodes):
                nd.start = float(i)
            return
        succs = [[] for _ in range(n)]
        indeg = [0] * n
        for nd in nodes:
            indeg[nd.gid] = len(nd.deps)
            for d in nd.deps:
                succs[d].append(nd.gid)
        prio = [0.0] * n
        for i in range(n - 1, -1, -1):
            m = 0.0
            for sgid in succs[i]:
                if prio[sgid] > m:
                    m = prio[sgid]
            prio[i] = nodes[i].lat + self.PHOP + m
        ready = {s: [] for s in self.STREAMS}
        rtime = [0.0] * n
        free = {s: 0.0 for s in self.STREAMS}
        for nd in nodes:
            if indeg[nd.gid] == 0:
                ready[nd.stream].append(nd.gid)
        done = 0
        while done < n:
            best = None
            for sname in self.STREAMS:
                rl = ready[sname]
                if not rl:
                    continue
                t = free[sname]
                c_now = None
                c_late = None
                for g in rl:
                    rt = rtime[g]
                    if rt <= t:
                        if c_now is None or prio[g] > prio[c_now] or (prio[g] == prio[c_now] and g < c_now):
                            c_now = g
                    elif c_late is None or rt < rtime[c_late] or (rt == rtime[c_late] and g < c_late):
                        c_late = g
                g = c_now if c_now is not None else c_late
                st = max(t, rtime[g])
                if best is None or st < best[0] or (st == best[0] and prio[g] > prio[best[1]]):
                    best = (st, g, sname)
            st, g, sname = best
            nd = nodes[g]
            nd.start = st
            nd.finish = st + nd.lat
            free[sname] = st + nd.dur
            ready[sname].remove(g)
            done += 1
            for sgid in succs[g]:
                indeg[sgid] -= 1
                if indeg[sgid] == 0:
                    sn = nodes[sgid]
                    rt = 0.0
                    for d in sn.deps:
                        f = nodes[d].finish + (self.HOP if nodes[d].stream != sn.stream else 0.02)
                        if f > rt:
                            rt = f
                    rtime[sgid] = rt
                    ready[sn.stream].append(sgid)
        self.est_total = max(nd.finish for nd in nodes)

    def plan_sync(self):
        nodes = self.nodes
        order = sorted(range(len(nodes)), key=lambda g: (nodes[g].start, g))
        self.order = order
        cnt = {}
        for g in order:
            nd = nodes[g]
            nd.idx = cnt.get(nd.cls, 0)
            cnt[nd.cls] = nd.idx + 1
        self.cnt = cnt
        known = {s: {} for s in self.STREAMS}
        known_dma = {s: {} for s in self.STREAMS}
        snap = {}
        ring_nodes = {}
        for g in order:
            nd = nodes[g]
            if nd.cls.startswith("dma_"):
                ring_nodes.setdefault(nd.cls, []).append(g)

        def is_known(stream, d):
            if d.cls.startswith("dma_"):
                st = known_dma[stream].get(d.cls)
                return st is not None and d.idx in st
            return known[stream].get(d.cls, -1) >= d.idx

        def learn(stream, d):
            if d.cls.startswith("dma_"):
                known_dma[stream].setdefault(d.cls, set()).add(d.idx)
            else:
                if known[stream].get(d.cls, -1) < d.idx:
                    known[stream][d.cls] = d.idx
            sn = snap.get(d.gid)
            if sn is not None:
                k, kd = sn
                mine = known[stream]
                for c, i in k.items():
                    if mine.get(c, -1) < i:
                        mine[c] = i
                for c, st in kd.items():
                    known_dma[stream].setdefault(c, set()).update(st)

        for g in order:
            nd = nodes[g]
            stream = nd.stream
            deps = sorted((nodes[d] for d in nd.deps), key=lambda x: (x.cls, x.idx), reverse=True)
            waits = []
            for d in deps:
                if d.cls == "pe" and nd.cls == "pe":
                    assert d.idx < nd.idx
                    continue
                if is_known(stream, d):
                    continue
                waits.append(d.gid)
                if not d.cls.startswith("dma_"):
                    d.sig = True
                learn(stream, d)
            if nd.cls.startswith("dma_") and nd.idx >= NS_DMA:
                gd = nodes[ring_nodes[nd.cls][nd.idx - NS_DMA]]
                if not is_known(stream, gd):
                    waits.append(gd.gid)
                    learn(stream, gd)
            nd.waits = waits
            kd = known_dma[stream]
            for c in list(kd.keys()):
                if len(kd[c]) > 64:
                    kd[c] = set(sorted(kd[c])[-48:])
            snap[g] = (dict(known[stream]), {c: set(st) for c, st in kd.items()})

    def emit(self):
        nc = self.nc
        self.schedule()
        self.plan_sync()
        nodes = self.nodes
        per_stream = {s: [] for s in self.STREAMS}
        for g in self.order:
            per_stream[nodes[g].stream].append(nodes[g])
        rank = {}
        for e in COMPUTE:
            k = 0
            for nd in per_stream[e]:
                if nd.cls == e and nd.sig:
                    k += 1
                    rank[nd.gid] = k
        with ExitStack() as es:
            sem = {e: es.enter_context(nc.semaphore("s_" + e)) for e in COMPUTE}
            rings = {}
            for ring in ("dma_sp", "dma_pool", "dma_act"):
                if self.cnt.get(ring, 0) > 0:
                    rings[ring] = [es.enter_context(nc.semaphore("%s_%d" % (ring, i))) for i in range(NS_DMA)]
            block = es.enter_context(nc.Block())

            def run_stream(sname, eng):
                for nd in per_stream[sname]:
                    for dg in nd.waits:
                        d = nodes[dg]
                        if d.cls.startswith("dma_"):
                            eng.wait_ge(rings[d.cls][d.idx % NS_DMA], 16 * (d.idx // NS_DMA + 1))
                        else:
                            eng.wait_ge(sem[d.cls], rank[dg])
                    ins = nd.fn(eng)
                    if nd.cls.startswith("dma_"):
                        ins.then_inc(rings[nd.cls][nd.idx % NS_DMA], 16)
                    elif nd.sig:
                        ins.then_inc(sem[nd.cls], 1)
                if sname == "sp":
                    for ring, sems in rings.items():
                        n_ = self.cnt[ring]
                        for i in range(min(NS_DMA, n_)):
                            last = i + NS_DMA * ((n_ - 1 - i) // NS_DMA)
                            eng.wait_ge(sems[i], 16 * (last // NS_DMA + 1))

            @block.tensor
            def _(pe):
                run_stream("pe", pe)

            @block.scalar
            def _(act):
                run_stream("act", act)

            @block.vector
            def _(dve):
                run_stream("dve", dve)

            @block.gpsimd
            def _(pool):
                run_stream("pool", pool)

            @block.sync
            def _(sp):
                run_stream("sp", sp)


class Pool_:
    def __init__(self, tiles, rs=None):
        self.tiles = tiles
        self.rs = rs if rs is not None else [R() for _ in tiles]
        self.i = 0

    def get(self):
        t, r = self.tiles[self.i], self.rs[self.i]
        self.i = (self.i + 1) % len(self.tiles)
        return t, r


class MK:
    def __init__(self, n_ptiles=4, do_sample=True, dbg=None, reorder=True):
        self.n_ptiles = n_ptiles
        self.do_sample = do_sample
        self.dbg = dbg or {}
        self.nc = bass.Bass("TRN2", target_bir_lowering=False)
        self.em = Em(self.nc, reorder=reorder)
        self.es = ExitStack()

    def din(self, name, shape):
        return self.nc.dram_tensor(name, list(shape), F32, kind="ExternalInput").ap()

    def dout(self, name, shape):
        return self.nc.dram_tensor(name, list(shape), F32, kind="ExternalOutput").ap()

    def sbt(self, name, shape, dt):
        return self.es.enter_context(self.nc.sbuf_tensor("sb_" + name, list(shape), dt))

    @staticmethod
    def nfree(ap):
        try:
            shp = ap.shape
        except Exception:
            shp = ap[:].shape
        n = 1
        for x in shp[1:]:
            n *= x
        return n

    def mm(self, out, lhsT, rhs, r, w, start=True, stop=True):
        n = self.nfree(rhs)
        d = max(0.06, n / 1900.0) * (4.0 if rhs.dtype == F32 else 1.0)
        self.em.op("pe", lambda e: e.matmul(out, lhsT=lhsT, rhs=rhs, start=start, stop=stop), r=r, w=w, dur=d)

    def tr(self, out, in_, ident, r, w):
        d = 0.07 * (4.0 if in_.dtype == F32 else 1.0)
        self.em.op("pe", lambda e: e.transpose(out=out, in_=in_, identity=ident), r=r, w=w, dur=d)

    def act(self, out, in_, func, r, w, bias=None, scale=None):
        kw = {}
        if bias is not None:
            kw["bias"] = bias
        if scale is not None:
            kw["scale"] = scale
        self.em.op("act", lambda e: e.activation(out=out, in_=in_, func=func, **kw), r=r, w=w, dur=0.2 + self.nfree(out) / 1200.0)

    def cp(self, eng, out, in_, r, w):
        if eng == "act":
            self.em.op("act", lambda e: e.copy(out=out, in_=in_), r=r, w=w, dur=0.2 + self.nfree(out) / 1200.0)
        else:
            self.em.op(eng, lambda e: e.tensor_copy(out=out, in_=in_), r=r, w=w, dur=0.12 + self.nfree(out) / 1500.0)

    def tt(self, out, in0, in1, op, r, w, eng="dve"):
        self.em.op(eng, lambda e: e.tensor_tensor(out=out, in0=in0, in1=in1, op=op), r=r, w=w, dur=0.12 + self.nfree(out) / 960.0)

    def ts(self, out, in0, s1, s2, op0, op1, r, w, eng="dve"):
        if s2 is None:
            self.em.op(eng, lambda e: e.tensor_scalar(out=out, in0=in0, scalar1=s1, scalar2=None, op0=op0), r=r, w=w, dur=0.12 + self.nfree(out) / 1500.0)
        else:
            self.em.op(eng, lambda e: e.tensor_scalar(out=out, in0=in0, scalar1=s1, scalar2=s2, op0=op0, op1=op1), r=r, w=w, dur=0.12 + self.nfree(out) / 1500.0)

    def stt(self, out, in0, scalar, in1, op0, op1, r, w, eng="dve"):
        self.em.op(eng, lambda e: e.scalar_tensor_tensor(out=out, in0=in0, scalar=scalar, in1=in1, op0=op0, op1=op1), r=r, w=w, dur=0.12 + self.nfree(out) / 960.0)

    def memset(self, ap, val, w, eng="dve"):
        self.em.op(eng, lambda e: e.memset(ap, val), w=w)

    def dma(self, out, in_, r=(), w=(), q="sp"):
        try:
            nb = 128 * self.nfree(in_) * 4
        except Exception:
            nb = 65536
        self.em.dma(q, lambda e: e.dma_start(out=out, in_=in_), r=r, w=w, nbytes=nb)

    def ps_alloc(self):
        assert self.ps_free, "out of PSUM banks"
        b = self.ps_free.pop(0)
        return b

    def ps_release(self, b):
        self.ps_free.append(b)

    def w_init(self):
        self.NSLOT = 3
        self.wslots = [self.sbt("wslot%d" % i, [128, 4096], BF16) for i in range(self.NSLOT)]
        self.wR = [[R() for _ in range(4)] for _ in range(self.NSLOT)]
        self.wlist = []
        self.w_issued = 0
        self.w_cur = -1

    def w_plan(self, spec):
        self.wlist.append(spec)

    def w_issue_upto(self, i):
        while self.w_issued <= min(i, len(self.wlist) - 1):
            g = self.w_issued
            slot = self.wslots[g % self.NSLOT]
            rr = self.wR[g % self.NSLOT]
            nd = len(self.wlist[g])
            for di, (dst_fn, src) in enumerate(self.wlist[g]):
                ww = rr
                self.dma(dst_fn(slot), src, w=ww, q="pool")
            self.w_issued += 1

    def w_next(self):
        self.w_cur += 1
        i = self.w_cur
        self.w_issue_upto(i + self.NSLOT - 1)
        return self.wslots[i % self.NSLOT], self.wR[i % self.NSLOT]

    def build(self):
        nc = self.nc
        P = 128
        self.xp = self.din("xp", [SEQ, D])
        self.xs = self.din("xs", [SB, ST, D])
        self.st_hg = self.din("st_hg", [4, 128, SB, 128])
        self.st_gd = self.din("st_gd", [4, 128, SB, 128])
        self.st_gc = self.din("st_gc", [SB, 3, 1536])
        self.st_fc = self.din("st_fc", [SB, 2, DFF])
        self.ck = self.din("ck", [SB, NMEM, D])
        self.cv = self.din("cv", [SB, NMEM, D])
        self.memp = self.din("memp", [NMEM, D])
        self.prm_d = self.din("prm", [128, NPRM])
        self.cst_d = self.din("cst", [128, NCST])
        self.w_in = self.din("w_in", [8, 128, 4096])
        self.w_tail = self.din("w_tail", [128, 64])
        self.w_out = self.din("w_out", [2, 128, 4096])
        self.w_mq = self.din("w_mq", [2, 128, 4096])
        self.w_mkv = self.din("w_mkv", [4, 128, 4096])
        self.w_mo = self.din("w_mo", [2, 128, 4096])
        self.w_up = self.din("w_up", [11, 128, 4096])
        self.w_down = self.din("w_down", [8, 128, 22 * 128])
        self.yp = self.dout("yp", [SEQ, D])
        self.ys = self.dout("ys", [SB, ST, D])
        self.p_hg = self.dout("p_hg", [4, 128, 128])
        self.p_gd = self.dout("p_gd", [4, 128, 128])
        self.p_gc = self.dout("p_gc", [3, 1536])
        self.p_fc = self.dout("p_fc", [2, DFF])
        self.p_mk = self.dout("p_mk", [NMEM, D])
        self.p_mv = self.dout("p_mv", [NMEM, D])
        self.s_hg = self.dout("s_hg", [4, 128, SB, 128])
        self.s_gd = self.dout("s_gd", [4, 128, SB, 128])
        self.s_gc = self.dout("s_gc", [SB, 3, 1536])
        self.s_fc = self.dout("s_fc", [SB, 2, DFF])
        self.dbg_out = {k: self.dout("dbg_" + k, shp) for k, shp in self.dbg.items()}

        with self.es:
            self.alloc()
            self.setup()
            self.plan_weights()
            tiles = [dict(kind="p", N=512, ti=ti, col0=ti * 512, last=(ti == self.n_ptiles - 1)) for ti in range(self.n_ptiles)]
            if self.do_sample:
                tiles.append(dict(kind="s", N=64, ti=0, col0=0, last=True))
            for i, T_ in enumerate(tiles):
                self.next_tile = tiles[i + 1] if i + 1 < len(tiles) else None
                self.tile(T_)
            self.em.emit()
        return nc

    def alloc(self):
        nc = self.nc
        self.cst = self.sbt("cst", [128, NCST], F32); self.Rcst = R()
        self.prm = self.sbt("prm", [128, NPRM], F32); self.Rprm = R()
        self.cb = self.sbt("cb", [128, 5 * 128], BF16); self.Rcb = R()
        self.sm = self.sbt("sm", [128, 16], F32); self.Rsm = R()
        self.io = [self.sbt("io%d" % i, [128, D], F32) for i in range(2)]
        self.Rio = [R(), R()]
        self.io_i = 0
        self.yo = [self.sbt("yo%d" % i, [128, D], F32) for i in range(2)]
        self.Ryo = [R(), R()]
        self.yo_i = 0
        self.x_pref = {}
        self.hs = self.sbt("hs", [128, 8, 512], F32); self.Rhs = [R() for _ in range(8)]
        self.hb = self.sbt("hb", [128, 8, 512], BF16); self.Rhb = [R() for _ in range(8)]
        self.mq = self.sbt("mq", [128, 8, 512], BF16); self.Rmq = [R() for _ in range(8)]
        self.AR = self.sbt("arena", [128, 8192], F32)
        self.tF = Pool_([self.sbt("tF%d" % i, [128, 520], F32) for i in range(12)])
        self.tB = Pool_([self.sbt("tB%d" % i, [128, 512], BF16) for i in range(12)])
        self.g_sF = Pool_([self.sbt("sF%d" % i, [128, 128], F32) for i in range(24)])
        self.g_sBp = Pool_([self.sbt("sB%d" % i, [128, 128], BF16) for i in range(48)])
        self.h_sF = [Pool_(self.g_sF.tiles[6 * h:6 * h + 6], self.g_sF.rs[6 * h:6 * h + 6]) for h in range(4)]
        self.h_sBp = [Pool_(self.g_sBp.tiles[12 * h:12 * h + 12], self.g_sBp.rs[12 * h:12 * h + 12]) for h in range(4)]
        self.h_sBl = [Pool_([self.sbt("sBl%d_%d" % (h, i), [128, 128], BF16) for i in range(8)]) for h in range(4)]
        self.sF, self.sBp, self.sBl = self.g_sF, self.g_sBp, self.h_sBl[0]
        self.interleaving = False
        self.egs = self.sbt("egs", [128, 4, 16], F32); self.Regs = [R() for _ in range(4)]
        self.egl_all = self.sbt("egl_all", [128, 4, 8], F32); self.Regl = [[R() for _ in range(8)] for _ in range(4)]
        self.vtp = Pool_([self.sbt("vtp%d" % i, [128, 1024], BF16) for i in range(2)])
        self.w_init()
        self.mkF = self.sbt("mkF", [128, 8, 256], BF16); self.RmkF = R()
        self.mvT = self.sbt("mvT", [128, 2, D], BF16); self.RmvT = R()
        self.Shg = self.sbt("Shg", [128, 4, 128], F32); self.RShg = [R() for _ in range(4)]
        self.Shgb = self.sbt("Shgb", [128, 4, 128], BF16); self.RShgb = [R() for _ in range(4)]
        self.Sgd = self.sbt("Sgd", [128, 4, 128], F32); self.RSgd = [R() for _ in range(4)]
        self.Sgdb = self.sbt("Sgdb", [128, 4, 128], BF16); self.RSgdb = [R() for _ in range(4)]
        self.gdh = self.sbt("gdh", [128, 12, 48], F32); self.Rgdh = [R() for _ in range(12)]
        self.ffh = self.sbt("ffh", [128, 22, 32], F32); self.Rffh = [R() for _ in range(22)]
        self.tms = self.sbt("tms", [64, 8, 20], F32); self.Rtms = R()
        self.rowg = self.sbt("rowg", [4, 512], F32); self.Rrowg = R()
        self.psb = [self.es.enter_context(nc.psum_tensor("psb%d" % i, [128, 512], F32)) for i in range(8)]
        self.Rps = [R(excl=True) for _ in range(8)]
        self.g_ps_free = list(range(8))
        self.ps_free = self.g_ps_free
        self.arena_rs = []

    def aview(self, off_f32, nelem_f32, dt, pattern=None, **kw):
        ap = self.AR[:, off_f32:off_f32 + nelem_f32]
        if dt == BF16:
            ap = ap.bitcast(BF16)
        if pattern:
            ap = ap.rearrange(pattern, **kw)
        return ap

    def arena_switch(self, new_rs):
        alias(new_rs, self.arena_rs)
        self.arena_rs = list(new_rs)

    def setup(self):
        cst, prm = self.cst, self.prm
        self.dma(cst[:], self.cst_d, w=[self.Rcst])
        self.dma(prm[:], self.prm_d, w=[self.Rprm])
        self.identf = cst[:, C_ID:C_ID + 128]
        cb = self.cb
        self.identb = cb[:, 0:128]
        self.ones1 = cb[:, 128:256]
        self.ones128 = cb[:, 256:384]
        self.ones1024 = cb[:, 384:512]
        self.cp("dve", self.identb, self.identf, r=[self.Rcst], w=[self.Rcb])
        self.mx = self.sbt("mx", [64, 256], F32); self.Rmx = R()
        self.mxb = self.sbt("mxb", [64, 128], BF16)
        self.Mb1x2 = self.mx[:, 0:128]
        self.Mb2x2 = self.mx[:, 128:256]
        self.idb2 = self.mxb[:, 0:128]
        for j in range(2):
            self.cp("dve", self.mx[:, j * 64:(j + 1) * 64], cst[0:64, C_MB1:C_MB1 + 64], r=[self.Rcst], w=[self.Rmx])
            self.cp("dve", self.mx[:, 128 + j * 64:128 + (j + 1) * 64], cst[0:64, C_MB2:C_MB2 + 64], r=[self.Rcst], w=[self.Rmx])
            self.cp("dve", self.mxb[:, j * 64:(j + 1) * 64], self.identf[0:64, 0:64], r=[self.Rcst], w=[self.Rmx])
        self.memset(self.ones1, 1.0, w=[self.Rcb])
        self.memset(self.ones128, 1.0 / 128.0, w=[self.Rcb])
        self.memset(self.ones1024, 1.0 / 1024.0, w=[self.Rcb])
        sm = self.sm
        self.tt(sm[:, 0:4], prm[:, P_L0:P_L0 + 4], prm[:, P_L1:P_L1 + 4], ALU.subtract, r=[self.Rprm], w=[self.Rsm])
        self.act(sm[:, 4:8], sm[:, 0:4], AF.Sigmoid, r=[self.Rsm], w=[self.Rsm], scale=-1.0)
        self.act(sm[:, 0:4], sm[:, 0:4], AF.Sigmoid, r=[self.Rsm], w=[self.Rsm])
        self.act(sm[0:4, 8:9], prm[0:4, P_ALOG:P_ALOG + 1], AF.Exp, r=[self.Rprm], w=[self.Rsm])
        self.ts(sm[0:4, 8:9], sm[0:4, 8:9], -1.0, None, ALU.mult, None, r=[self.Rsm], w=[self.Rsm])
        self.memset(sm[:, 9:10], LN_EPS, w=[self.Rsm])
        self.memset(sm[:, 10:11], RMS_EPS, w=[self.Rsm])
        self.memset(sm[:, 11:12], 1.0, w=[self.Rsm])
        self.one_col = sm[:, 11:12]
        self.eps_ln = sm[:, 9:10]
        self.eps_rms = sm[:, 10:11]
        self.lb = lambda h: sm[:, h:h + 1]
        self.oml = lambda h: sm[:, 4 + h:5 + h]
        for h in range(4):
            self.memset(self.Shg[:, h, :], 0.0, w=[self.RShg[h]])
            self.memset(self.Shgb[:, h, :], 0.0, w=[self.RShgb[h]])
            self.memset(self.Sgd[:, h, :], 0.0, w=[self.RSgd[h]])
            self.memset(self.Sgdb[:, h, :], 0.0, w=[self.RSgdb[h]])
        self.memset(self.gdh[:], 0.0, w=self.Rgdh)
        self.memset(self.ffh[:], 0.0, w=self.Rffh)

    def plan_weights(self):
        def v3(slot, n):
            return slot[:, 0:8 * n].rearrange("p (k n) -> p k n", n=n)

        full = lambda src: [(lambda s: s[:, 0:4096], src)]
        for g in range(4):
            self.w_plan(full(self.w_mkv[g]))
        ntile = self.n_ptiles + (1 if self.do_sample else 0)
        for _ in range(ntile):
            for h in range(4):
                self.w_plan(full(self.w_in[h]))
            self.w_plan([(lambda s: s[:, 0:64], self.w_tail)])
            for h in range(4):
                self.w_plan(full(self.w_in[4 + h]))
            for g in range(2):
                self.w_plan(full(self.w_out[g]))
            for g in range(2):
                self.w_plan(full(self.w_mq[g]))
            for g in range(2):
                self.w_plan(full(self.w_mo[g]))
            for g in range(11):
                self.w_plan(full(self.w_up[g]))
            for g in range(8):
                self.w_plan([(lambda s: s[:, 0:22 * 128], self.w_down[g])])
        self.v3 = v3

    def mark(self, name):
        if not hasattr(self, 'marks'):
            self.marks = []
        self.marks.append((name, len(self.em.nodes)))

    def tile(self, T):
        self.T = T
        self.mark('tile_%s%d' % (T['kind'], T['ti']))
        upto = getattr(self, "upto", 99)
        if upto >= 0 and T["kind"] == "p" and T["ti"] == 0:
            self.stage_memkv()
        if upto >= 1:
            if T["kind"] == "s":
                self.stage_sample_hist()
            self.stage_load_x()
            if T["kind"] == "s" and "hs_s" in self.dbg_out:
                for k in range(8):
                    self.dma(self.dbg_out["hs_s"][:, k, :], self.hs[:, k, 0:64], r=[self.Rhs[k]])
        self.mark('mixer')
        if upto >= 2:
            self.stage_mixer()
        def skip(n):
            for _ in range(n):
                self.w_next()
        if upto < 2:
            skip(9)
        elif upto < 3:
            skip(5)
        elif upto < 4:
            skip(4)
        self.mark('wout')
        if upto >= 5:
            self.stage_wout_ln1()
        else:
            skip(2)
        self.mark('attn')
        if upto >= 6:
            self.stage_attn_ln2()
        else:
            skip(4)
        self.mark('ffn')
        if upto >= 7:
            self.stage_ffn_ln3()
        else:
            skip(19)
        self.mark('store')
        if upto >= 8:
            self.stage_store_y()

    def layer_norm(self, N, gcol, bcol, shadow):
        hs, hb = self.hs, self.hb
        b_mean = self.ps_alloc(); b_sq = self.ps_alloc()
        pm, pq = self.psb[b_mean], self.psb[b_sq]
        for m in range(8):
            yb, Ryb = self.tB.get()
            ysq, Rysq = self.tB.get()
            self.cp("act", yb[:, :N], hs[:, m, :N], r=[self.Rhs[m]], w=[Ryb])
            self.act(ysq[:, :N], hs[:, m, :N], AF.Square, r=[self.Rhs[m]], w=[Rysq])
            self.mm(pm[:, :N], self.ones1024, yb[:, :N], r=[self.Rcb, Ryb], w=[self.Rps[b_mean]], start=(m == 0), stop=(m == 7))
            self.mm(pq[:, :N], self.ones1024, ysq[:, :N], r=[self.Rcb, Rysq], w=[self.Rps[b_sq]], start=(m == 0), stop=(m == 7))
        mean, Rmean = self.tF.get()
        var, Rvar = self.tF.get()
        rstd, Rrstd = self.tF.get()
        nmr, Rnmr = self.tF.get()
        self.cp("act", mean[:, :N], pm[:, :N], r=[self.Rps[b_mean]], w=[Rmean])
        self.tt(var[:, :N], mean[:, :N], mean[:, :N], ALU.mult, r=[Rmean], w=[Rvar])
        self.tt(var[:, :N], pq[:, :N], var[:, :N], ALU.subtract, r=[self.Rps[b_sq], Rvar], w=[Rvar])
        self.ps_release(b_mean); self.ps_release(b_sq)
        self.act(var[:, :N], var[:, :N], AF.Ln, r=[Rvar], w=[Rvar], bias=self.eps_ln)
        self.act(rstd[:, :N], var[:, :N], AF.Exp, r=[Rvar], w=[Rrstd], scale=-0.5)
        self.tt(nmr[:, :N], mean[:, :N], rstd[:, :N], ALU.mult, r=[Rmean, Rrstd], w=[Rnmr])
        for m in range(8):
            t1, Rt1 = self.tF.get()
            self.tt(t1[:, :N], hs[:, m, :N], rstd[:, :N], ALU.mult, r=[self.Rhs[m], Rrstd], w=[Rt1])
            self.tt(t1[:, :N], t1[:, :N], nmr[:, :N], ALU.subtract, r=[Rt1, Rnmr], w=[Rt1])
            g = self.prm[:, gcol + m:gcol + m + 1]
            b = self.prm[:, bcol + m:bcol + m + 1]
            self.act(hs[:, m, :N], t1[:, :N], AF.Identity, r=[Rt1, self.Rprm], w=[self.Rhs[m]], scale=g, bias=b)
            if shadow:
                self.act(hb[:, m, :N], t1[:, :N], AF.Identity, r=[Rt1, self.Rprm], w=[self.Rhb[m]], scale=g, bias=b)

    def proj_residual(self, src, Rsrc, nk, N, ngroups, slot_view, cols_per_group):
        mpg = cols_per_group // 128
        for g in range(ngroups):
            slot, Rw = self.w_next()
            wv = slot_view(slot)
            for mi in range(mpg):
                m = g * mpg + mi
                b = self.ps_alloc(); ps = self.psb[b]
                for k in range(nk):
                    self.mm(ps[:, :N], wv[:, k, mi * 128:(mi + 1) * 128], src[:, k, :N], r=Rw + [Rsrc[k]], w=[self.Rps[b]], start=(k == 0), stop=(k == nk - 1))
                self.stt(self.hs[:, m, :N], self.hs[:, m, :N], ALPHA, ps[:, :N], ALU.mult, ALU.add, r=[self.Rhs[m], self.Rps[b]], w=[self.Rhs[m]])
                self.ps_release(b)

    def stage_memkv(self):
        N = NMEM
        for blk in range(2):
            io, Rio = self.io[blk], self.Rio[blk]
            self.dma(io[:], self.memp[blk * 128:(blk + 1) * 128, :], w=[Rio])
            for half in range(2):
                b = self.ps_alloc(); ps = self.psb[b]
                for kk in range(4):
                    k = half * 4 + kk
                    self.tr(ps[:, kk * 128:(kk + 1) * 128], io[:, k * 128:(k + 1) * 128], self.identf, r=[Rio, self.Rcst], w=[self.Rps[b]])
                for kk in range(4):
                    k = half * 4 + kk
                    self.cp("act" if kk % 2 else "dve", self.mq[:, k, blk * 128:(blk + 1) * 128], ps[:, kk * 128:(kk + 1) * 128], r=[self.Rps[b]], w=[self.Rmq[k]])
                self.ps_release(b)
        v3 = self.v3
        mku = getattr(self, "mku", 99)
        if mku < 1:
            return
        for g in range(4):
            if g >= mku:
                return
            slot, Rw = self.w_next()
            wv = v3(slot, 512)
            if g < 2:
                for mi in range(4):
                    dch = g * 4 + mi
                    b = self.ps_alloc(); ps = self.psb[b]
                    for k in range(8):
                        self.mm(ps[:, :N], wv[:, k, mi * 128:(mi + 1) * 128], self.mq[:, k, :N], r=Rw + [self.Rmq[k]], w=[self.Rps[b]], start=(k == 0), stop=(k == 7))
                    self.cp("act", self.mkF[:, dch, :], ps[:, :N], r=[self.Rps[b]], w=[self.RmkF])
                    self.ps_release(b)
            for blk in range(2):
                b = self.ps_alloc(); ps = self.psb[b]
                for k in range(8):
                    self.mm(ps[:, :], self.mq[:, k, blk * 128:(blk + 1) * 128], wv[:, k, :], r=Rw + [self.Rmq[k]], w=[self.Rps[b]], start=(k == 0), stop=(k == 7))
                o, Ro = self.tF.get()
                self.cp("dve", o[:, 0:512], ps[:, :], r=[self.Rps[b]], w=[Ro])
                if g >= 2:
                    self.cp("act", self.mvT[:, blk, (g - 2) * 512:(g - 1) * 512], ps[:, :], r=[self.Rps[b]], w=[self.RmvT])
                self.ps_release(b)
                dst = self.p_mk if g < 2 else self.p_mv
                self.dma(dst[blk * 128:(blk + 1) * 128, (g % 2) * 512:(g % 2 + 1) * 512], o[:, 0:512], r=[Ro])

    def x_block_dma(self, T, blk):
        i = self.io_i; self.io_i ^= 1
        io, Rio = self.io[i], self.Rio[i]
        if T["kind"] == "p":
            r0 = T["col0"] + blk * 128
            self.dma(io[:], self.xp[r0:r0 + 128, :], w=[Rio])
        else:
            for t in range(ST):
                self.dma(io[t * SB:(t + 1) * SB, :], self.xs[:, t, :], w=[Rio])
        return io, Rio

    def stage_load_x(self):
        T = self.T
        N = T["N"]
        nblk = (N + 127) // 128
        key = (T["kind"], T["ti"])
        for blk in range(nblk):
            if (key, blk) in self.x_pref:
                io, Rio = self.x_pref.pop((key, blk))
            else:
                io, Rio = self.x_block_dma(T, blk)
            np_ = 128 if T["kind"] == "p" else 64
            for half in range(2):
                b = self.ps_alloc(); ps = self.psb[b]
                for kk in range(4):
                    k = half * 4 + kk
                    self.tr(ps[:, kk * 128:kk * 128 + np_], io[0:np_, k * 128:(k + 1) * 128], self.identf[0:np_, 0:np_], r=[Rio, self.Rcst], w=[self.Rps[b]])
                c0 = blk * 128
                k0 = half * 4
                self.cp("act", self.hs[:, k0:k0 + 4, c0:c0 + np_], ps[:, :].rearrange("p (k n) -> p k n", n=128)[:, :, 0:np_], r=[self.Rps[b]], w=self.Rhs[k0:k0 + 4])
                self.ps_release(b)
                self.cp("dve", self.hb[:, k0:k0 + 4, c0:c0 + np_], self.hs[:, k0:k0 + 4, c0:c0 + np_], r=self.Rhs[k0:k0 + 4], w=self.Rhb[k0:k0 + 4])
        nxt = self.next_tile
        if nxt is not None:
            nb = 2 if nxt["kind"] == "p" else 1
            for blk in range(nb):
                self.x_pref[((nxt["kind"], nxt["ti"]), blk)] = self.x_block_dma(nxt, blk)

    def stage_sample_hist(self):
        for (src, nrow, ncol, halo, Rh) in ((self.st_gc, 3, 1536, self.gdh, self.Rgdh), (self.st_fc, 2, DFF, self.ffh, self.Rffh)):
            np_ = nrow * SB
            for c0 in range(0, ncol, 1024):
                w_ = min(1024, ncol - c0)
                i = self.yo_i; self.yo_i ^= 1
                io, Rio = self.yo[i], self.Ryo[i]
                for r_ in range(nrow):
                    self.dma(io[r_ * SB:(r_ + 1) * SB, 0:w_], src[:, r_, c0:c0 + w_], w=[Rio])
                for jj in range(w_ // 128):
                    j = c0 // 128 + jj
                    b = self.ps_alloc(); ps = self.psb[b]
                    self.tr(ps[:, 0:np_], io[0:np_, jj * 128:(jj + 1) * 128], self.identf[0:np_, 0:np_], r=[Rio, self.Rcst], w=[self.Rps[b]])
                    self.cp("act" if jj % 2 else "dve", halo[:, j, 0:np_], ps[:, 0:np_], r=[self.Rps[b]], w=[Rh[j]])
                    self.ps_release(b)

    def stage_wout_ln1(self):
        N = self.T["N"]
        self.proj_residual(self.mq, self.Rmq, 8, N, 2, lambda s: self.v3(s, 512), 512)
        self.layer_norm(N, P_LN + 0, P_LN + 8, shadow=True)

    def stage_store_y(self):
        T = self.T
        N = T["N"]
        nblk = (N + 127) // 128
        for blk in range(nblk):
            np_ = min(128, N - blk * 128)
            i = self.yo_i; self.yo_i ^= 1
            io, Rio = self.yo[i], self.Ryo[i]
            for half in range(2):
                b = self.ps_alloc(); ps = self.psb[b]
                for kk in range(4):
                    k = half * 4 + kk
                    self.tr(ps[0:np_, kk * 128:(kk + 1) * 128], self.hs[:, k, blk * 128:blk * 128 + np_], self.identf, r=[self.Rhs[k], self.Rcst], w=[self.Rps[b]])
                self.cp("act" if half else "dve", io[0:np_, half * 512:(half + 1) * 512], ps[0:np_, :], r=[self.Rps[b]], w=[Rio])
                self.ps_release(b)
            if T["kind"] == "p":
                r0 = T["col0"] + blk * 128
                self.dma(self.yp[r0:r0 + 128, :], io[:], r=[Rio])
            else:
                for t in range(ST):
                    self.dma(self.ys[:, t, :], io[t * SB:(t + 1) * SB, :], r=[Rio])

    def stage_mixer(self):
        T = self.T
        samp = T["kind"] == "s"
        upto = getattr(self, "upto", 99)
        self.unit = SB if samp else 1
        self.nch = 1 if samp else T["N"] // 64
        self.mo = 64 if samp else 0
        if samp:
            self.sst2 = [self.aview(0, 2048, F32, "p (b v) -> p b v", v=128), self.aview(6144, 2048, F32, "p (b v) -> p b v", v=128)]
            self.Rsst2 = [R(), R()]
            self.sstb2 = [self.aview(2048, 1024, BF16, "p (b v) -> p b v", v=128), self.aview(3072, 1024, BF16, "p (b v) -> p b v", v=128)]
            self.vblk2 = [self.aview(4096, 1024, BF16), self.aview(5120, 1024, BF16)]
            self.Rsstb2, self.Rvblk2 = [R(), R()], [R(), R()]
            self.arena_switch(self.Rsst2 + self.Rsstb2 + self.Rvblk2)
            self.sst_q = [(self.st_hg, h) for h in range(4)] + [(self.st_gd, h) for h in range(4)]
            self.sst_issued = 0
            self.sst_cur = -1
            self.issue_state_load()
            self.set_ctx(0)
            for h in range(4):
                P = self.hgrn_prelude(h, None)
                for _ in self.hgrn_loop(h, P):
                    pass
                self.hgrn_epilogue(h, P)
            self.mark('gdn')
            if upto >= 3:
                self.gdn_rows()
            if upto >= 4:
                for h in range(4):
                    P = self.gdn_prelude(h, None)
                    for _ in self.gdn_loop(h, P):
                        pass
                    self.gdn_epilogue(h, P)
            self.set_ctx(None)
            return
        views = []
        allr = []
        for h in range(4):
            base = h * 2048
            V = dict(qt=self.aview(base, 256, BF16), kt=self.aview(base + 256, 256, BF16), kg=self.aview(base + 512, 256, BF16),
                     sg=self.aview(base + 768, 256, BF16), vt=self.aview(base + 1024, 512, BF16), oc=self.aview(base + 1536, 512, F32))
            for k in list(V.keys()):
                V["R" + k] = R()
                allr.append(V["R" + k])
            views.append(V)
        self.arena_switch(allr)
        Ps = [self.hgrn_prelude(h, views[h]) for h in range(4)]
        for _ in self.hgrn_loop_batched(Ps):
            pass
        for h in range(4):
            self.hgrn_epilogue(h, Ps[h])
        self.mark('gdn')
        if upto >= 3:
            self.gdn_rows()
        if upto >= 4:
            views = []
            allr = []
            for h in range(4):
                base = h * 1536
                V = dict(qn=self.aview(base, 256, BF16), kn=self.aview(base + 256, 256, BF16), vv=self.aview(base + 512, 256, BF16),
                         sg=self.aview(base + 768, 256, BF16), oc=self.aview(base + 1024, 512, F32))
                for k in list(V.keys()):
                    V["R" + k] = R()
                    allr.append(V["R" + k])
                views.append(V)
            self.arena_switch(allr)
            Ps = [self.gdn_prelude(h, views[h]) for h in range(4)]
            self.run_interleaved([self.gdn_loop_pair(h, Ps[h]) for h in range(4)])
            for h in range(4):
                self.gdn_epilogue(h, Ps[h])

    def set_ctx(self, h):
        if h is None:
            self.sF, self.sBp, self.ps_free = self.g_sF, self.g_sBp, self.g_ps_free
            self.sBl = self.h_sBl[0]
        else:
            self.sF, self.sBp, self.sBl = self.h_sF[h], self.h_sBp[h], self.h_sBl[h]
            self.ps_free = self.h_ps_free[h] if self.interleaving else self.g_ps_free

    def run_interleaved(self, gens):
        assert len(self.g_ps_free) == 8, "PSUM banks must all be free before interleaving"
        self.interleaving = True
        self.h_ps_free = [[2 * h, 2 * h + 1] for h in range(4)]
        active = list(range(len(gens)))
        for i in list(active):
            self.set_ctx(i)
            for _ in range(i * getattr(self, "stagger", 0)):
                try:
                    next(gens[i])
                except StopIteration:
                    active.remove(i)
                    break
        while active:
            for i in list(active):
                self.set_ctx(i)
                try:
                    next(gens[i])
                except StopIteration:
                    active.remove(i)
        for h in range(4):
            assert sorted(self.h_ps_free[h]) == [2 * h, 2 * h + 1]
        self.interleaving = False
        self.g_ps_free[:] = list(range(8))
        self.set_ctx(None)

    def proj_fm(self, wv, Rw, c0, N, M=128):
        b = self.ps_alloc(); ps = self.psb[b]
        for k in range(8):
            self.mm(ps[0:M, :N], wv[:, k, c0:c0 + M], self.hb[:, k, :N], r=Rw + [self.Rhb[k]], w=[self.Rps[b]], start=(k == 0), stop=(k == 7))
        return b

    def cumsum_rows(self, out, Rout, src, Rsrc, npart, N):
        if self.T["kind"] == "p":
            msk = self.cst[0:npart, C_SCAN:C_SCAN + N]
            self.em.op("dve", lambda e: e.tensor_tensor_scan(out=out[0:npart, :N], data0=msk, data1=src[0:npart, :N], initial=0.0, op0=ALU.mult, op1=ALU.add), r=[Rsrc, self.Rcst], w=[Rout], dur=0.12 + N / 960.0)
        else:
            self.cp("dve", out[0:npart, 0:SB], src[0:npart, 0:SB], r=[Rsrc], w=[Rout])
            for t in range(1, ST):
                self.tt(out[0:npart, t * SB:(t + 1) * SB], out[0:npart, (t - 1) * SB:t * SB], src[0:npart, t * SB:(t + 1) * SB], ALU.add, r=[Rsrc, Rout], w=[Rout])

    def rms_gate_out(self, oc, Roc, N, sg, Rsg, gcol, mchunk):
        sq, Rsq = self.tB.get()
        self.act(sq[:, :N], oc[:, :N], AF.Square, r=[Roc], w=[Rsq])
        b2 = self.ps_alloc(); pm = self.psb[b2]
        self.mm(pm[:, :N], self.ones128, sq[:, :N], r=[self.Rcb, Rsq], w=[self.Rps[b2]])
        rs, Rrs = self.tF.get()
        self.act(rs[:, :N], pm[:, :N], AF.Ln, r=[self.Rps[b2]], w=[Rrs], bias=self.eps_rms)
        self.ps_release(b2)
        self.act(rs[:, :N], rs[:, :N], AF.Exp, r=[Rrs], w=[Rrs], scale=-0.5)
        t1, Rt1 = self.tF.get()
        self.tt(t1[:, :N], oc[:, :N], rs[:, :N], ALU.mult, r=[Roc, Rrs], w=[Rt1])
        self.stt(self.mq[:, mchunk, :N], t1[:, :N], self.prm[:, gcol:gcol + 1], sg[:, :N], ALU.mult, ALU.mult, r=[Rt1, Rsg, self.Rprm], w=[self.Rmq[mchunk]])

    def issue_state_load(self):
        if self.sst_issued < len(self.sst_q):
            src, h = self.sst_q[self.sst_issued]
            i = self.sst_issued % 2
            self.dma(self.sst2[i], src[h], w=[self.Rsst2[i]])
            self.sst_issued += 1

    def load_sample_state(self, src, h):
        self.sst_cur += 1
        assert self.sst_q[self.sst_cur][1] == h
        i = self.sst_cur % 2
        self.sst, self.Rsst = self.sst2[i], self.Rsst2[i]
        self.sstb, self.Rsstb = self.sstb2[i], self.Rsstb2[i]
        self.vblk, self.Rvblk = self.vblk2[i], self.Rvblk2[i]
        self.snew, self.Rsnew = self.sst, self.Rsst
        self.cp("act", self.sstb, self.sst, r=[self.Rsst], w=[self.Rsstb])
        self.issue_state_load()

    def sample_state_update(self, ktm, Rktm, vsrc, Rvsrc, eg16, Reg, dst, h):
        vb3 = self.vblk[0:64, :].rearrange("p (b v) -> p b v", v=128)
        bm = self.cst[0:64, C_BM:C_BM + SB]
        self.tt(vb3, vsrc.unsqueeze(1).to_broadcast([64, SB, 128]), bm.unsqueeze(2).to_broadcast([64, SB, 128]), ALU.mult, r=[Rvsrc, self.Rcst], w=[self.Rvblk])
        for k4 in range(4):
            b = self.ps_alloc(); ps = self.psb[b]
            self.mm(ps[:, :], ktm, self.vblk[0:64, k4 * 512:(k4 + 1) * 512], r=[Rktm, self.Rvblk], w=[self.Rps[b]])
            sl = slice(4 * k4, 4 * k4 + 4)
            self.tt(self.snew[:, sl, :], self.sst[:, sl, :], eg16[:, sl].unsqueeze(2).to_broadcast([128, 4, 128]), ALU.mult, r=[self.Rsst, Reg], w=[self.Rsnew])
            self.tt(self.snew[:, sl, :], self.snew[:, sl, :], ps[:, :].rearrange("p (b v) -> p b v", v=128), ALU.add, r=[self.Rsnew, self.Rps[b]], w=[self.Rsnew])
            self.ps_release(b)
        self.dma(dst[h], self.snew, r=[self.Rsnew])

    def hgrn_prelude(self, h, V):
        T = self.T
        N = T["N"]; samp = T["kind"] == "s"; nch = self.nch
        slot, Rw = self.w_next()
        wv = self.v3(slot, 512)
        P = dict(h=h)
        if V is None:
            for k, pool in (("qt", self.tB), ("kt", self.tB), ("kg", self.tB), ("sg", self.tB), ("vt", self.vtp), ("oc", self.tF)):
                t_, r_ = pool.get()
                P[k], P["R" + k] = t_[:], r_
        else:
            P.update(V)
        qt, kt, kg, sg, vt, Rqt, Rkt, Rkg, Rsg, Rvt = P["qt"], P["kt"], P["kg"], P["sg"], P["vt"], P["Rqt"], P["Rkt"], P["Rkg"], P["Rsg"], P["Rvt"]
        vt3 = vt.rearrange("p (c n) -> p c n", n=128)
        P["vt3"] = vt3
        bv = self.proj_fm(wv, Rw, 384, N)
        vfm, Rvfm = self.tB.get()
        self.cp("act", vfm[:, :N], self.psb[bv][:, :N], r=[self.Rps[bv]], w=[Rvfm])
        self.ps_release(bv)
        b = self.ps_alloc(); pb_ = self.psb[b][:].bitcast(BF16)
        for c in range(nch):
            self.tr(pb_[0:64, c * 128:(c + 1) * 128], vfm[:, c * 64:(c + 1) * 64], self.identb, r=[Rvfm, self.Rcb], w=[self.Rps[b]])
        self.cp("dve", vt[0:64, 0:nch * 128], pb_[0:64, 0:nch * 128], r=[self.Rps[b]], w=[Rvt])
        self.ps_release(b)
        bq = self.proj_fm(wv, Rw, 0, N)
        qf, Rqf = self.tF.get()
        self.act(qf[:, :N], self.psb[bq][:, :N], AF.Silu, r=[self.Rps[bq]], w=[Rqf])
        self.ps_release(bq)
        bg = self.proj_fm(wv, Rw, 256, N)
        self.act(sg[:, :N], self.psb[bg][:, :N], AF.Silu, r=[self.Rps[bg]], w=[Rsg])
        self.ps_release(bg)
        bf = self.proj_fm(wv, Rw, 128, N)
        ee, Ree = self.tF.get(); l1, Rl1 = self.tF.get(); l2, Rl2 = self.tF.get()
        self.act(ee[:, :N], self.psb[bf][:, :N], AF.Exp, r=[self.Rps[bf]], w=[Ree], scale=-1.0)
        self.ps_release(bf)
        self.act(l1[:, :N], ee[:, :N], AF.Ln, r=[Ree, self.Rsm], w=[Rl1], scale=self.lb(h), bias=self.one_col)
        self.act(l2[:, :N], ee[:, :N], AF.Ln, r=[Ree, self.Rsm], w=[Rl2], bias=self.one_col)
        lf, Rlf = l1, Rl1
        self.tt(lf[:, :N], l1[:, :N], l2[:, :N], ALU.subtract, r=[Rl1, Rl2], w=[Rlf])
        sn, Rsn = l2, Rl2
        self.act(sn[:, :N], l2[:, :N], AF.Exp, r=[Rl2], w=[Rsn], scale=-1.0)
        self.tt(sn[:, :N], sn[:, :N], ee[:, :N], ALU.mult, r=[Rsn, Ree], w=[Rsn])
        g, Rg = self.tF.get()
        self.cumsum_rows(g, Rg, lf, Rlf, 128, N)
        ep, Rep = self.tF.get()
        en, Ren = g, Rg
        self.act(ep[:, :N], g[:, :N], AF.Exp, r=[Rg], w=[Rep])
        self.act(en[:, :N], g[:, :N], AF.Exp, r=[Rg], w=[Ren], scale=-1.0)
        self.tt(qt[:, :N], qf[:, :N], ep[:, :N], ALU.mult, r=[Rqf, Rep], w=[Rqt])
        self.stt(kt[:, :N], sn[:, :N], self.oml(h), en[:, :N], ALU.mult, ALU.mult, r=[Rsn, Ren, self.Rsm], w=[Rkt])
        eg = self.egs[:, h, :]
        if not samp:
            ep3 = ep[:, :N].rearrange("p (c j) -> p c j", j=64)
            self.cp("dve", eg[:, 0:nch].unsqueeze(2), ep3[:, :, 63:64], r=[Rep], w=[self.Regs[h]])
            self.tt(kg[:, :N].rearrange("p (c j) -> p c j", j=64), kt[:, :N].rearrange("p (c j) -> p c j", j=64),
                    ep3[:, :, 63:64].to_broadcast([128, nch, 64]), ALU.mult, r=[Rkt, Rep], w=[Rkg])
        else:
            self.cp("dve", eg[:, 0:SB], ep[:, 48:64], r=[Rep], w=[self.Regs[h]])
            self.tt(kg[:, :N].rearrange("p (t s) -> p t s", s=SB), kt[:, :N].rearrange("p (t s) -> p t s", s=SB),
                    ep[:, 48:64].unsqueeze(1).to_broadcast([128, ST, SB]), ALU.mult, r=[Rkt, Rep], w=[Rkg])
            self.load_sample_state(self.st_hg, h)
        return P

    def hgrn_loop(self, h, P):
        T = self.T
        N = T["N"]; samp = T["kind"] == "s"; nch = self.nch; mo = self.mo
        qt, kt, kg, vt3, oc = P["qt"], P["kt"], P["kg"], P["vt3"], P["oc"]
        Rqt, Rkt, Rkg, Rvt, Roc = P["Rqt"], P["Rkt"], P["Rkg"], P["Rvt"], P["Roc"]
        maskT = self.cst[0:64, C_MT + mo:C_MT + mo + 64]
        eg = self.egs[:, h, :]; Reg = self.Regs[h]
        for c in range(nch):
            cs = slice(c * 64, (c + 1) * 64)
            b = self.ps_alloc(); ps = self.psb[b]
            self.mm(ps[0:64, 0:64], kt[:, cs], qt[:, cs], r=[Rkt, Rqt], w=[self.Rps[b]])
            b2 = self.ps_alloc(); psb_ = self.psb[b2][:].bitcast(BF16)
            self.tr(psb_[0:64, 0:128], kg[:, cs], self.identb, r=[Rkg, self.Rcb], w=[self.Rps[b2]])
            yield
            sT, RsT = self.sBp.get()
            self.tt(sT[0:64, 0:64], ps[0:64, 0:64], maskT, ALU.mult, r=[self.Rps[b], self.Rcst], w=[RsT])
            self.ps_release(b)
            ktm, Rktm = self.sBp.get()
            self.cp("act", ktm[0:64, :], psb_[0:64, 0:128], r=[self.Rps[b2]], w=[Rktm])
            self.ps_release(b2)
            yield
            bo = self.ps_alloc(); po = self.psb[bo]
            self.mm(po[:, 0:64], vt3[0:64, c, :], sT[0:64, 0:64], r=[Rvt, RsT], w=[self.Rps[bo]], start=True, stop=False)
            if not samp:
                self.mm(po[:, 0:64], self.Shgb[:, h, :], qt[:, cs], r=[self.RShgb[h], Rqt], w=[self.Rps[bo]], start=False, stop=True)
                b = self.ps_alloc(); ps = self.psb[b]
                self.mm(ps[:, 0:128], ktm[0:64, :], vt3[0:64, c, :], r=[Rktm, Rvt], w=[self.Rps[b]])
                yield
                self.stt(self.Shg[:, h, :], self.Shg[:, h, :], eg[:, c:c + 1], ps[:, 0:128], ALU.mult, ALU.add, r=[self.RShg[h], Reg, self.Rps[b]], w=[self.RShg[h]])
                self.ps_release(b)
                self.cp("act", oc[:, cs], po[:, 0:64], r=[self.Rps[bo]], w=[Roc])
                self.ps_release(bo)
                yield
                self.cp("act", self.Shgb[:, h, :], self.Shg[:, h, :], r=[self.RShg[h]], w=[self.RShgb[h]])
                yield
            else:
                for sq_ in range(SB):
                    self.mm(po[:, sq_:64:SB], self.sstb[:, sq_, :], qt[:, sq_:64:SB], r=[self.Rsstb, Rqt], w=[self.Rps[bo]], start=False, stop=(sq_ == SB - 1))
                self.cp("act", oc[:, 0:64], po[:, 0:64], r=[self.Rps[bo]], w=[Roc])
                self.ps_release(bo)
                self.sample_state_update(ktm[0:64, :], Rktm, vt3[0:64, 0, :], Rvt, eg[:, 0:SB], Reg, self.s_hg, h)
                yield


    def hgrn_loop_batched(self, Ps):
        T = self.T
        nch = self.nch
        maskT = self.cst[0:64, C_MT:C_MT + 64]
        RS, RSb = self.RShg, self.RShgb
        Rq = [P["Rqt"] for P in Ps]; Rk = [P["Rkt"] for P in Ps]; Rkg = [P["Rkg"] for P in Ps]; Rv = [P["Rvt"] for P in Ps]; Roc = [P["Roc"] for P in Ps]
        oc_all = self.AR[:, 0:8192].rearrange("p (h x) -> p h x", h=4)
        for c in range(nch):
            cs = slice(c * 64, (c + 1) * 64)
            b1 = self.ps_alloc(); ps1 = self.psb[b1]
            b2 = self.ps_alloc(); pb2 = self.psb[b2][:].bitcast(BF16)
            for h, P in enumerate(Ps):
                self.mm(ps1[0:64, h * 64:(h + 1) * 64], P["kt"][:, cs], P["qt"][:, cs], r=[Rk[h], Rq[h]], w=[self.Rps[b1]])
            for h, P in enumerate(Ps):
                self.tr(pb2[0:64, h * 128:(h + 1) * 128], P["kg"][:, cs], self.identb, r=[Rkg[h], self.Rcb], w=[self.Rps[b2]])
            sT, RsT = self.tB.get()
            self.tt(sT[0:64, 0:256].rearrange("p (h s) -> p h s", h=4), ps1[0:64, 0:256].rearrange("p (h s) -> p h s", h=4),
                    maskT.unsqueeze(1).to_broadcast([64, 4, 64]), ALU.mult, r=[self.Rps[b1], self.Rcst], w=[RsT])
            self.ps_release(b1)
            ktm, Rktm = self.tB.get()
            self.cp("act", ktm[0:64, 0:512], pb2[0:64, 0:512], r=[self.Rps[b2]], w=[Rktm])
            self.ps_release(b2)
            bo = self.ps_alloc(); po = self.psb[bo]
            bd = self.ps_alloc(); pd = self.psb[bd]
            for h, P in enumerate(Ps):
                self.mm(po[:, h * 64:(h + 1) * 64], P["vt3"][0:64, c, :], sT[0:64, h * 64:(h + 1) * 64], r=[Rv[h], RsT], w=[self.Rps[bo]], start=True, stop=False)
                self.mm(po[:, h * 64:(h + 1) * 64], self.Shgb[:, h, :], P["qt"][:, cs], r=[RSb[h], Rq[h]], w=[self.Rps[bo]], start=False, stop=True)
            for h, P in enumerate(Ps):
                self.mm(pd[:, h * 128:(h + 1) * 128], ktm[0:64, h * 128:(h + 1) * 128], P["vt3"][0:64, c, :], r=[Rktm, Rv[h]], w=[self.Rps[bd]])
            self.tt(self.Shg[:], self.Shg[:], self.egs[:, :, c:c + 1].to_broadcast([128, 4, 128]), ALU.mult, r=RS + self.Regs, w=RS)
            self.tt(self.Shgb[:], self.Shg[:], pd[:, :].rearrange("p (h v) -> p h v", h=4), ALU.add, r=RS + [self.Rps[bd]], w=RSb)
            self.tt(self.Shg[:], self.Shg[:], pd[:, :].rearrange("p (h v) -> p h v", h=4), ALU.add, r=RS + [self.Rps[bd]], w=RS)
            self.ps_release(bd)
            self.cp("act", oc_all[:, :, 1536 + c * 64:1536 + (c + 1) * 64], po[:, 0:256].rearrange("p (h s) -> p h s", h=4), r=[self.Rps[bo]], w=Roc)
            self.ps_release(bo)
            yield

    def hgrn_epilogue(self, h, P):
        T = self.T
        N = T["N"]
        self.rms_gate_out(P["oc"], P["Roc"], N, P["sg"], P["Rsg"], P_HGG, h)
        if T["kind"] == "p" and T["last"]:
            self.dma(self.p_hg[h], self.Shg[:, h, :], r=[self.RShg[h]])

    def gdn_rows(self):
        T = self.T
        N = T["N"]; samp = T["kind"] == "s"; nch = self.nch
        slot, Rw = self.w_next()
        wv = self.v3(slot, 8)
        prm, sm = self.prm, self.sm
        bb = self.proj_fm(wv, Rw, 0, N, M=4)
        beta, Rbeta = self.tF.get()
        self.act(beta[0:4, :N], self.psb[bb][0:4, :N], AF.Sigmoid, r=[self.Rps[bb]], w=[Rbeta])
        self.ps_release(bb)
        ba = self.proj_fm(wv, Rw, 4, N, M=4)
        la, Rla = self.tF.get()
        self.act(la[0:4, :N], self.psb[ba][0:4, :N], AF.Exp, r=[self.Rps[ba], self.Rprm], w=[Rla], bias=prm[0:4, P_DTB:P_DTB + 1])
        self.ps_release(ba)
        self.act(la[0:4, :N], la[0:4, :N], AF.Ln, r=[Rla], w=[Rla], bias=1.0)
        self.ts(la[0:4, :N], la[0:4, :N], sm[0:4, 8:9], None, ALU.mult, None, r=[Rla, self.Rsm], w=[Rla])
        gc, Rgc = self.rowg, self.Rrowg
        self.cumsum_rows(gc, Rgc, la, Rla, 4, N)
        ngc, Rngc = self.tF.get(); ngam, Rngam = self.tF.get(); ekd, Rekd = self.tF.get()
        self.ts(ngc[0:4, :N], gc[0:4, :N], -1.0, None, ALU.mult, None, r=[Rgc], w=[Rngc])
        self.act(ngam[0:4, :N], gc[0:4, :N], AF.Exp, r=[Rgc], w=[Rngam])
        self.ts(ngam[0:4, :N], ngam[0:4, :N], -1.0, None, ALU.mult, None, r=[Rngam], w=[Rngam])
        if not samp:
            gc3 = gc[0:4, :N].rearrange("p (c j) -> p c j", j=64)
            self.tt(ekd[0:4, :N].rearrange("p (c j) -> p c j", j=64), gc3[:, :, 63:64].to_broadcast([4, nch, 64]), gc3, ALU.subtract, r=[Rgc], w=[Rekd])
        else:
            self.tt(ekd[0:4, :N].rearrange("p (t s) -> p t s", s=SB), gc[0:4, 48:64].unsqueeze(1).to_broadcast([4, ST, SB]),
                    gc[0:4, :N].rearrange("p (t s) -> p t s", s=SB), ALU.subtract, r=[Rgc], w=[Rekd])
        self.act(ekd[0:4, :N], ekd[0:4, :N], AF.Exp, r=[Rekd], w=[Rekd])
        rows = [(0, beta, Rbeta), (1, gc, Rgc), (4, ekd, Rekd)]
        b = self.ps_alloc(); ps = self.psb[b]
        for c in range(nch):
            for qi, row, Rrow in rows:
                self.tr(ps[0:64, c * 20 + qi * 4:c * 20 + qi * 4 + 4], row[0:4, c * 64:(c + 1) * 64], self.identf[0:4, 0:4], r=[Rrow, self.Rcst], w=[self.Rps[b]])
        ps3 = ps[0:64, 0:nch * 20].rearrange("p (c q) -> p c q", q=20)
        self.cp("dve", self.tms[0:64, 0:nch, 0:8], ps3[:, :, 0:8], r=[self.Rps[b]], w=[self.Rtms])
        self.cp("dve", self.tms[0:64, 0:nch, 16:20], ps3[:, :, 16:20], r=[self.Rps[b]], w=[self.Rtms])
        self.ps_release(b)
        self.ts(self.tms[0:64, 0:nch, 8:12], self.tms[0:64, 0:nch, 4:8], -1.0, None, ALU.mult, None, r=[self.Rtms], w=[self.Rtms])
        self.act(self.tms[0:64, 0:nch, 12:16], self.tms[0:64, 0:nch, 4:8], AF.Exp, r=[self.Rtms], w=[self.Rtms])
        self.ts(self.tms[0:64, 0:nch, 12:16], self.tms[0:64, 0:nch, 12:16], -1.0, None, ALU.mult, None, r=[self.Rtms], w=[self.Rtms])

    def gdn_prelude(self, h, V):
        T = self.T
        N = T["N"]; samp = T["kind"] == "s"; u = self.unit
        H = 3 * u
        slot, Rw = self.w_next()
        wv = self.v3(slot, 512)
        prm = self.prm
        P = dict(h=h)
        if V is None:
            for k, pool in (("qn", self.tB), ("kn", self.tB), ("vv", self.tB), ("sg", self.tB), ("oc", self.tF)):
                t_, r_ = pool.get()
                P[k], P["R" + k] = t_[:], r_
        else:
            P.update(V)
        for qi, key in enumerate(("qn", "kn", "vv")):
            o, Ro = P[key], P["R" + key]
            j = qi * 4 + h
            b = self.proj_fm(wv, Rw, qi * 128, N)
            cw, Rcw = self.tF.get()
            self.cp("act", cw[:, H:H + N], self.psb[b][:, :N], r=[self.Rps[b]], w=[Rcw])
            self.ps_release(b)
            self.cp("dve", cw[:, 0:H], self.gdh[:, j, 0:H], r=[self.Rgdh[j]], w=[Rcw])
            a, Ra = self.tF.get()
            wc = lambda tap: prm[:, P_GDC + j * 4 + tap:P_GDC + j * 4 + tap + 1]
            self.ts(a[:, :N], cw[:, 3 * u:3 * u + N], wc(3), None, ALU.mult, None, r=[Rcw, self.Rprm], w=[Ra])
            for tap in (2, 1, 0):
                self.stt(a[:, :N], cw[:, tap * u:tap * u + N], wc(tap), a[:, :N], ALU.mult, ALU.add, r=[Rcw, Ra, self.Rprm], w=[Ra])
            self.cp("dve", self.gdh[:, j, 0:H], cw[:, N:N + H], r=[Rcw], w=[self.Rgdh[j]])
            if T["last"]:
                self.conv_state_out(self.gdh[:, j, 0:H], self.Rgdh[j], H, j, self.s_gc if samp else self.p_gc, 3)
            if qi < 2:
                xf, Rxf = a, Ra
                self.act(xf[:, :N], a[:, :N], AF.Silu, r=[Ra], w=[Rxf])
                sq, Rsq = self.tB.get()
                self.act(sq[:, :N], xf[:, :N], AF.Square, r=[Rxf], w=[Rsq])
                b2 = self.ps_alloc(); pm = self.psb[b2]
                self.mm(pm[:, :N], self.ones1, sq[:, :N], r=[self.Rcb, Rsq], w=[self.Rps[b2]])
                rn, Rrn = self.tF.get()
                self.act(rn[:, :N], pm[:, :N], AF.Ln, r=[self.Rps[b2]], w=[Rrn], bias=self.eps_rms)
                self.ps_release(b2)
                self.act(rn[:, :N], rn[:, :N], AF.Exp, r=[Rrn], w=[Rrn], scale=-0.5)
                if qi == 0:
                    self.stt(o[:, :N], xf[:, :N], 128.0 ** -0.5, rn[:, :N], ALU.mult, ALU.mult, r=[Rxf, Rrn], w=[Ro])
                else:
                    self.tt(o[:, :N], xf[:, :N], rn[:, :N], ALU.mult, r=[Rxf, Rrn], w=[Ro])
            else:
                self.act(o[:, :N], a[:, :N], AF.Silu, r=[Ra], w=[Ro])
        bz = self.proj_fm(wv, Rw, 384, N)
        self.act(P["sg"][:, :N], self.psb[bz][:, :N], AF.Silu, r=[self.Rps[bz]], w=[P["Rsg"]])
        self.ps_release(bz)
        if samp:
            self.load_sample_state(self.st_gd, h)
        return P

    def gdn_loop(self, h, P):
        T = self.T
        N = T["N"]; samp = T["kind"] == "s"; nch = self.nch; mo = self.mo
        cst = self.cst
        qn, kn, vv, oc = P["qn"], P["kn"], P["vv"], P["oc"]
        Rqn, Rkn, Rvv, Roc = P["Rqn"], P["Rkn"], P["Rvv"], P["Roc"]
        Mb1 = cst[0:64, C_MB1 + mo:C_MB1 + mo + 64]
        Mb2 = cst[0:64, C_MB2 + mo:C_MB2 + mo + 64]
        sel_h = cst[0:4, C_SEL + h * 128:C_SEL + (h + 1) * 128]
        nsq = 1 if samp else 5
        idb64 = self.identb[0:64, 0:64]
        for c in range(nch):
            cs = slice(c * 64, (c + 1) * 64)
            tm = lambda qi: self.tms[0:64, c, qi * 4 + h:qi * 4 + h + 1]
            bG = self.ps_alloc(); pG = self.psb[bG]
            grow = self.rowg[0:4, cs]
            self.mm(pG[:, 0:64], sel_h, grow, r=[self.Rcst, self.Rrowg], w=[self.Rps[bG]])
            self.mm(pG[0:64, 64:128], sel_h[:, 0:64], grow, r=[self.Rcst, self.Rrowg], w=[self.Rps[bG]], start=True, stop=False)
            self.mm(pG[0:64, 64:128], self.identf[0:64, 0:64], Mb1, r=[self.Rcst], w=[self.Rps[bG]], start=False, stop=True)
            self.mm(pG[0:64, 128:192], sel_h[:, 0:64], grow, r=[self.Rcst, self.Rrowg], w=[self.Rps[bG]], start=True, stop=False)
            self.mm(pG[0:64, 128:192], self.identf[0:64, 0:64], Mb2, r=[self.Rcst], w=[self.Rps[bG]], start=False, stop=True)
            b = self.ps_alloc(); ps = self.psb[b]
            self.mm(ps[0:64, 0:64], kn[:, cs], kn[:, cs], r=[Rkn], w=[self.Rps[b]])
            self.mm(ps[0:64, 64:128], kn[:, cs], qn[:, cs], r=[Rkn, Rqn], w=[self.Rps[b]])
            yield
            gbc, Rgbc = self.sF.get(); Ds, RDs = self.sF.get(); DTi, RDTi = self.sF.get()
            self.act(Ds[0:64, 0:64], pG[0:64, 64:128], AF.Exp, r=[self.Rps[bG], self.Rtms], w=[RDs], scale=-1.0, bias=tm(1))
            self.act(gbc[:, 0:64], pG[:, 0:64], AF.Exp, r=[self.Rps[bG]], w=[Rgbc])
            self.act(DTi[0:64, 0:64], pG[0:64, 128:192], AF.Exp, r=[self.Rps[bG], self.Rtms], w=[RDTi], scale=1.0, bias=tm(2))
            self.ps_release(bG)
            yield
            A, RA = self.sBp.get(); PT, RPT = self.sBl.get()
            self.stt(A[0:64, 0:64], ps[0:64, 0:64], tm(0), Ds[0:64, 0:64], ALU.mult, ALU.mult, r=[self.Rps[b], self.Rtms, RDs], w=[RA])
            self.tt(PT[0:64, 0:64], ps[0:64, 64:128], DTi[0:64, 0:64], ALU.mult, r=[self.Rps[b], RDTi], w=[RPT])
            self.ps_release(b)
            qg, Rqg = self.sBl.get()
            self.tt(qg[:, 0:64], qn[:, cs], gbc[:, 0:64], ALU.mult, r=[Rqn, Rgbc], w=[Rqg])
            yield
            b = self.ps_alloc(); pb = self.psb[b][:].bitcast(BF16)
            self.tr(pb[0:64, 0:64], A[0:64, 0:64], idb64, r=[RA, self.Rcb], w=[self.Rps[b]])
            b2 = self.ps_alloc(); pb2 = self.psb[b2][:].bitcast(BF16)
            self.tr(pb2[0:64, 0:128], kn[:, cs], self.identb, r=[Rkn, self.Rcb], w=[self.Rps[b2]])
            self.tr(pb2[0:64, 128:256], vv[:, cs], self.identb, r=[Rvv, self.Rcb], w=[self.Rps[b2]])
            yield
            AT, RAT = self.sBp.get()
            self.cp("act", AT[0:64, 0:64], pb[0:64, 0:64], r=[self.Rps[b]], w=[RAT])
            self.ps_release(b)
            kd, Rkd = self.sBl.get(); vtm, Rvtm = self.sBl.get()
            self.act(kd[0:64, :], pb2[0:64, 0:128], AF.Identity, r=[self.Rps[b2], self.Rtms], w=[Rkd], scale=tm(4))
            self.cp("act", vtm[0:64, :], pb2[0:64, 128:256], r=[self.Rps[b2]], w=[Rvtm])
            self.ps_release(b2)
            yield
            Tt, RTt = self.sBp.get()
            self.tt(Tt[0:64, 0:64], idb64, AT[0:64, 0:64], ALU.subtract, r=[self.Rcb, RAT], w=[RTt])
            X, RX, XT, RXT = A, RA, AT, RAT
            for i in range(nsq):
                last = (i == nsq - 1)
                b = self.ps_alloc(); ps = self.psb[b]
                self.mm(ps[0:64, 0:64], XT[0:64, 0:64], X[0:64, 0:64], r=[RX, RXT], w=[self.Rps[b]])
                if not last:
                    self.mm(ps[0:64, 64:128], X[0:64, 0:64], XT[0:64, 0:64], r=[RX, RXT], w=[self.Rps[b]])
                yield
                Xn, RXn = self.sBp.get()
                self.cp("act", Xn[0:64, 0:64], ps[0:64, 0:64], r=[self.Rps[b]], w=[RXn])
                if not last:
                    XTn, RXTn = self.sBp.get()
                    self.cp("act", XTn[0:64, 0:64], ps[0:64, 64:128], r=[self.Rps[b]], w=[RXTn])
                else:
                    XTn, RXTn = None, None
                self.ps_release(b)
                yield
                b = self.ps_alloc(); ps = self.psb[b]
                self.mm(ps[0:64, 0:64], Xn[0:64, 0:64], Tt[0:64, 0:64], r=[RXn, RTt], w=[self.Rps[b]])
                yield
                Ttn, RTtn = self.sBp.get()
                self.tt(Ttn[0:64, 0:64], ps[0:64, 0:64], Tt[0:64, 0:64], ALU.add, r=[self.Rps[b], RTt], w=[RTtn])
                self.ps_release(b)
                X, RX, XT, RXT, Tt, RTt = Xn, RXn, XTn, RXTn, Ttn, RTtn
            b = self.ps_alloc()
            if not samp:
                ps = self.psb[b]
                self.mm(ps[0:64, 0:128], kn[:, cs], self.Sgdb[:, h, :], r=[Rkn, self.RSgdb[h]], w=[self.Rps[b]])
                ks_src = ps[0:64, 0:128]
            else:
                b1 = self.ps_alloc(); ps1 = self.psb[b1]
                for sq_ in range(SB):
                    self.mm(ps1[:, sq_:64:SB], self.sstb[:, sq_, :], kn[:, sq_:64:SB], r=[self.Rsstb, Rkn], w=[self.Rps[b1]])
                kst, Rkst = self.sBp.get()
                self.cp("act", kst[:, 0:64], ps1[:, 0:64], r=[self.Rps[b1]], w=[Rkst])
                self.ps_release(b1)
                pb = self.psb[b][:].bitcast(BF16)
                self.tr(pb[0:64, 0:128], kst[:, 0:64], self.identb, r=[Rkst, self.Rcb], w=[self.Rps[b]])
                ks_src = pb[0:64, 0:128]
            yield
            r1, Rr1 = self.sF.get()
            self.stt(r1[0:64, :], ks_src, tm(3), vtm[0:64, :], ALU.mult, ALU.add, r=[self.Rps[b], self.Rtms, Rvtm], w=[Rr1])
            self.ps_release(b)
            rb, Rrb = self.sBp.get()
            self.ts(rb[0:64, :], r1[0:64, :], tm(0), None, ALU.mult, None, r=[Rr1, self.Rtms], w=[Rrb])
            yield
            b = self.ps_alloc(); ps = self.psb[b]
            self.mm(ps[0:64, 0:128], Tt[0:64, 0:64], rb[0:64, :], r=[RTt, Rrb], w=[self.Rps[b]])
            yield
            Ub, RUb = self.sBp.get()
            self.cp("act", Ub[0:64, :], ps[0:64, 0:128], r=[self.Rps[b]], w=[RUb])
            self.ps_release(b)
            yield
            bo = self.ps_alloc(); pO = self.psb[bo]
            self.mm(pO[:, 0:64], Ub[0:64, :], PT[0:64, 0:64], r=[RUb, RPT], w=[self.Rps[bo]], start=True, stop=False)
            if not samp:
                self.mm(pO[:, 0:64], self.Sgdb[:, h, :], qg[:, 0:64], r=[self.RSgdb[h], Rqg], w=[self.Rps[bo]], start=False, stop=True)
                b = self.ps_alloc(); ps = self.psb[b]
                self.mm(ps[:, 0:128], kd[0:64, :], Ub[0:64, :], r=[Rkd, RUb], w=[self.Rps[b]])
                yield
                self.stt(self.Sgd[:, h, :], self.Sgd[:, h, :], gbc[:, 63:64], ps[:, 0:128], ALU.mult, ALU.add, r=[self.RSgd[h], Rgbc, self.Rps[b]], w=[self.RSgd[h]])
                self.ps_release(b)
                self.cp("act", oc[:, cs], pO[:, 0:64], r=[self.Rps[bo]], w=[Roc])
                self.ps_release(bo)
                yield
                self.cp("act", self.Sgdb[:, h, :], self.Sgd[:, h, :], r=[self.RSgd[h]], w=[self.RSgdb[h]])
                yield
            else:
                for sq_ in range(SB):
                    self.mm(pO[:, sq_:64:SB], self.sstb[:, sq_, :], qg[:, sq_:64:SB], r=[self.Rsstb, Rqg], w=[self.Rps[bo]], start=False, stop=(sq_ == SB - 1))
                self.cp("act", oc[:, 0:64], pO[:, 0:64], r=[self.Rps[bo]], w=[Roc])
                self.ps_release(bo)
                self.sample_state_update(kd[0:64, :], Rkd, Ub[0:64, :], RUb, gbc[:, 48:64], Rgbc, self.s_gd, h)
                yield

    def gdn_loop_pair(self, h, P):
        T = self.T
        cst = self.cst
        qn, kn, vv, oc = P["qn"], P["kn"], P["vv"], P["oc"]
        Rqn, Rkn, Rvv, Roc = P["Rqn"], P["Rkn"], P["Rvv"], P["Roc"]
        sel_h = cst[0:4, C_SEL + h * 128:C_SEL + (h + 1) * 128]
        Mb1x2, Mb2x2, idb2 = self.Mb1x2, self.Mb2x2, self.idb2
        nsq = 5
        for p_ in range(self.nch // 2):
            c0 = 2 * p_
            cs2 = slice(c0 * 64, c0 * 64 + 128)
            csj = [slice((c0 + j) * 64, (c0 + j + 1) * 64) for j in range(2)]
            hj = [slice(j * 64, (j + 1) * 64) for j in range(2)]
            tm = lambda j, qi: self.tms[0:64, c0 + j, qi * 4 + h:qi * 4 + h + 1]
            bG = self.ps_alloc(); pG = self.psb[bG]
            grow2 = self.rowg[0:4, cs2]
            self.mm(pG[:, 0:128], sel_h, grow2, r=[self.Rcst, self.Rrowg], w=[self.Rps[bG]])
            self.mm(pG[0:64, 128:256], sel_h[:, 0:64], grow2, r=[self.Rcst, self.Rrowg], w=[self.Rps[bG]], start=True, stop=False)
            self.mm(pG[0:64, 128:256], self.identf[0:64, 0:64], Mb1x2, r=[self.Rcst, self.Rmx], w=[self.Rps[bG]], start=False, stop=True)
            self.mm(pG[0:64, 256:384], sel_h[:, 0:64], grow2, r=[self.Rcst, self.Rrowg], w=[self.Rps[bG]], start=True, stop=False)
            self.mm(pG[0:64, 256:384], self.identf[0:64, 0:64], Mb2x2, r=[self.Rcst, self.Rmx], w=[self.Rps[bG]], start=False, stop=True)
            b = self.ps_alloc(); ps = self.psb[b]
            for j in range(2):
                self.mm(ps[0:64, hj[j]], kn[:, csj[j]], kn[:, csj[j]], r=[Rkn], w=[self.Rps[b]])
            for j in range(2):
                self.mm(ps[0:64, 128 + j * 64:128 + (j + 1) * 64], kn[:, csj[j]], qn[:, csj[j]], r=[Rkn, Rqn], w=[self.Rps[b]])
            yield
            gbc, Rgbc = self.sF.get(); Ds, RDs = self.sF.get(); DTi, RDTi = self.sF.get()
            for j in range(2):
                self.act(Ds[0:64, hj[j]], pG[0:64, 128 + j * 64:128 + (j + 1) * 64], AF.Exp, r=[self.Rps[bG], self.Rtms], w=[RDs], scale=-1.0, bias=tm(j, 1))
            self.act(gbc[:, 0:128], pG[:, 0:128], AF.Exp, r=[self.Rps[bG]], w=[Rgbc])
            for j in range(2):
                self.act(DTi[0:64, hj[j]], pG[0:64, 256 + j * 64:256 + (j + 1) * 64], AF.Exp, r=[self.Rps[bG], self.Rtms], w=[RDTi], scale=1.0, bias=tm(j, 2))
            self.ps_release(bG)
            yield
            A, RA = self.sBp.get(); PT, RPT = self.sBl.get()
            for j in range(2):
                self.stt(A[0:64, hj[j]], ps[0:64, hj[j]], tm(j, 0), Ds[0:64, hj[j]], ALU.mult, ALU.mult, r=[self.Rps[b], self.Rtms, RDs], w=[RA])
            self.tt(PT[0:64, 0:128], ps[0:64, 128:256], DTi[0:64, 0:128], ALU.mult, r=[self.Rps[b], RDTi], w=[RPT])
            self.ps_release(b)
            qg, Rqg = self.sBl.get()
            self.tt(qg[:, 0:128], qn[:, cs2], gbc[:, 0:128], ALU.mult, r=[Rqn, Rgbc], w=[Rqg])
            egl2 = self.egl_all[:, h, c0:c0 + 2]
            self.cp("dve", egl2, gbc[:, 63:128:64], r=[Rgbc], w=[self.Regl[h][c0], self.Regl[h][c0 + 1]])
            yield
            b = self.ps_alloc(); pb = self.psb[b][:].bitcast(BF16)
            for j in range(2):
                self.tr(pb[0:64, hj[j]], A[0:64, hj[j]], self.identb[0:64, 0:64], r=[RA, self.Rcb], w=[self.Rps[b]])
            b2 = self.ps_alloc(); pb2 = self.psb[b2][:].bitcast(BF16)
            for j in range(2):
                self.tr(pb2[0:64, j * 128:(j + 1) * 128], kn[:, csj[j]], self.identb, r=[Rkn, self.Rcb], w=[self.Rps[b2]])
            for j in range(2):
                self.tr(pb2[0:64, 256 + j * 128:256 + (j + 1) * 128], vv[:, csj[j]], self.identb, r=[Rvv, self.Rcb], w=[self.Rps[b2]])
            yield
            AT, RAT = self.sBp.get()
            self.cp("act", AT[0:64, 0:128], pb[0:64, 0:128], r=[self.Rps[b]], w=[RAT])
            self.ps_release(b)
            kds, vtms = [], []
            for j in range(2):
                kd, Rkd = self.sBl.get()
                self.act(kd[0:64, :], pb2[0:64, j * 128:(j + 1) * 128], AF.Identity, r=[self.Rps[b2], self.Rtms], w=[Rkd], scale=tm(j, 4))
                kds.append((kd, Rkd))
            for j in range(2):
                vtm, Rvtm = self.sBl.get()
                self.cp("act", vtm[0:64, :], pb2[0:64, 256 + j * 128:256 + (j + 1) * 128], r=[self.Rps[b2]], w=[Rvtm])
                vtms.append((vtm, Rvtm))
            self.ps_release(b2)
            yield
            Tt, RTt = self.sBp.get()
            self.tt(Tt[0:64, 0:128], idb2, AT[0:64, 0:128], ALU.subtract, r=[self.Rmx, RAT], w=[RTt])
            X, RX, XT, RXT = A, RA, AT, RAT
            for i in range(nsq):
                last = (i == nsq - 1)
                b = self.ps_alloc(); ps = self.psb[b]
                for j in range(2):
                    self.mm(ps[0:64, hj[j]], XT[0:64, hj[j]], X[0:64, hj[j]], r=[RX, RXT], w=[self.Rps[b]])
                if not last:
                    for j in range(2):
                        self.mm(ps[0:64, 128 + j * 64:128 + (j + 1) * 64], X[0:64, hj[j]], XT[0:64, hj[j]], r=[RX, RXT], w=[self.Rps[b]])
                yield
                Xn, RXn = self.sBp.get()
                self.cp("act", Xn[0:64, 0:128], ps[0:64, 0:128], r=[self.Rps[b]], w=[RXn])
                if not last:
                    XTn, RXTn = self.sBp.get()
                    self.cp("act", XTn[0:64, 0:128], ps[0:64, 128:256], r=[self.Rps[b]], w=[RXTn])
                else:
                    XTn, RXTn = None, None
                self.ps_release(b)
                yield
                b = self.ps_alloc(); ps = self.psb[b]
                for j in range(2):
                    self.mm(ps[0:64, hj[j]], Xn[0:64, hj[j]], Tt[0:64, hj[j]], r=[RXn, RTt], w=[self.Rps[b]])
                yield
                Ttn, RTtn = self.sBp.get()
                self.tt(Ttn[0:64, 0:128], ps[0:64, 0:128], Tt[0:64, 0:128], ALU.add, r=[self.Rps[b], RTt], w=[RTtn])
                self.ps_release(b)
                X, RX, XT, RXT, Tt, RTt = Xn, RXn, XTn, RXTn, Ttn, RTtn
            for j in range(2):
                c = c0 + j
                cs = csj[j]
                kd, Rkd = kds[j]; vtm, Rvtm = vtms[j]
                b = self.ps_alloc(); ps = self.psb[b]
                self.mm(ps[0:64, 0:128], kn[:, cs], self.Sgdb[:, h, :], r=[Rkn, self.RSgdb[h]], w=[self.Rps[b]])
                yield
                r1, Rr1 = self.sF.get()
                self.stt(r1[0:64, :], ps[0:64, 0:128], tm(j, 3), vtm[0:64, :], ALU.mult, ALU.add, r=[self.Rps[b], self.Rtms, Rvtm], w=[Rr1])
                self.ps_release(b)
                rb, Rrb = self.sBp.get()
                self.ts(rb[0:64, :], r1[0:64, :], tm(j, 0), None, ALU.mult, None, r=[Rr1, self.Rtms], w=[Rrb])
                yield
                b = self.ps_alloc(); ps = self.psb[b]
                self.mm(ps[0:64, 0:128], Tt[0:64, hj[j]], rb[0:64, :], r=[RTt, Rrb], w=[self.Rps[b]])
                yield
                Ub, RUb = self.sBp.get()
                self.cp("act", Ub[0:64, :], ps[0:64, 0:128], r=[self.Rps[b]], w=[RUb])
                self.ps_release(b)
                yield
                bo = self.ps_alloc(); pO = self.psb[bo]
                self.mm(pO[:, 0:64], Ub[0:64, :], PT[0:64, hj[j]], r=[RUb, RPT], w=[self.Rps[bo]], start=True, stop=False)
                self.mm(pO[:, 0:64], self.Sgdb[:, h, :], qg[:, hj[j]], r=[self.RSgdb[h], Rqg], w=[self.Rps[bo]], start=False, stop=True)
                b = self.ps_alloc(); ps = self.psb[b]
                self.mm(ps[:, 0:128], kd[0:64, :], Ub[0:64, :], r=[Rkd, RUb], w=[self.Rps[b]])
                yield
                self.stt(self.Sgdb[:, h, :], self.Sgd[:, h, :], self.egl_all[:, h, c:c + 1], ps[:, 0:128], ALU.mult, ALU.add, r=[self.RSgd[h], self.Regl[h][c], self.Rps[b]], w=[self.RSgdb[h]])
                self.stt(self.Sgd[:, h, :], self.Sgd[:, h, :], self.egl_all[:, h, c:c + 1], ps[:, 0:128], ALU.mult, ALU.add, r=[self.RSgd[h], self.Regl[h][c], self.Rps[b]], w=[self.RSgd[h]])
                self.ps_release(b)
                self.cp("act", oc[:, cs], pO[:, 0:64], r=[self.Rps[bo]], w=[Roc])
                self.ps_release(bo)
                yield

    def gdn_epilogue(self, h, P):
        T = self.T
        self.rms_gate_out(P["oc"], P["Roc"], T["N"], P["sg"], P["Rsg"], P_GDG, 4 + h)
        if T["kind"] == "p" and T["last"]:
            self.dma(self.p_gd[h], self.Sgd[:, h, :], r=[self.RSgd[h]])

    def conv_state_out(self, src, Rsrc, H, j, dst, nrow):
        b = self.ps_alloc(); ps = self.psb[b]
        self.tr(ps[0:H, 0:128], src, self.identf, r=[Rsrc, self.Rcst], w=[self.Rps[b]])
        o, Ro = self.sF.get()
        self.cp("act", o[0:H, :], ps[0:H, 0:128], r=[self.Rps[b]], w=[Ro])
        self.ps_release(b)
        if self.T["kind"] == "p":
            self.dma(dst[:, j * 128:(j + 1) * 128], o[0:H, :], r=[Ro])
        else:
            for t in range(nrow):
                self.dma(dst[:, t, j * 128:(j + 1) * 128], o[t * SB:(t + 1) * SB, :], r=[Ro])

    def stage_attn_ln2(self):
        T = self.T
        N = T["N"]; samp = T["kind"] == "s"
        v3 = self.v3
        for g in range(2):
            slot, Rw = self.w_next()
            wv = v3(slot, 512)
            for mi in range(4):
                m = g * 4 + mi
                b = self.proj_fm(wv, Rw, mi * 128, N)
                self.cp("act", self.mq[:, m, :N], self.psb[b][:, :N], r=[self.Rps[b]], w=[self.Rmq[m]])
                self.ps_release(b)
        oa = self.aview(0, 2048, BF16, "p (k n) -> p k n", n=512)
        Roa = [R() for _ in range(8)]
        new_rs = list(Roa)
        if samp:
            kcs = [self.aview(2048, 1024, BF16, "p (j d) -> p j d", d=D), self.aview(5120, 1024, BF16, "p (j d) -> p j d", d=D)]
            vcs = [self.aview(3072, 1024, BF16, "p (j d) -> p j d", d=D), self.aview(6144, 1024, BF16, "p (j d) -> p j d", d=D)]
            Rkcs, Rvcs = [R(), R()], [R(), R()]
            kF = self.aview(4096, 1024, BF16, "p (k m) -> p k m", m=256); RkF = R()
            new_rs += Rkcs + Rvcs + [RkF]
        self.arena_switch(new_rs)
        sc = 256.0 ** -0.5
        mq, Rmq = self.mq, self.Rmq
        if not samp:
            for h in range(4):
                pTs = []
                for j in range(2):
                    b = self.ps_alloc(); ps = self.psb[b]
                    for dc in range(2):
                        self.mm(ps[:, :N], self.mkF[:, 2 * h + dc, j * 128:(j + 1) * 128], mq[:, 2 * h + dc, :N], r=[self.RmkF, Rmq[2 * h + dc]], w=[self.Rps[b]], start=(dc == 0), stop=(dc == 1))
                    pT, RpT = self.tB.get()
                    self.act(pT[:, :N], ps[:, :N], AF.Exp, r=[self.Rps[b]], w=[RpT], scale=sc)
                    self.ps_release(b)
                    pTs.append((pT, RpT))
                b = self.ps_alloc(); ps = self.psb[b]
                for j in range(2):
                    self.mm(ps[:, :N], self.ones1, pTs[j][0][:, :N], r=[self.Rcb, pTs[j][1]], w=[self.Rps[b]], start=(j == 0), stop=(j == 1))
                rden, Rrden = self.tF.get()
                self.act(rden[:, :N], ps[:, :N], AF.Ln, r=[self.Rps[b]], w=[Rrden])
                self.act(rden[:, :N], rden[:, :N], AF.Exp, r=[Rrden], w=[Rrden], scale=-1.0)
                self.ps_release(b)
                for dc in range(2):
                    dch = 2 * h + dc
                    b = self.ps_alloc(); ps = self.psb[b]
                    for j in range(2):
                        self.mm(ps[:, :N], self.mvT[:, j, dch * 128:(dch + 1) * 128], pTs[j][0][:, :N], r=[self.RmvT, pTs[j][1]], w=[self.Rps[b]], start=(j == 0), stop=(j == 1))
                    self.tt(oa[:, dch, :N], ps[:, :N], rden[:, :N], ALU.mult, r=[self.Rps[b], Rrden], w=[Roa[dch]])
                    self.ps_release(b)
        else:
            def kv_load(i_):
                self.dma(kcs[i_ % 2], self.ck[i_].rearrange("(j p) d -> p j d", p=128), w=[Rkcs[i_ % 2]], q="pool")
                self.dma(vcs[i_ % 2], self.cv[i_].rearrange("(j p) d -> p j d", p=128), w=[Rvcs[i_ % 2]], q="pool")
            kv_load(0)
            for sq_ in range(SB):
                kc, vc, Rkc, Rvc = kcs[sq_ % 2], vcs[sq_ % 2], Rkcs[sq_ % 2], Rvcs[sq_ % 2]
                if sq_ + 1 < SB:
                    kv_load(sq_ + 1)
                for j in range(2):
                    b = self.ps_alloc(); pb = self.psb[b][:].bitcast(BF16)
                    for dch in range(8):
                        self.tr(pb[:, dch * 128:(dch + 1) * 128], kc[:, j, dch * 128:(dch + 1) * 128], self.identb, r=[Rkc, self.Rcb], w=[self.Rps[b]])
                    self.cp("act" if j else "dve", kF[:, :, j * 128:(j + 1) * 128], pb[:, 0:1024].rearrange("p (k m) -> p k m", m=128), r=[self.Rps[b]], w=[RkF])
                    self.ps_release(b)
                cols = slice(sq_, 64, SB)
                bS = self.ps_alloc(); pS = self.psb[bS]
                for h in range(4):
                    for j in range(2):
                        c0 = (h * 2 + j) * 4
                        for dc in range(2):
                            self.mm(pS[:, c0:c0 + 4], kF[:, 2 * h + dc, j * 128:(j + 1) * 128], mq[:, 2 * h + dc, cols], r=[RkF, Rmq[2 * h + dc]], w=[self.Rps[bS]], start=(dc == 0), stop=(dc == 1))
                pT, RpT = self.sBp.get()
                self.act(pT[:, 0:32], pS[:, 0:32], AF.Exp, r=[self.Rps[bS]], w=[RpT], scale=sc)
                self.ps_release(bS)
                bD = self.ps_alloc(); pD = self.psb[bD]
                for h in range(4):
                    for j in range(2):
                        c0 = (h * 2 + j) * 4
                        self.mm(pD[:, h * 4:(h + 1) * 4], self.ones1, pT[:, c0:c0 + 4], r=[self.Rcb, RpT], w=[self.Rps[bD]], start=(j == 0), stop=(j == 1))
                rden, Rrden = self.sF.get()
                self.act(rden[:, 0:16], pD[:, 0:16], AF.Ln, r=[self.Rps[bD]], w=[Rrden])
                self.act(rden[:, 0:16], rden[:, 0:16], AF.Exp, r=[Rrden], w=[Rrden], scale=-1.0)
                self.ps_release(bD)
                bO = self.ps_alloc(); pO = self.psb[bO]
                for h in range(4):
                    for dc in range(2):
                        dch = 2 * h + dc
                        for j in range(2):
                            c0 = (h * 2 + j) * 4
                            self.mm(pO[:, dch * 4:(dch + 1) * 4], vc[:, j, dch * 128:(dch + 1) * 128], pT[:, c0:c0 + 4], r=[Rvc, RpT], w=[self.Rps[bO]], start=(j == 0), stop=(j == 1))
                for h in range(4):
                    self.tt(oa[:, 2 * h:2 * h + 2, cols], pO[:, 8 * h:8 * h + 8].rearrange("p (a t) -> p a t", t=4),
                            rden[:, 4 * h:4 * h + 4].unsqueeze(1).to_broadcast([128, 2, 4]), ALU.mult,
                            r=[self.Rps[bO], Rrden], w=[Roa[2 * h], Roa[2 * h + 1]])
                self.ps_release(bO)
        self.proj_residual(oa, Roa, 8, N, 2, lambda s: v3(s, 512), 512)
        self.layer_norm(N, P_LN + 16, P_LN + 24, shadow=True)

    def stage_ffn_ln3(self):
        T = self.T
        N = T["N"]; samp = T["kind"] == "s"; u = self.unit
        H2 = 2 * u
        prm = self.prm
        actb = self.aview(0, 5632, BF16, "p (k n) -> p k n", n=512)
        Ract = [R() for _ in range(22)]
        self.arena_switch(Ract)
        for g in range(11):
            slot, Rw = self.w_next()
            wv = self.v3(slot, 512)
            for jj in range(2):
                j = 2 * g + jj
                bg = self.proj_fm(wv, Rw, jj * 128, N)
                bv = self.proj_fm(wv, Rw, 256 + jj * 128, N)
                gw, Rgw = self.tF.get()
                self.cp("act", gw[:, H2:H2 + N], self.psb[bg][:, :N], r=[self.Rps[bg]], w=[Rgw])
                self.ps_release(bg)
                self.cp("dve", gw[:, 0:H2], self.ffh[:, j, 0:H2], r=[self.Rffh[j]], w=[Rgw])
                a, Ra = self.tF.get()
                wc = lambda tap: prm[:, P_FC + j * 3 + tap:P_FC + j * 3 + tap + 1]
                self.ts(a[:, :N], gw[:, 2 * u:2 * u + N], wc(2), prm[:, P_FB + j:P_FB + j + 1], ALU.mult, ALU.add, r=[Rgw, self.Rprm], w=[Ra])
                for tap in (1, 0):
                    self.stt(a[:, :N], gw[:, tap * u:tap * u + N], wc(tap), a[:, :N], ALU.mult, ALU.add, r=[Rgw, Ra, self.Rprm], w=[Ra])
                self.cp("dve", self.ffh[:, j, 0:H2], gw[:, N:N + H2], r=[Rgw], w=[self.Rffh[j]])
                if T["last"]:
                    self.conv_state_out(self.ffh[:, j, 0:H2], self.Rffh[j], H2, j, self.s_fc if samp else self.p_fc, 2)
                ge, Rge = self.tF.get()
                self.act(ge[:, :N], a[:, :N], AF.Gelu, r=[Ra], w=[Rge])
                self.tt(actb[:, j, :N], ge[:, :N], self.psb[bv][:, :N], ALU.mult, r=[Rge, self.Rps[bv]], w=[Ract[j]])
                self.ps_release(bv)
        self.proj_residual(actb, Ract, 22, N, 8, lambda s: s[:, 0:22 * 128].rearrange("p (k n) -> p k n", n=128), 128)
        self.layer_norm(N, P_LN + 32, P_LN + 40, shadow=False)


def _pack_params(inp):
    prm = np.zeros((128, NPRM), np.float32)
    lbl = np.asarray(inp["hgrn_lb_logits"], np.float32)
    prm[:, P_L0:P_L0 + 4] = lbl[0].reshape(4, 128).T
    prm[:, P_L1:P_L1 + 4] = lbl[1].reshape(4, 128).T
    wc = np.asarray(inp["w_gd_conv"], np.float32)[0]
    prm[:, P_GDC:P_GDC + 48] = wc.reshape(4, 12, 128).transpose(2, 1, 0).reshape(128, 48)
    prm[:, P_HGG] = np.asarray(inp["hg_norm_g"], np.float32)[0]
    prm[:, P_GDG] = np.asarray(inp["gd_norm_g"], np.float32)[0]
    for i, k in enumerate(("ln1_g", "ln1_b", "ln2_g", "ln2_b", "ln3_g", "ln3_b")):
        prm[:, P_LN + 8 * i:P_LN + 8 * i + 8] = np.asarray(inp[k], np.float32)[0].reshape(8, 128).T
    fc = np.asarray(inp["w_ffn_conv"], np.float32)[0]
    prm[:, P_FC:P_FC + 66] = fc.reshape(3, 22, 128).transpose(2, 1, 0).reshape(128, 66)
    prm[:, P_FB:P_FB + 22] = np.asarray(inp["b_ffn_conv"], np.float32)[0].reshape(22, 128).T
    prm[0:4, P_ALOG] = np.asarray(inp["gd_a_log"], np.float32)[0]
    prm[0:4, P_DTB] = np.asarray(inp["gd_dt_bias"], np.float32)[0]
    return prm


def _consts():
    c = np.zeros((128, NCST), np.float32)
    c[:, C_ID:C_ID + 128] = np.eye(128, dtype=np.float32)
    i = np.arange(64)
    c[0:64, C_MT:C_MT + 64] = (i[:, None] <= i[None, :])
    c[0:64, C_MB1:C_MB1 + 64] = np.where(i[None, :] < i[:, None], 0.0, BIG)
    c[0:64, C_MB2:C_MB2 + 64] = np.where(i[None, :] >= i[:, None], 0.0, -BIG)
    tt_, ss_ = i // SB, i % SB
    same = ss_[:, None] == ss_[None, :]
    c[0:64, C_MT + 64:C_MT + 128] = same & (tt_[:, None] <= tt_[None, :])
    c[0:64, C_MB1 + 64:C_MB1 + 128] = np.where(same & (tt_[None, :] < tt_[:, None]), 0.0, BIG)
    c[0:64, C_MB2 + 64:C_MB2 + 128] = np.where(same & (tt_[None, :] >= tt_[:, None]), 0.0, -BIG)
    sm = np.ones(512, np.float32); sm[0::64] = 0.0
    c[:, C_SCAN:C_SCAN + 512] = sm[None, :]
    c[0:64, C_BM:C_BM + SB] = (ss_[:, None] == np.arange(SB)[None, :])
    for h in range(4):
        c[h, C_SEL + h * 128:C_SEL + (h + 1) * 128] = 1.0
    return c


_OUT_NAMES = ["yp", "ys", "p_hg", "p_gd", "p_gc", "p_fc", "p_mk", "p_mv", "s_hg", "s_gd", "s_gc", "s_fc"]


def make_in_maps(inp, cores):
    f = lambda k: np.ascontiguousarray(np.asarray(inp[k], np.float32))
    prm = _pack_params(inp)
    cst = _consts()
    shared = {"prm": prm, "cst": cst}
    def grp(W, col_lists):
        nk = W.shape[0] // 128
        Wr = W.reshape(nk, 128, W.shape[1])
        out = []
        for cols in col_lists:
            out.append(np.ascontiguousarray(Wr[:, :, cols].transpose(1, 0, 2).reshape(128, nk * len(cols))))
        return np.stack(out, axis=0)

    ar = np.arange
    w_in = f("w_in")[0]
    hg = [np.concatenate([ar(h * 128, (h + 1) * 128) + base for base in (0, 512, 1536, 1024)]) for h in range(4)]
    gd = [np.concatenate([ar(h * 128, (h + 1) * 128) + base for base in (2048, 2560, 3072, 3584)]) for h in range(4)]
    shared["w_in"] = grp(w_in, hg + gd)
    shared["w_tail"] = grp(w_in, [ar(4096, 4104)])[0]
    c512 = lambda n: [ar(g * 512, (g + 1) * 512) for g in range(n)]
    shared["w_out"] = grp(f("w_out")[0], c512(2))
    shared["w_mq"] = grp(f("w_mq")[0], c512(2))
    shared["w_mkv"] = grp(f("w_mkv")[0], c512(4))
    shared["w_mo"] = grp(f("w_mo")[0], c512(2))
    shared["w_up"] = grp(f("w_up")[0], [np.concatenate([ar(g * 256, (g + 1) * 256), DFF + ar(g * 256, (g + 1) * 256)]) for g in range(11)])
    shared["w_down"] = grp(f("w_down")[0], [ar(g * 128, (g + 1) * 128) for g in range(8)])
    xp, xs = f("x_prompt"), f("x_sample")
    st_hg, st_gd, st_gc, st_fc = f("state_hgrn")[0], f("state_gdn")[0], f("state_gdn_conv")[0], f("state_ffn_conv")[0]
    ck, cv, memp = f("cache_mem_k")[0], f("cache_mem_v")[0], f("mem_prompt")
    maps = []
    for c in cores:
        sl = slice(c * SB, (c + 1) * SB)
        m = dict(shared)
        m.update({"xp": xp[c], "xs": xs[sl], "st_hg": np.ascontiguousarray(st_hg[sl].transpose(1, 2, 0, 3)), "st_gd": np.ascontiguousarray(st_gd[sl].transpose(1, 2, 0, 3)), "st_gc": st_gc[sl], "st_fc": st_fc[sl],
                  "ck": ck[sl].reshape(SB, NMEM, D), "cv": cv[sl].reshape(SB, NMEM, D), "memp": memp[c]})
        maps.append(m)
    return maps


def assemble(results):
    cat = lambda k: np.concatenate([r[k] for r in results], axis=0)
    stk = lambda k: np.stack([r[k] for r in results], axis=0)
    yp = stk("yp")
    ys = cat("ys")
    return (yp, ys,
            stk("p_hg")[None], stk("p_gd")[None], stk("p_gc")[None], stk("p_fc")[None],
            stk("p_mk").reshape(1, NCORE, NMEM, 4, 256), stk("p_mv").reshape(1, NCORE, NMEM, 4, 256),
            np.concatenate([r["s_hg"].transpose(2, 0, 1, 3) for r in results], axis=0)[None],
            np.concatenate([r["s_gd"].transpose(2, 0, 1, 3) for r in results], axis=0)[None],
            cat("s_gc")[None], cat("s_fc")[None])


def kernel(**inputs):
    nc = MK().build()
    maps = make_in_maps(inputs, list(range(NCORE)))
    res = run_bass_kernel_spmd(nc, maps, core_ids=list(range(NCORE)))
    return assemble(res.results)
```
